# Optimizing a Trainium2 kernel written in Bass

```python
import jax, jax.numpy as jnp
from jax import lax
import numpy as np

D_MODEL = 1024
BATCH = 8
SEQ = 2048
DEPTH = 1
DEC_BATCH = 128
DEC_SEQ = 8
PAST_LEN = 16384
PAGE_SIZE = 128

MIX_WIDTH = D_MODEL
RET_HEADS = 8
RET_DK = 64
RET_DV = MIX_WIDTH // 2 // RET_HEADS
RWKV_HEADS = 8
RWKV_N = MIX_WIDTH // 2 // RWKV_HEADS
RET_W = RET_HEADS * RET_DV
RWKV_W = RWKV_HEADS * RWKV_N
DECAY_LORA = 64
AAA_LORA = 64
GATE_LORA = 128
RET_PROJ = 2 * RET_HEADS * RET_DK + 2 * RET_W
RWKV_PROJ = 3 * RWKV_W + DECAY_LORA + AAA_LORA + GATE_LORA
IN_PROJ = RET_PROJ + RWKV_PROJ
RET_CHUNK = 128
ROPE_BASE = 10000.0
PEER_HEADS = 8
PEER_NKEYS = 128
PEER_EXPERTS = PEER_NKEYS * PEER_NKEYS
PEER_DKEY = 256
PEER_TOPK = 16
PEER_BLOCK = 256
PLE_DIM = 256
RMS_EPS = 1e-6
GN_EPS = 1e-5
RWKV_LN_EPS = 64e-5

kernel_name = 'hymba_retnet_rwkv7_peer_step'


def rmsnorm(x, g):
    xf = x.astype(jnp.float32)
    y = xf * lax.rsqrt(jnp.mean(xf * xf, axis=-1, keepdims=True) + RMS_EPS)
    return (y * g.astype(jnp.float32)).astype(x.dtype)


def head_norm(x, eps):
    xf = x.astype(jnp.float32)
    xc = xf - jnp.mean(xf, axis=-1, keepdims=True)
    return xc * lax.rsqrt(jnp.mean(xc * xc, axis=-1, keepdims=True) + eps)


def rope(x, pos):
    d = x.shape[-1]
    half = d // 2
    inv = ROPE_BASE ** (-jnp.arange(half, dtype=jnp.float32) / half)
    ang = pos.astype(jnp.float32)[:, None] * inv[None, :]
    cos = jnp.cos(ang)[None, :, None, :]
    sin = jnp.sin(ang)[None, :, None, :]
    xf = x.astype(jnp.float32)
    x1, x2 = xf[..., :half], xf[..., half:]
    return jnp.concatenate([x1 * cos - x2 * sin, x2 * cos + x1 * sin], axis=-1)


def retention_chunkwise(q, k, v, s0):
    B, L, H, DK = q.shape
    DV = v.shape[-1]
    C = RET_CHUNK if L % RET_CHUNK == 0 else L
    n = L // C
    lg = jnp.log(1.0 - 2.0 ** (-5.0 - jnp.arange(H, dtype=jnp.float32)))
    idx = jnp.arange(C, dtype=jnp.float32)
    diff = idx[:, None] - idx[None, :]
    dmat = jnp.where(diff[None] >= 0, jnp.exp(jnp.maximum(diff, 0.0)[None] * lg[:, None, None]), 0.0)
    xi = jnp.exp((idx + 1.0)[None, :] * lg[:, None])
    zeta = jnp.exp((C - 1.0 - idx)[None, :] * lg[:, None])
    g_chunk = jnp.exp(C * lg)

    def to_chunks(t):
        return t.astype(jnp.float32).reshape(B, n, C, H, t.shape[-1]).transpose(1, 0, 3, 2, 4)

    def step(S, xs):
        qc, kc, vc = xs
        inner = jnp.einsum('bhid,bhjd->bhij', qc, kc) * dmat
        o = jnp.einsum('bhij,bhjv->bhiv', inner, vc) + jnp.einsum('bhid,bhdv->bhiv', qc, S) * xi[:, :, None]
        S = S * g_chunk[:, None, None] + jnp.einsum('bhjd,bhjv->bhdv', kc * zeta[:, :, None], vc)
        return S, o

    S, o = lax.scan(step, s0.astype(jnp.float32), (to_chunks(q), to_chunks(k), to_chunks(v)))
    o = o.transpose(1, 0, 3, 2, 4).reshape(B, L, H, DV)
    return S, o


def rwkv7_scan(r, log_w, k, kk, a, v, s0):
    def tm(t):
        return t.astype(jnp.float32).transpose(1, 0, 2, 3)

    def step(S, xs):
        rt, lwt, kt, kkt, at, vt = xs
        sa = jnp.einsum('bhij,bhj->bhi', S, -kkt)
        S = (S * jnp.exp(lwt)[:, :, None, :] + sa[..., None] * (kkt * at)[:, :, None, :]
             + vt[..., None] * kt[:, :, None, :])
        y = jnp.einsum('bhij,bhj->bhi', S, rt)
        return S, y

    S, y = lax.scan(step, s0.astype(jnp.float32), (tm(r), tm(log_w), tm(k), tm(kk), tm(a), tm(v)))
    return S, y.transpose(1, 0, 2, 3)


def token_mixers(xn, pos, s_ret, s_wkv, s_shift, w_in, rw_mu, rw_w0, rw_w2, rw_a0, rw_a2, rw_g2,
                 rw_kk, rw_ka, rw_rk, rw_ln_g, rw_ln_b, w_out):
    B, L, _ = xn.shape
    proj = xn @ w_in
    ret_p, rw_p = proj[..., :RET_PROJ], proj[..., RET_PROJ:]
    qk = RET_HEADS * RET_DK
    q = rope(ret_p[..., :qk].reshape(B, L, RET_HEADS, RET_DK), pos)
    k = rope(ret_p[..., qk:2 * qk].reshape(B, L, RET_HEADS, RET_DK), pos) * (RET_DK ** -0.5)
    v = ret_p[..., 2 * qk:2 * qk + RET_W].reshape(B, L, RET_HEADS, RET_DV)
    g_ret = ret_p[..., 2 * qk + RET_W:]
    s_ret_new, o_ret = retention_chunkwise(q, k, v, s_ret)
    o_ret = (head_norm(o_ret, GN_EPS).reshape(B, L, RET_W) * jax.nn.silu(g_ret.astype(jnp.float32))).astype(xn.dtype)
    prev = jnp.concatenate([s_shift[:, None, :].astype(rw_p.dtype), rw_p[:, :-1]], axis=1)
    f = rw_p + (prev - rw_p) * rw_mu
    new_shift = rw_p[:, -1]
    o1, o2, o3 = RWKV_W, 2 * RWKV_W, 3 * RWKV_W
    r = f[..., :o1]
    k7 = f[..., o1:o2]
    v7 = f[..., o2:o3]
    fw = f[..., o3:o3 + DECAY_LORA].astype(jnp.float32)
    fa = f[..., o3 + DECAY_LORA:o3 + DECAY_LORA + AAA_LORA].astype(jnp.float32)
    fg = f[..., o3 + DECAY_LORA + AAA_LORA:].astype(jnp.float32)
    w = -jax.nn.softplus(-(rw_w0 + jnp.tanh(fw) @ rw_w2)) - 0.5
    log_w = -jnp.exp(w)
    a = jax.nn.sigmoid(rw_a0 + fa @ rw_a2)
    gate = jax.nn.sigmoid(fg) @ rw_g2
    kk = (k7 * rw_kk).astype(jnp.float32).reshape(B, L, RWKV_HEADS, RWKV_N)
    kk = kk / jnp.maximum(jnp.sqrt(jnp.sum(kk * kk, axis=-1, keepdims=True)), 1e-12)
    k7 = k7.astype(jnp.float32) * (1.0 + (a - 1.0) * rw_ka)
    hs = (B, L, RWKV_HEADS, RWKV_N)
    r4, k4, v4, a4 = r.astype(jnp.float32).reshape(hs), k7.reshape(hs), v7.astype(jnp.float32).reshape(hs), a.reshape(hs)
    s_wkv_new, o_rw = rwkv7_scan(r4, log_w.reshape(hs), k4, kk, a4, v4, s_wkv)
    o_rw = head_norm(o_rw, RWKV_LN_EPS) * rw_ln_g.reshape(RWKV_HEADS, RWKV_N) + rw_ln_b.reshape(RWKV_HEADS, RWKV_N)
    o_rw = o_rw + jnp.sum(r4 * k4 * rw_rk, axis=-1, keepdims=True) * v4
    o_rw = (o_rw.reshape(B, L, RWKV_W) * gate).astype(xn.dtype)
    out = jnp.concatenate([o_ret, o_rw], axis=-1) @ w_out
    return out, s_ret_new, s_wkv_new, new_shift


def peer(xn, wq, keys, u_tab, v_tab):
    B, L, D = xn.shape
    T = B * L
    nb = -(-T // PEER_BLOCK)
    xt = jnp.pad(xn.reshape(T, D), ((0, nb * PEER_BLOCK - T), (0, 0))).reshape(nb, PEER_BLOCK, D)

    def block(xb):
        q = (xb @ wq).reshape(PEER_BLOCK, PEER_HEADS, 2, PEER_DKEY // 2)
        s = jnp.einsum('phcd,hcnd->phcn', q, keys).astype(jnp.float32)
        sv, si = lax.top_k(s, PEER_TOPK)
        cand = sv[:, :, 0, :, None] + sv[:, :, 1, None, :]
        cv, ci = lax.top_k(cand.reshape(PEER_BLOCK, PEER_HEADS, PEER_TOPK * PEER_TOPK), PEER_TOPK)
        e = (jnp.take_along_axis(si[:, :, 0], ci // PEER_TOPK, axis=-1) * PEER_NKEYS
             + jnp.take_along_axis(si[:, :, 1], ci % PEER_TOPK, axis=-1))
        gsm = jax.nn.softmax(cv, axis=-1)
        hval = jnp.einsum('pd,phkd->phk', xb, u_tab[e])
        act = (jax.nn.gelu(hval.astype(jnp.float32), approximate=False) * gsm).astype(xb.dtype)
        return jnp.einsum('phk,phkd->pd', act, v_tab[e])

    y = lax.map(block, xt).reshape(nb * PEER_BLOCK, D)[:T]
    return y.reshape(B, L, D)


def layer(h, p_l, pos, s_ret, s_wkv, s_shift, lp):
    (norm_mix, w_in, rw_mu, rw_w0, rw_w2, rw_a0, rw_a2, rw_g2, rw_kk, rw_ka, rw_rk, rw_ln_g, rw_ln_b,
     w_out, norm_ffn, peer_wq, peer_keys, peer_u, peer_v, ple_norm, ple_gate, ple_proj) = lp
    xn = rmsnorm(h, norm_mix)
    mix, s_ret_new, s_wkv_new, new_shift = token_mixers(
        xn, pos, s_ret, s_wkv, s_shift, w_in, rw_mu, rw_w0, rw_w2, rw_a0, rw_a2, rw_g2,
        rw_kk, rw_ka, rw_rk, rw_ln_g, rw_ln_b, w_out)
    h = h + mix
    h = h + peer(rmsnorm(h, norm_ffn), peer_wq, peer_keys, peer_u, peer_v)
    gate = jax.nn.sigmoid((rmsnorm(h, ple_norm) @ ple_gate).astype(jnp.float32))
    h = h + (gate * (p_l @ ple_proj)).astype(h.dtype)
    return h, s_ret_new, s_wkv_new, new_shift


def setup_inputs(seed: int = 0) -> dict:
    key = jax.random.key(seed)
    ks = jax.random.split(key, 40)
    f32 = jnp.float32

    def nrm(k, shape, scale):
        return jax.random.normal(k, shape, f32) * scale

    return {
        'x_prompt': nrm(ks[0], (BATCH, SEQ, D_MODEL), 1.0),
        'x_sample': nrm(ks[1], (DEC_BATCH, DEC_SEQ, D_MODEL), 1.0),
        'state_ret': nrm(ks[2], (DEPTH, DEC_BATCH, RET_HEADS, RET_DK, RET_DV), 0.5),
        'state_wkv': nrm(ks[3], (DEPTH, DEC_BATCH, RWKV_HEADS, RWKV_N, RWKV_N), 0.5),
        'state_shift': nrm(ks[4], (DEPTH, DEC_BATCH, RWKV_PROJ), 1.0),
        'p_prompt': nrm(ks[5], (DEPTH, BATCH, SEQ, PLE_DIM), 1.0),
        'p_sample': nrm(ks[6], (DEPTH, DEC_BATCH, DEC_SEQ, PLE_DIM), 1.0),
        'norm_mix': 1.0 + nrm(ks[7], (DEPTH, D_MODEL), 0.05),
        'w_in': nrm(ks[8], (DEPTH, D_MODEL, IN_PROJ), D_MODEL ** -0.5),
        'rw_mu': jax.random.uniform(ks[9], (DEPTH, RWKV_PROJ), f32, 0.0, 1.0),
        'rw_w0': jax.random.uniform(ks[10], (DEPTH, RWKV_W), f32, -5.0, 0.0),
        'rw_w2': nrm(ks[11], (DEPTH, DECAY_LORA, RWKV_W), 0.1),
        'rw_a0': nrm(ks[12], (DEPTH, RWKV_W), 0.5),
        'rw_a2': nrm(ks[13], (DEPTH, AAA_LORA, RWKV_W), 0.5 * AAA_LORA ** -0.5),
        'rw_g2': nrm(ks[14], (DEPTH, GATE_LORA, RWKV_W), GATE_LORA ** -0.5),
        'rw_kk': 0.85 + nrm(ks[15], (DEPTH, RWKV_W), 0.05),
        'rw_ka': 1.0 + nrm(ks[16], (DEPTH, RWKV_W), 0.05),
        'rw_rk': nrm(ks[17], (DEPTH, RWKV_HEADS, RWKV_N), 0.1),
        'rw_ln_g': 1.0 + nrm(ks[18], (DEPTH, RWKV_W), 0.05),
        'rw_ln_b': nrm(ks[19], (DEPTH, RWKV_W), 0.01),
        'w_out': nrm(ks[20], (DEPTH, MIX_WIDTH, D_MODEL), MIX_WIDTH ** -0.5),
        'norm_ffn': 1.0 + nrm(ks[21], (DEPTH, D_MODEL), 0.05),
        'peer_wq': nrm(ks[22], (DEPTH, D_MODEL, PEER_HEADS * PEER_DKEY), D_MODEL ** -0.5),
        'peer_keys': nrm(ks[23], (DEPTH, PEER_HEADS, 2, PEER_NKEYS, PEER_DKEY // 2), (PEER_DKEY // 2) ** -0.5),
        'peer_u': nrm(ks[24], (DEPTH, PEER_EXPERTS, D_MODEL), D_MODEL ** -0.5),
        'peer_v': nrm(ks[25], (DEPTH, PEER_EXPERTS, D_MODEL), 0.5 * PEER_HEADS ** -0.5),
        'ple_norm': 1.0 + nrm(ks[26], (DEPTH, D_MODEL), 0.05),
        'ple_gate': nrm(ks[27], (DEPTH, D_MODEL, D_MODEL), D_MODEL ** -0.5),
        'ple_proj': nrm(ks[28], (DEPTH, PLE_DIM, D_MODEL), PLE_DIM ** -0.5),
        'norm_final': 1.0 + nrm(ks[29], (D_MODEL,), 0.05),
    }


def reference(x_prompt, x_sample, state_ret, state_wkv, state_shift, p_prompt, p_sample,
              norm_mix, w_in, rw_mu, rw_w0, rw_w2, rw_a0, rw_a2, rw_g2, rw_kk, rw_ka, rw_rk,
              rw_ln_g, rw_ln_b, w_out, norm_ffn, peer_wq, peer_keys, peer_u, peer_v,
              ple_norm, ple_gate, ple_proj, norm_final):
    pos_p = jnp.arange(SEQ, dtype=jnp.int32)
    pos_s = PAST_LEN + jnp.arange(DEC_SEQ, dtype=jnp.int32)
    zero_ret = jnp.zeros((BATCH, RET_HEADS, RET_DK, RET_DV), jnp.float32)
    zero_wkv = jnp.zeros((BATCH, RWKV_HEADS, RWKV_N, RWKV_N), jnp.float32)
    zero_shift = jnp.zeros((BATCH, RWKV_PROJ), x_prompt.dtype)
    hp, hs = x_prompt, x_sample
    rp_l, wp_l, sp_l, rs_l, ws_l, ss_l = [], [], [], [], [], []
    for i in range(DEPTH):
        lp = (norm_mix[i], w_in[i], rw_mu[i], rw_w0[i], rw_w2[i], rw_a0[i], rw_a2[i], rw_g2[i],
              rw_kk[i], rw_ka[i], rw_rk[i], rw_ln_g[i], rw_ln_b[i], w_out[i], norm_ffn[i],
              peer_wq[i], peer_keys[i], peer_u[i], peer_v[i], ple_norm[i], ple_gate[i], ple_proj[i])
        hp, rp, wp, sp = layer(hp, p_prompt[i], pos_p, zero_ret, zero_wkv, zero_shift, lp)
        hs, rs, ws, ss = layer(hs, p_sample[i], pos_s, state_ret[i], state_wkv[i], state_shift[i], lp)
        rp_l.append(rp); wp_l.append(wp); sp_l.append(sp)
        rs_l.append(rs); ws_l.append(ws); ss_l.append(ss)
    y_prompt = rmsnorm(hp, norm_final)
    y_sample = rmsnorm(hs, norm_final)
    ret_prompt = jnp.stack(rp_l)
    wkv_prompt = jnp.stack(wp_l)
    shift_prompt = jnp.stack(sp_l)
    ret_sample = jnp.stack(rs_l)
    wkv_sample = jnp.stack(ws_l)
    shift_sample = jnp.stack(ss_l)
    return (y_prompt, y_sample, ret_prompt, wkv_prompt, shift_prompt, ret_sample, wkv_sample, shift_sample)
```

```python
import numpy as np
from contextlib import ExitStack
import concourse.bass as bass
import concourse.mybir as mybir
from concourse.bass_utils import run_bass_kernel_spmd

F32 = mybir.dt.float32
BF16 = mybir.dt.bfloat16
U32 = mybir.dt.uint32
AF = mybir.ActivationFunctionType
ALU = mybir.AluOpType
AX = mybir.AxisListType

NCORES = 8
NT = 17
TOK = NT * 128
D = 1024
RMS_EPS = 1e-6
GN_EPS = 1e-5
RWKV_LN_EPS = 64e-5
SAME_ENGINE_SYNC = True
DUM_A = 0
DUM_B = 0


class Prog:
    ENG = ("pe", "dve", "act", "pool", "sp")

    def __init__(self, nc, n_dma_sems=64):
        self.nc = nc
        self.lists = {k: [] for k in self.ENG}
        self.cnt = {k: 0 for k in self.ENG}
        self.waited = {k: {} for k in self.ENG}
        self.bufs = {}
        self.pending = {k: ([], []) for k in self.ENG}
        self.sems = {}
        self.n_dma_sems = n_dma_sems
        self.chan_sem = {}
        self.chan_cnt = {}
        self.chan_last = {}
        self.free_dma = list(range(n_dma_sems))
        self.out_events = []
        self.dead = False

    def alloc_sems(self, stack):
        for k in self.ENG:
            self.sems["prog_" + k] = stack.enter_context(self.nc.semaphore("prog_" + k))
        for i in range(self.n_dma_sems):
            self.sems["dma%d" % i] = stack.enter_context(self.nc.semaphore("dma%d" % i))

    def _chan(self, chan):
        if chan not in self.chan_sem:
            if not self.free_dma:
                raise RuntimeError("out of DMA semaphores")
            self.chan_sem[chan] = "dma%d" % self.free_dma.pop(0)
            self.chan_cnt[chan] = 0
        return self.chan_sem[chan]

    def _b(self, key):
        if key not in self.bufs:
            self.bufs[key] = {"w": None, "r": []}
        return self.bufs[key]

    def _deps(self, reads, writes):
        deps = []
        for k in reads:
            b = self._b(k)
            if b["w"] is not None:
                deps.append(b["w"])
        for k in writes:
            b = self._b(k)
            if b["w"] is not None:
                deps.append(b["w"])
            deps.extend(b["r"])
        return deps

    def _emit_waits(self, e, deps, ss=True):
        own = "prog_" + e
        for (s, v) in deps:
            if s == own and not SAME_ENGINE_SYNC:
                continue
            if self.waited[e].get(s, 0) < v:
                self.waited[e][s] = v
                self.lists[e].append(("wait", s, v))

    def _commit(self, ev, reads, writes):
        for k in reads:
            self._b(k)["r"].append(ev)
        for k in writes:
            self.bufs[k] = {"w": ev, "r": []}

    def op(self, e, fn, reads=(), writes=(), inc=True, ss=True):
        if self.dead:
            return None
        reads = list(reads); writes = list(writes)
        self._emit_waits(e, self._deps(reads, writes), ss)
        if not inc:
            self.pending[e][0].extend(reads)
            self.pending[e][1].extend(writes)
            self.lists[e].append(("op", fn, None))
            return None
        self.cnt[e] += 1
        ev = ("prog_" + e, self.cnt[e])
        self.lists[e].append(("op", fn, "prog_" + e))
        pr, pw = self.pending[e]
        self._commit(ev, reads + pr, writes + pw)
        self.pending[e] = ([], [])
        return ev

    def dma(self, q, fn, chan, reads=(), writes=(), is_output=False):
        if self.dead:
            return None
        reads = list(reads); writes = list(writes)
        sem = self._chan(chan)
        deps = self._deps(reads, writes)
        if chan in self.chan_last:
            deps.append(self.chan_last[chan])
        self._emit_waits(q, deps)
        self.chan_cnt[chan] += 16
        ev = (sem, self.chan_cnt[chan])
        self.chan_last[chan] = ev
        self.lists[q].append(("dma", fn, sem))
        self._commit(ev, reads, writes)
        if is_output:
            self.out_events.append(ev)
        return ev

    def barrier(self):
        deps = [ev for ev in self.chan_last.values()]
        for e in self.ENG:
            if self.cnt[e] > 0:
                deps.append(("prog_" + e, self.cnt[e]))
        for e in self.ENG:
            self._emit_waits(e, deps)

    def finish(self):
        deps = list(self.out_events) + [ev for ev in self.chan_last.values()]
        for e in self.ENG:
            if e != "sp" and self.cnt[e] > 0:
                deps.append(("prog_" + e, self.cnt[e]))
        self._emit_waits("sp", deps)

    def replay(self, e, engine):
        for item in self.lists[e]:
            if item[0] == "wait":
                engine.wait_ge(self.sems[item[1]], item[2])
            elif item[0] == "op":
                ins = item[1](engine)
                if item[2] is not None:
                    ins.then_inc(self.sems[item[2]], 1)
            else:
                ins = item[1](engine)
                ins.then_inc(self.sems[item[2]], 16)

    def run_block(self, block):
        p = self

        @block.sync
        def _(eng):
            p.replay("sp", eng)

        @block.tensor
        def _(eng):
            p.replay("pe", eng)

        @block.vector
        def _(eng):
            p.replay("dve", eng)

        @block.scalar
        def _(eng):
            p.replay("act", eng)

        @block.gpsimd
        def _(eng):
            p.replay("pool", eng)


def _const_tables():
    f = np.float32
    H = 8
    lg = np.log(f(1.0) - f(2.0) ** (-5.0 - np.arange(H, dtype=f))).astype(f)
    t = {}

    def decay(e):
        return np.exp(e[None].astype(f) * lg.reshape((H,) + (1,) * e.ndim)).astype(f)

    idx = np.arange(128, dtype=f)
    diff = idx[None, :] - idx[:, None]
    dp = np.where(diff >= 0, decay(np.maximum(diff, 0)), 0).astype(f)
    t["DT_p"] = np.ascontiguousarray(dp.transpose(1, 0, 2)).reshape(128, H * 128)
    b = np.arange(128) // 8
    l = (np.arange(128) % 8).astype(f)
    same = (b[:, None] == b[None, :])
    dl = l[None, :] - l[:, None]
    ds = np.where(same[None] & (dl[None] >= 0), decay(np.maximum(dl, 0)), 0).astype(f)
    t["DT_s"] = np.ascontiguousarray(ds.transpose(1, 0, 2)).reshape(128, H * 128)
    def xi_tab(e):
        x = decay(e)
        o = np.zeros((128, 4, 128), f)
        for g in range(4):
            for hh in range(2):
                o[hh * 64:(hh + 1) * 64, g, :] = x[2 * g + hh][None, :]
        return o.reshape(128, 512)
    t["XI_p"] = xi_tab(idx + 1.0)
    t["XI_s"] = xi_tab(l + 1.0)
    t["Z_p"] = (decay(127.0 - idx).T * f(0.125)).astype(f)
    t["Z_s"] = (decay(7.0 - l).T * f(0.125)).astype(f)
    def gc_tab(C):
        g_ = np.exp(f(C) * lg).astype(f)
        o = np.zeros((128, 4), f)
        for g in range(4):
            for hh in range(2):
                o[hh * 64:(hh + 1) * 64, g] = g_[2 * g + hh]
        return o
    t["GC_p"] = gc_tab(128.0)
    t["GC_s"] = gc_tab(8.0)
    t["Mk2"] = (b[:, None] == np.arange(16)[None, :]).astype(f)
    bm = np.zeros((128, 128), f)
    bm[:64, :64] = 1; bm[64:, 64:] = 1
    t["blockmask"] = bm
    hm = np.zeros((128, 2), f)
    hm[:64, 0] = 1; hm[64:, 1] = 1
    t["halfmask2"] = hm
    aw = np.zeros((128, 2, 255), f)
    aw[:64, 0, 127] = 1; aw[64:, 1, 127] = 1
    t["Awin"] = aw.reshape(128, 510)
    i2 = np.zeros((128, 64), f)
    i2[np.arange(128), np.arange(128) % 64] = 1
    t["I2"] = i2
    t["ident"] = np.eye(128, dtype=f)
    t["iota16"] = np.tile(np.arange(16, dtype=f)[None, :], (128, 1))
    half = 32
    inv = (f(10000.0) ** (-np.arange(half, dtype=f) / f(half))).astype(f)
    cs = np.zeros((NT, 128, 64), f)
    for n in range(NT):
        if n < 16:
            pos = (n * 128 + np.arange(128)).astype(f)
        else:
            pos = (16384 + (np.arange(128) % 8)).astype(f)
        ang = (pos[:, None] * inv[None, :]).astype(f)
        cs[n, :, :32] = np.cos(ang.astype(np.float64)).astype(f)
        cs[n, :, 32:] = np.sin(ang.astype(np.float64)).astype(f)
    t["cs"] = cs
    return t


CT_ORDER = ["DT_p", "DT_s", "XI_p", "XI_s", "Z_p", "Z_s", "GC_p", "GC_s", "Mk2", "blockmask",
            "halfmask2", "Awin", "I2", "ident", "iota16"]


def build_program(phases=("A", "B"), debug=(), nt_limit=NT):
    nc = bass.Bass("TRN2", target_bir_lowering=False)
    P = Prog(nc)
    tabs = _const_tables()
    ct_off = {}
    off = 0
    for k in CT_ORDER:
        ct_off[k] = (off, tabs[k].shape[1])
        off += tabs[k].shape[1]
    NCT = off

    def din(name, shape, dt=F32):
        return nc.dram_tensor(name, list(shape), dt, kind="ExternalInput").ap()

    def dout(name, shape, dt=F32):
        return nc.dram_tensor(name, list(shape), dt, kind="ExternalOutput").ap()

    x_all = din("x_all", [TOK, D])
    p_all = din("p_all", [TOK, 256])
    s_ret = din("s_ret", [16, 8, 64, 64])
    s_wkv = din("s_wkv", [16, 8, 64, 64])
    s_shift = din("s_shift", [16, 1792])
    w_in = din("w_in", [D, 3840])
    w_out = din("w_out", [D, D])
    peer_wq = din("peer_wq", [D, 2048])
    peer_keys = din("peer_keys", [16, 128, 128])
    peer_u = din("peer_u", [16384, D])
    peer_v = din("peer_v", [16384, D])
    ple_gate = din("ple_gate", [D, D])
    ple_proj = din("ple_proj", [256, D])
    rw_w2a2 = din("rw_w2a2", [128, 512])
    rw_g2 = din("rw_g2", [128, 512])
    cols_d = din("cols", [128, 64])
    rows_d = din("rows", [128, 3072])
    ctab_d = din("ctab", [128, NCT])
    cs_d = din("cs", [NT, 128, 64])

    y_o = dout("y", [TOK, D])
    retp_o = dout("ret_p", [8, 64, 64])
    wkvp_o = dout("wkv_p", [8, 64, 64])
    shiftp_o = dout("shift_p", [1792])
    rets_o = dout("ret_s", [16, 8, 64, 64])
    wkvs_o = dout("wkv_s", [16, 8, 64, 64])
    shifts_o = dout("shift_s", [16, 1792])
    h1_d = dout("h1_scr", [TOK, D])
    dbg_outs = {}

    with ExitStack() as st:
        ARENA_F32 = 53200
        arena = st.enter_context(nc.sbuf_tensor("arena", [128, ARENA_F32], F32))
        aoff = [0]

        def sb(name, shape, dt=F32):
            nel = 1
            for d_ in shape[1:]:
                nel *= d_
            esz = 4 if dt in (F32, U32) else 2
            n4 = ((nel * esz + 31) // 32) * 8
            o = aoff[0]
            aoff[0] += n4
            assert aoff[0] <= ARENA_F32, ("arena overflow", name, aoff[0])
            v = arena[0:shape[0], o:o + n4]
            if dt != F32:
                v = v.bitcast(dt)
            v = v[:, 0:nel]
            if len(shape) == 3:
                v = v.rearrange("p (a b) -> p a b", a=shape[1])
            elif len(shape) == 4:
                v = v.rearrange("p (a b c) -> p a b c", a=shape[1], b=shape[2])
            elif len(shape) == 5:
                v = v.rearrange("p (a b c d) -> p a b c d", a=shape[1], b=shape[2], c=shape[3])
            return v

        ps = st.enter_context(nc.psum_tensor("ps", [128, 4096], F32))

        def bank(b, lo=0, hi=512):
            return ps[:, b * 512 + lo:b * 512 + hi]

        def bk(b):
            return "ps%d" % b

        NDUM = [0]

        def dummies(k, bnk, src, src_key):
            for _ in range(k):
                P.op("pe", lambda e: e.matmul(bank(bnk), lhsT=src[:, 0:128], rhs=src[:, 0:512], start=True, stop=True),
                     reads=[src_key], writes=[bk(bnk)], inc=False)

        cols = sb("cols", [128, 64])
        ctab = sb("ctab", [128, NCT])
        C_GMIX, C_GFFN, C_GPLE, C_MU, C_W0, C_A0, C_KK, C_KA, C_RK, C_OMKA = 0, 8, 16, 24, 38, 42, 46, 50, 54, 58

        def ct(name):
            o, w = ct_off[name]
            return ctab[:, o:o + w]

        ident = ct("ident")
        P.alloc_sems(st)
        block = st.enter_context(nc.Block())

        def ckpt(name):
            if ("cut_" + name) in debug:
                P.dead = True

        def dbg(name, ap, shape, reads):
            if name not in debug:
                return
            o = dout("dbg_" + name, shape)
            dbg_outs[name] = o
            P.dma("sp", lambda e: e.dma_start(out=o, in_=ap), "dbg_" + name, reads=reads, is_output=True)

        P.dma("sp", lambda e: e.dma_start(out=cols[:], in_=cols_d), "cols", writes=["cols"])
        P.dma("sp", lambda e: e.dma_start(out=ctab[:], in_=ctab_d), "ctab", writes=["ctab"])
        P.op("dve", lambda e: e.tensor_scalar(out=cols[:, C_OMKA:C_OMKA + 4], in0=cols[:, C_KA:C_KA + 4],
                                              scalar1=-1.0, scalar2=1.0, op0=ALU.mult, op1=ALU.add),
             reads=["cols"], writes=["cols"])

        def bc(ap2, n, axis_last=True):
            k = ap2.shape[1]
            return ap2.unsqueeze(2).to_broadcast([ap2.shape[0], k, n])

        def rms_rstd(src, src_key, junk, junk_key, ssq, rstd, tag):
            P.op("act", lambda e: e.activation(out=junk, in_=src, func=AF.Square, accum_out=ssq),
                 reads=[src_key], writes=[junk_key, tag + "ssq"])
            P.op("act", lambda e: e.activation(out=rstd, in_=ssq, func=AF.Sqrt, scale=1.0 / D, bias=eps_t[:, 0:1]),
                 reads=[tag + "ssq", "eps"], writes=[tag + "rstd"])
            P.op("dve", lambda e: e.reciprocal(out=rstd, in_=rstd), reads=[tag + "rstd"], writes=[tag + "rstd"])

        def transposes8(src, src_key, dstT, dst_key, gcol_off, b0, b1):
            for hb, bnk in ((0, b0), (1, b1)):
                for c4 in range(4):
                    c = hb * 4 + c4
                    P.op("pe", lambda e, c=c, c4=c4, bnk=bnk: e.transpose(out=bank(bnk, c4 * 128, c4 * 128 + 128),
                                                                         in_=src[:, c * 128:(c + 1) * 128], identity=ident),
                         reads=[src_key, "ctab"], writes=[bk(bnk)], inc=(c4 == 3))
                if gcol_off is None:
                    P.op("act", lambda e, hb=hb, bnk=bnk: e.copy(out=dstT[:, hb * 4:hb * 4 + 4, :],
                                                                 in_=bank(bnk).rearrange("p (c t) -> p c t", c=4)),
                         reads=[bk(bnk)], writes=[dst_key])
                else:
                    P.op("dve", lambda e, hb=hb, bnk=bnk: e.tensor_tensor(
                        out=dstT[:, hb * 4:hb * 4 + 4, :], in0=bank(bnk).rearrange("p (c t) -> p c t", c=4),
                        in1=bc(cols[:, gcol_off + hb * 4:gcol_off + hb * 4 + 4], 128), op=ALU.mult),
                        reads=[bk(bnk), "cols"], writes=[dst_key])

        def head_norm(src3, src_key, xc, xc_key, eps_ap, tag):
            P.op("dve", lambda e: e.tensor_reduce(out=hn_m[:], in_=src3, axis=AX.X, op=ALU.add),
                 reads=[src_key], writes=["hn_m"])
            P.op("dve", lambda e: e.tensor_scalar(out=hn_m[:], in0=hn_m[:], scalar1=-1.0 / 64, scalar2=None, op0=ALU.mult),
                 reads=["hn_m"], writes=["hn_m"])
            P.op("dve", lambda e: e.tensor_tensor(out=xc, in0=src3, in1=bc(hn_m[:], 64), op=ALU.add),
                 reads=[src_key, "hn_m"], writes=[xc_key])
            P.op("act", lambda e: e.activation(out=hn_sq[:], in_=xc.rearrange("p h d -> p (h d)"), func=AF.Square),
                 reads=[xc_key], writes=["hn_sq"])
            P.op("dve", lambda e: e.tensor_reduce(out=hn_v[:], in_=hn_sq[:].rearrange("p (h d) -> p h d", h=8), axis=AX.X, op=ALU.add),
                 reads=["hn_sq"], writes=["hn_v"])
            P.op("act", lambda e: e.activation(out=hn_v[:], in_=hn_v[:], func=AF.Sqrt, scale=1.0 / 64, bias=eps_ap),
                 reads=["hn_v", "eps"], writes=["hn_v"])
            P.op("dve", lambda e: e.reciprocal(out=hn_v[:], in_=hn_v[:]), reads=["hn_v"], writes=["hn_v"])
            P.op("dve", lambda e: e.tensor_tensor(out=xc, in0=xc, in1=bc(hn_v[:], 64), op=ALU.mult),
                 reads=[xc_key, "hn_v"], writes=[xc_key])

        eps_t = sb("eps_t", [128, 4])
        P.op("pool", lambda e: e.memset(eps_t[:, 0:1], RMS_EPS), writes=["eps"], inc=False)
        P.op("pool", lambda e: e.memset(eps_t[:, 1:2], GN_EPS), writes=["eps"], inc=False)
        P.op("pool", lambda e: e.memset(eps_t[:, 2:3], RWKV_LN_EPS), writes=["eps"], inc=False)
        P.op("pool", lambda e: e.memset(eps_t[:, 3:4], 0.0), writes=["eps"])
        hn_m = sb("hn_m", [128, 8])
        hn_v = sb("hn_v", [128, 8])
        hn_sq = sb("hn_sq", [128, 512])
        ssq = sb("ssq", [128, 1])
        rstd = sb("rstd", [128, 1])
        junk = sb("junk", [128, 1024])

        if "A" in phases:
            a_mark = aoff[0]
            sa_ = sb

            win_bf = sa_("win_bf", [128, 8, 3840], BF16)
            wout_bf = sa_("wout_bf", [128, 8, 1024], BF16)
            w2a2 = sa_("w2a2", [128, 512])
            g2 = sa_("g2", [128, 512])
            rowsA = sa_("rowsA", [128, 1024])
            w_in_v = w_in.rearrange("(c p) n -> p c n", p=128)
            for c in range(8):
                for hf in range(2):
                    P.dma("pool", lambda e, c=c, hf=hf: e.dma_start(out=win_bf[:, c, hf * 1920:(hf + 1) * 1920],
                                                                    in_=w_in_v[:, c, hf * 1920:(hf + 1) * 1920]),
                          "wload", writes=["win_bf"])
            w_out_v = w_out.rearrange("(c p) n -> p c n", p=128)
            for c in range(8):
                P.dma("pool", lambda e, c=c: e.dma_start(out=wout_bf[:, c, :], in_=w_out_v[:, c, :]),
                      "wload", writes=["wout_bf"])
            P.dma("act", lambda e: e.dma_start(out=w2a2[:], in_=rw_w2a2), "w2a2", writes=["w2a2"])
            P.dma("act", lambda e: e.dma_start(out=g2[:], in_=rw_g2), "g2", writes=["g2"])
            P.dma("act", lambda e: e.dma_start(out=rowsA[:], in_=rows_d[:, 0:1024]), "rowsA", writes=["rowsA"])

            xt = sa_("xt", [128, 1024])
            xs = sa_("xs", [128, 1024])
            xnT = sa_("xnT", [128, 8, 128], BF16)
            cst = sa_("cst", [128, 64])
            qkr = sa_("qkr", [128, 16, 64])
            rt1 = sa_("rt1", [128, 16, 32])
            rt2 = sa_("rt2", [128, 16, 32])
            vtb = sa_("vtb", [128, 512], BF16)
            sgl = sa_("sgl", [128, 512])
            kz = sa_("kz", [128, 8, 64], BF16)
            qbd = sa_("qbd", [128, 4, 2, 128], BF16)
            kT = sa_("kT", [128, 4, 128], BF16)
            qxT = sa_("qxT", [128, 4, 128], BF16)
            PT = sa_("PT", [128, 8, 128], BF16)
            Sst = sa_("Sst", [128, 4, 64])
            Sbd = sa_("Sbd", [128, 4, 2, 64], BF16)
            S0b = sa_("S0b", [128, 2, 4, 64])
            S0bd = sa_("S0bd", [128, 2, 4, 2, 64], BF16)
            kzm = sa_("kzm", [128, 2, 512], BF16)
            xc = sa_("xc", [128, 8, 64])
            oall = sa_("oall", [128, 1024])
            rw = sa_("rw", [128, 14, 144])
            fm = sa_("fm", [128, 14, 128])
            th = sa_("th", [128, 128])
            sgm = sa_("sgm", [128, 4, 128])
            ew = sa_("ew", [128, 4, 128])
            aa = sa_("aa", [128, 4, 128])
            sigfg = sa_("sigfg", [128, 128])
            gate = sa_("gate", [128, 512])
            kk = sa_("kk", [128, 4, 128])
            nrm = sa_("nrm", [128, 4, 128])
            kp = sa_("kp", [128, 4, 128])
            nkka = sa_("nkka", [128, 4, 128])
            prk = sa_("prk", [128, 4, 128])
            sqk = prk
            bonus = sa_("bonus", [128, 8])
            vtok = sa_("vtok", [128, 512])
            ST = sa_("ST", [128, 256])
            NR = 2
            RH = [sa_("RH%d" % i, [128, 256], BF16) for i in range(NR)]
            T2 = [sa_("T2_%d" % i, [128, 256], BF16) for i in range(NR)]
            DV = [sa_("DV%d" % i, [128, 2, 4, 256], BF16) for i in range(2)]
            Qr = [sa_("Qr%d" % i, [128, 256]) for i in range(NR)]
            Pr = [sa_("Pr%d" % i, [128, 256]) for i in range(NR)]
            Mm = sa_("Mm", [128, 256])
            vhi = sa_("vhi", [128, 4, 128], BF16)
            vlo = sa_("vlo", [128, 4, 128], BF16)
            I2h = sa_("I2h", [128, 64], BF16)
            bmh = sa_("bmh", [128, 128], BF16)
            Awh = sa_("Awh", [128, 2, 255], BF16)
            Xw = sa_("Xw", [64, 8, 64])
            shst = Xw[0:16].rearrange("p h j -> p (h j)")
            xc2 = xc
            oT = sa_("oT", [128, 8, 128], BF16)
            h1t = xs

            P.op("pool", lambda e: e.memset(Sst[:], 0.0), writes=["Sst"])
            P.op("pool", lambda e: e.memset(Sbd[:], 0.0), writes=["Sbd"])
            P.op("pool", lambda e: e.memset(qbd[:], 0.0), writes=["qbd"])
            P.op("pool", lambda e: e.memset(S0bd[:], 0.0), writes=["S0bd0", "S0bd1"])
            P.op("pool", lambda e: e.memset(ST[:], 0.0), writes=["ST0", "ST1", "ST2", "ST3"])
            P.op("pool", lambda e: e.memset(rw[:], 0.0), writes=["rw"])

            ST3 = ST[:].rearrange("p (g i) -> p g i", g=4)
            P.op("act", lambda e: e.copy(out=I2h[:], in_=ct("I2")), reads=["ctab"], writes=["I2h"])
            P.op("act", lambda e: e.copy(out=bmh[:], in_=ct("blockmask")), reads=["ctab"], writes=["bmh"])
            P.op("act", lambda e: e.copy(out=Awh[:].rearrange("p h w -> p (h w)"), in_=ct("Awin")), reads=["ctab"], writes=["Awh"])
            I2hb = I2h[:].unsqueeze(1).to_broadcast([128, 4, 64])
            blockmask = ct("blockmask")
            Aw = ct("Awin").rearrange("p (h w) -> p h w", h=2)

            def colb(t_, tl):
                return t_[:, :, tl:tl + 1].to_broadcast([128, 4, 64])

            def wkv_state_in(b):
                P.dma("sp", lambda e: e.dma_start(out=Xw[:], in_=s_wkv[b].rearrange("h i j -> i h j")), "Xw",
                      writes=["Xw"])
                for g in range(4):
                    P.op("pe", lambda e, g=g: e.transpose(out=bank(7, g * 64, g * 64 + 64),
                                                          in_=Xw[:, 2 * g:2 * g + 2, :].rearrange("i h j -> i (h j)"),
                                                          identity=ident[0:64, 0:64]),
                         reads=["Xw", "ctab"], writes=[bk(7)], inc=(g == 3))
                P.op("act", lambda e: e.copy(out=ST[:], in_=bank(7, 0, 256)), reads=[bk(7)], writes=["ST0", "ST1", "ST2", "ST3"])

            def wkv_state_out(dst):
                for g in range(4):
                    P.op("pe", lambda e, g=g: e.transpose(out=ps[0:64, 7 * 512 + g * 128:7 * 512 + g * 128 + 128],
                                                          in_=ST[:, g * 64:(g + 1) * 64], identity=ident),
                         reads=["ST0", "ST1", "ST2", "ST3", "ctab"], writes=[bk(7)], inc=(g == 3))
                P.op("act", lambda e: e.copy(out=Xw[:].rearrange("i h j -> i (h j)"), in_=ps[0:64, 7 * 512:7 * 512 + 512]),
                     reads=[bk(7)], writes=["Xw"])
                P.dma("sp", lambda e: e.dma_start(out=dst.rearrange("h i j -> i h j"), in_=Xw[:]), "Xwo",
                      reads=["Xw"], is_output=True)

            def scan_pre(t, r):
                vb = 2 + (t % 2)
                sl = t % 2
                for g in range(4):
                    P.op("act", lambda e, g=g: e.activation(out=DV[sl][:, 0, 0, g * 64:(g + 1) * 64], in_=I2h[:],
                                                            func=AF.Copy, scale=fm[:, 8 + g, t:t + 1]),
                         reads=["fm", "I2h"], writes=["DV%d" % sl], inc=(g == 3))
                P.op("pe", lambda e: e.matmul(bank(vb, 0, 256), lhsT=bmh[:], rhs=DV[sl][:, 0, 0, :], start=True, stop=True),
                     reads=["DV%d" % sl, "bmh"], writes=[bk(vb)])

            def scan_y(t, r):
                t2 = T2[r]
                P.op("dve", lambda e: e.tensor_tensor(out=t2[:].rearrange("p (g i) -> p g i", g=4), in0=ST3,
                                                      in1=colb(fm[:, 0:4, :], t), op=ALU.mult),
                     reads=["ST0", "ST1", "ST2", "ST3", "fm"], writes=["T2_%d" % r])
                for hh in range(2):
                    P.op("pe", lambda e, hh=hh: e.matmul(
                        bank(6, hh * 256, hh * 256 + 256),
                        lhsT=Awh[:, hh, 127 - t:255 - t], rhs=t2[:], start=(t == 0 and hh == 0), stop=(t == 127 and hh == 1),
                        skip_group_check=True),
                        reads=["T2_%d" % r, "Awh"], writes=[bk(6)], inc=(hh == 1))

            def scan_step(t, r, prev_t):
                rh = RH[r]
                pb = 4 + (t % 2)
                scan_pre(t, r)
                P.op("dve", lambda e: e.tensor_tensor(out=rh[:].rearrange("p (g i) -> p g i", g=4), in0=ST3,
                                                      in1=colb(kk, t), op=ALU.mult),
                     reads=["ST0", "ST1", "ST2", "ST3", "kk"], writes=["RH%d" % r])
                if prev_t is not None:
                    scan_y(prev_t, prev_t % NR)
                P.op("pe", lambda e: e.matmul(bank(pb, 0, 256), lhsT=bmh[:], rhs=rh[:], start=True, stop=True),
                     reads=["RH%d" % r, "bmh"], writes=[bk(pb)])
                dummies(DUM_A, 7, wout_bf[:, 0, :], "wout_bf")
                vb = 2 + (t % 2)
                P.op("pool", lambda e: e.tensor_tensor(out=Pr[r][:].rearrange("p (g i) -> p g i", g=4), in0=ST3,
                                                       in1=colb(ew, t), op=ALU.mult),
                     reads=["ST0", "ST1", "ST2", "ST3", "ew"], writes=["P%d" % r])
                for g in range(4):
                    gs_ = slice(g * 64, (g + 1) * 64)
                    P.op("dve", lambda e, g=g, gs_=gs_: e.scalar_tensor_tensor(
                        out=Qr[r][:, gs_], in0=bank(vb, g * 64, g * 64 + 64), scalar=kp[:, g, t:t + 1], in1=Pr[r][:, gs_],
                        op0=ALU.mult, op1=ALU.add),
                        reads=[bk(vb), "kp", "P%d" % r], writes=["Q%d_%d" % (r, g)])
                for g in range(4):
                    gs_ = slice(g * 64, (g + 1) * 64)
                    P.op("dve", lambda e, g=g, gs_=gs_: e.scalar_tensor_tensor(
                        out=ST[:, gs_], in0=bank(pb, g * 64, g * 64 + 64), scalar=nkka[:, g, t:t + 1], in1=Qr[r][:, gs_],
                        op0=ALU.mult, op1=ALU.add),
                        reads=[bk(pb), "nkka", "Q%d_%d" % (r, g)], writes=["ST%d" % g])

            for n in range(min(NT, nt_limit)):
                smp = (n == 16)
                sfx = "_s" if smp else "_p"
                NB, L = (16, 8) if smp else (1, 128)
                r0, r1 = n * 128, (n + 1) * 128
                P.dma("sp", lambda e, r0=r0, r1=r1: e.dma_start(out=xt[:], in_=x_all[r0:r1, :]), "xt", writes=["xt"])
                P.dma("sp", lambda e, n=n: e.dma_start(out=cst[:], in_=cs_d[n]), "cst", writes=["cst"])
                rms_rstd(xt[:], "xt", junk[:], "junk", ssq[:], rstd[:], "a")
                P.op("dve", lambda e: e.tensor_scalar(out=xs[:], in0=xt[:], scalar1=rstd[:, 0:1], scalar2=None, op0=ALU.mult),
                     reads=["xt", "arstd"], writes=["xs"])
                transposes8(xs, "xs", xnT, "xnT", C_GMIX, 6, 7)
                ckpt("a1")
                for blk in range(4):
                    for c in range(8):
                        P.op("pe", lambda e, blk=blk, c=c: e.matmul(bank(blk), lhsT=xnT[:, c, :],
                                                                    rhs=win_bf[:, c, blk * 512:(blk + 1) * 512],
                                                                    start=(c == 0), stop=(c == 7)),
                             reads=["xnT", "win_bf"], writes=[bk(blk)], inc=(c == 7))
                if n > 0 and not smp:
                    P.op("pool", lambda e: e.tensor_copy(out=rw[:, :, 0:1], in_=rw[:, :, 128:129]), reads=["rw"], writes=["rw"])
                if smp:
                    ssh = fm[0:16, :, :].rearrange("p c t -> p (c t)")
                    P.dma("sp", lambda e: e.dma_start(out=ssh, in_=s_shift), "ssh", writes=["fm"])
                    for c in range(14):
                        P.op("pe", lambda e, c=c: e.transpose(out=bank(6, c * 16, c * 16 + 16), in_=ssh[:, c * 128:(c + 1) * 128],
                                                              identity=ident[0:16, 0:16]),
                             reads=["fm", "ctab"], writes=[bk(6)], inc=(c == 13))
                    P.op("act", lambda e: e.copy(out=rw[:].rearrange("p c (b l) -> p c b l", l=9)[:, :, :, 0:1],
                                                 in_=bank(6, 0, 224).rearrange("p (c b o) -> p c b o", c=14, o=1)),
                         reads=[bk(6)], writes=["rw"])
                for cg in range(4):
                    bnk = 4 + cg % 2
                    ncs = 4 if cg < 3 else 2
                    for c4 in range(ncs):
                        c = cg * 4 + c4
                        for dc in range(8):
                            P.op("pe", lambda e, c=c, c4=c4, dc=dc, bnk=bnk: e.matmul(
                                bank(bnk, c4 * 128, c4 * 128 + 128), lhsT=win_bf[:, dc, 2048 + c * 128:2048 + (c + 1) * 128],
                                rhs=xnT[:, dc, :], start=(dc == 0), stop=(dc == 7)),
                                reads=["xnT", "win_bf"], writes=[bk(bnk)], inc=(dc == 7 and c4 == ncs - 1))
                    if smp:
                        P.op("act", lambda e, cg=cg, ncs=ncs, bnk=bnk: e.copy(
                            out=rw[:, cg * 4:cg * 4 + ncs, :].rearrange("p c (b l) -> p c b l", l=9)[:, :, :, 1:9],
                            in_=bank(bnk, 0, ncs * 128).rearrange("p (c b l) -> p c b l", c=ncs, l=8)),
                            reads=[bk(bnk)], writes=["rw"])
                    else:
                        P.op("act", lambda e, cg=cg, ncs=ncs, bnk=bnk: e.copy(
                            out=rw[:, cg * 4:cg * 4 + ncs, 1:129],
                            in_=bank(bnk, 0, ncs * 128).rearrange("p (c t) -> p c t", c=ncs)),
                            reads=[bk(bnk)], writes=["rw"])
                ckpt("a2")
                qk3 = ps[:, 0:1024].rearrange("p (h d) -> p h d", d=64)
                cosb = cst[:, 0:32].unsqueeze(1).to_broadcast([128, 16, 32])
                sinb = cst[:, 32:64].unsqueeze(1).to_broadcast([128, 16, 32])
                P.op("dve", lambda e: e.tensor_tensor(out=rt1[:], in0=qk3[:, :, 0:32], in1=cosb, op=ALU.mult),
                     reads=[bk(0), bk(1), "cst"], writes=["rt1"])
                P.op("dve", lambda e: e.tensor_tensor(out=rt2[:], in0=qk3[:, :, 32:64], in1=sinb, op=ALU.mult),
                     reads=[bk(0), bk(1), "cst"], writes=["rt2"])
                P.op("pool", lambda e: e.tensor_tensor(out=qkr[:, :, 0:32], in0=rt1[:], in1=rt2[:], op=ALU.subtract),
                     reads=["rt1", "rt2"], writes=["qkr"])
                P.op("dve", lambda e: e.tensor_tensor(out=rt1[:], in0=qk3[:, :, 32:64], in1=cosb, op=ALU.mult),
                     reads=[bk(0), bk(1), "cst"], writes=["rt1"])
                P.op("dve", lambda e: e.tensor_tensor(out=rt2[:], in0=qk3[:, :, 0:32], in1=sinb, op=ALU.mult),
                     reads=[bk(0), bk(1), "cst"], writes=["rt2"])
                P.op("pool", lambda e: e.tensor_tensor(out=qkr[:, :, 32:64], in0=rt1[:], in1=rt2[:], op=ALU.add),
                     reads=["rt1", "rt2"], writes=["qkr"])
                P.op("act", lambda e: e.copy(out=vtb[:], in_=bank(2)), reads=[bk(2)], writes=["vtb"])
                P.op("act", lambda e: e.activation(out=sgl[:], in_=bank(3), func=AF.Silu), reads=[bk(3)], writes=["sgl"])
                Zt = ct("Z" + sfx)
                P.op("dve", lambda e, Zt=Zt: e.tensor_tensor(out=kz[:], in0=qkr[:, 8:16, :], in1=bc(Zt, 64), op=ALU.mult),
                     reads=["qkr", "ctab"], writes=["kz"])
                ckpt("a3")
                qkr2 = qkr[:].rearrange("p (a b) d -> p a (b d)", b=2)
                for pr in range(8):
                    bnk = 2 + pr // 4
                    P.op("pe", lambda e, pr=pr, bnk=bnk: e.transpose(out=bank(bnk, (pr % 4) * 128, (pr % 4) * 128 + 128),
                                                                    in_=qkr2[:, pr, :], identity=ident),
                         reads=["qkr", "ctab"], writes=[bk(bnk)], inc=(pr % 4 == 3))
                for hh in range(2):
                    P.op("act", lambda e, hh=hh: e.copy(out=qbd[hh * 64:(hh + 1) * 64, :, hh, :],
                                                        in_=ps[hh * 64:(hh + 1) * 64, 1024:1536].rearrange("p (g t) -> p g t", g=4)),
                         reads=[bk(2)], writes=["qbd"])
                P.op("act", lambda e: e.activation(out=kT[:].rearrange("p g t -> p (g t)"), in_=bank(3), func=AF.Copy, scale=0.125),
                     reads=[bk(3)], writes=["kT"])
                XIt = ct("XI" + sfx)
                P.op("dve", lambda e, XIt=XIt: e.tensor_tensor(out=qxT[:].rearrange("p g t -> p (g t)"), in0=bank(2), in1=XIt, op=ALU.mult),
                     reads=[bk(2), "ctab"], writes=["qxT"])
                ckpt("a4")
                for g in range(4):
                    bnk = g // 2
                    P.op("pe", lambda e, g=g, bnk=bnk: e.matmul(
                        bank(bnk, (g % 2) * 256, (g % 2) * 256 + 256), lhsT=kT[:, g, :],
                        rhs=qbd[:, g, :, :].rearrange("p a t -> p (a t)"), start=True, stop=True),
                        reads=["kT", "qbd"], writes=[bk(bnk)], inc=(g % 2 == 1))
                DTt = ct("DT" + sfx)
                for hb in range(2):
                    P.op("dve", lambda e, hb=hb, DTt=DTt: e.tensor_tensor(
                        out=PT[:, hb * 4:hb * 4 + 4, :].rearrange("p h t -> p (h t)"), in0=bank(hb),
                        in1=DTt[:, hb * 512:(hb + 1) * 512], op=ALU.mult),
                        reads=[bk(hb), "ctab"], writes=["PT"])
                GCt = ct("GC" + sfx)
                ckpt("a5")

                def state_upd(dst, dst_key, lhs_of_g, lhs_key, ub, GCt=GCt):
                    for g in range(4):
                        P.op("pe", lambda e, g=g: e.matmul(bank(ub, g * 128, g * 128 + 128), lhsT=lhs_of_g(g),
                                                           rhs=vtb[:, g * 128:(g + 1) * 128], start=True, stop=True),
                             reads=[lhs_key, "vtb"], writes=[bk(ub)], inc=(g == 3))
                    P.op("dve", lambda e: e.tensor_tensor(out=dst, in0=dst, in1=bc(GCt, 64), op=ALU.mult),
                         reads=[dst_key, "ctab"], writes=[dst_key])
                    for hh in range(2):
                        P.op("dve", lambda e, hh=hh: e.tensor_tensor(
                            out=dst[hh * 64:(hh + 1) * 64], in0=dst[hh * 64:(hh + 1) * 64],
                            in1=ps[hh * 64:(hh + 1) * 64, ub * 512:(ub + 1) * 512].rearrange("p (g x) -> p g x", g=4)[:, :, hh * 64:hh * 64 + 64],
                            op=ALU.add),
                            reads=[dst_key, bk(ub)], writes=[dst_key])

                if not smp:
                    for g in range(4):
                        for hh in range(2):
                            h = 2 * g + hh
                            P.op("pe", lambda e, h=h, hh=hh: e.matmul(bank(2, h * 64, h * 64 + 64), lhsT=PT[:, h, :],
                                                                      rhs=vtb[:, h * 64:(h + 1) * 64], start=(hh == 0), stop=False,
                                                                      skip_group_check=True),
                                 reads=["PT", "vtb"], writes=[bk(2)], inc=False)
                        P.op("pe", lambda e, g=g: e.matmul(bank(2, g * 128, g * 128 + 128), lhsT=qxT[:, g, :],
                                                           rhs=Sbd[:, g, :, :].rearrange("p a v -> p (a v)"), start=False, stop=True,
                                                           skip_group_check=True),
                             reads=["qxT", "Sbd"], writes=[bk(2)], inc=(g == 3))
                    state_upd(Sst[:], "Sst", lambda g: kz[:, 2 * g:2 * g + 2, :].rearrange("p a d -> p (a d)"), "kz", 3)
                    for hh in range(2):
                        P.op("act", lambda e, hh=hh: e.copy(out=Sbd[hh * 64:(hh + 1) * 64, :, hh, :], in_=Sst[hh * 64:(hh + 1) * 64, :, :]),
                             reads=["Sst"], writes=["Sbd"])
                    if n == 15:
                        for hh in range(2):
                            P.dma("sp", lambda e, hh=hh: e.dma_start(
                                out=retp_o.rearrange("(g hh) d v -> hh d g v", hh=2)[hh], in_=Sst[hh * 64:(hh + 1) * 64, :, :]),
                                "retp", reads=["Sst"], is_output=True)
                    head_norm(bank(2).rearrange("p (h d) -> p h d", h=8), bk(2), xc[:], "xc", eps_t[:, 1:2], "r")
                else:
                    for h in range(8):
                        P.op("pe", lambda e, h=h: e.matmul(bank(2, h * 64, h * 64 + 64), lhsT=PT[:, h, :],
                                                           rhs=vtb[:, h * 64:(h + 1) * 64], start=True, stop=True),
                             reads=["PT", "vtb"], writes=[bk(2)], inc=(h == 7))
                    kzf = kz[:].rearrange("p h d -> p (h d)")
                    for b in range(16):
                        sl = b % 2
                        for hh in range(2):
                            P.dma("sp", lambda e, b=b, hh=hh, sl=sl: e.dma_start(
                                out=S0b[hh * 64:(hh + 1) * 64, sl, :, :],
                                in_=s_ret[b].rearrange("(g hh) d v -> hh d g v", hh=2)[hh]),
                                "S0b%d" % sl, writes=["S0b%d" % sl])
                        for hh in range(2):
                            P.op("act", lambda e, sl=sl, hh=hh: e.copy(out=S0bd[hh * 64:(hh + 1) * 64, sl, :, hh, :],
                                                                       in_=S0b[hh * 64:(hh + 1) * 64, sl, :, :]),
                                 reads=["S0b%d" % sl], writes=["S0bd%d" % sl])
                        for g in range(4):
                            P.op("pe", lambda e, g=g, b=b, sl=sl: e.matmul(
                                bank(3, g * 128 + b * 8, g * 128 + b * 8 + 8), lhsT=S0bd[:, sl, g, :, :].rearrange("p a v -> p (a v)"),
                                rhs=qxT[:, g, b * 8:b * 8 + 8], start=True, stop=True),
                                reads=["S0bd%d" % sl, "qxT"], writes=[bk(3)], inc=(g == 3))
                        P.op("dve", lambda e, b=b, sl=sl: e.tensor_scalar(out=kzm[:, sl, :], in0=kzf, scalar1=ct("Mk2")[:, b:b + 1],
                                                                          scalar2=None, op0=ALU.mult),
                             reads=["kz", "ctab"], writes=["kzm%d" % sl])
                        state_upd(S0b[:, sl], "S0b%d" % sl, lambda g, sl=sl: kzm[:, sl, g * 128:(g + 1) * 128], "kzm%d" % sl, sl)
                        for hh in range(2):
                            P.dma("sp", lambda e, b=b, hh=hh, sl=sl: e.dma_start(
                                out=rets_o[b].rearrange("(g hh) d v -> hh d g v", hh=2)[hh],
                                in_=S0b[hh * 64:(hh + 1) * 64, sl, :, :]),
                                "S0o%d" % sl, reads=["S0b%d" % sl], is_output=True)
                    P.op("act", lambda e: e.copy(out=junk[:, 0:512], in_=bank(3)), reads=[bk(3)], writes=["junk"])
                    for g in range(4):
                        P.op("pe", lambda e, g=g: e.transpose(out=bank(3, g * 128, g * 128 + 128), in_=junk[:, g * 128:(g + 1) * 128],
                                                              identity=ident),
                             reads=["junk", "ctab"], writes=[bk(3)], inc=(g == 3))
                    P.op("act", lambda e: e.copy(out=junk[:, 512:1024], in_=bank(3)), reads=[bk(3)], writes=["junk"])
                    P.op("dve", lambda e: e.tensor_tensor(out=xc[:].rearrange("p h d -> p (h d)"), in0=bank(2), in1=junk[:, 512:1024], op=ALU.add),
                         reads=[bk(2), "junk"], writes=["xc"])
                    head_norm(xc[:], "xc", xc[:], "xc", eps_t[:, 1:2], "r")
                P.op("dve", lambda e: e.tensor_tensor(out=oall[:, 0:512], in0=xc[:].rearrange("p h d -> p (h d)"), in1=sgl[:], op=ALU.mult),
                     reads=["xc", "sgl"], writes=["oall_r"])

                ckpt("a6")
                if smp:
                    rwv = rw[:].rearrange("p c (b l) -> p c b l", l=9)
                    prev, cur = rwv[:, :, :, 0:8], rwv[:, :, :, 1:9]
                    fmv = fm[:].rearrange("p c (b l) -> p c b l", l=8)
                    mub = cols[:, C_MU:C_MU + 14].unsqueeze(2).unsqueeze(3).to_broadcast([128, 14, 16, 8])
                else:
                    prev, cur = rw[:, :, 0:128], rw[:, :, 1:129]
                    fmv = fm[:]
                    mub = bc(cols[:, C_MU:C_MU + 14], 128)
                P.op("dve", lambda e, prev=prev, cur=cur, fmv=fmv: e.tensor_tensor(out=fmv, in0=prev, in1=cur, op=ALU.subtract),
                     reads=["rw"], writes=["fm"])
                P.op("dve", lambda e, fmv=fmv, mub=mub: e.tensor_tensor(out=fmv, in0=fmv, in1=mub, op=ALU.mult),
                     reads=["fm", "cols"], writes=["fm"])
                P.op("dve", lambda e, fmv=fmv, cur=cur: e.tensor_tensor(out=fmv, in0=fmv, in1=cur, op=ALU.add),
                     reads=["fm", "rw"], writes=["fm"])
                if n == 15:
                    P.op("pe", lambda e: e.transpose(out=ps[0:14, 7 * 512:7 * 512 + 128], in_=rw[:, :, 128], identity=ident),
                         reads=["rw", "ctab"], writes=[bk(7)])
                    P.op("act", lambda e: e.copy(out=shst[0:14, 0:128], in_=ps[0:14, 7 * 512:7 * 512 + 128]), reads=[bk(7)], writes=["Xw"])
                    P.dma("sp", lambda e: e.dma_start(out=shiftp_o.rearrange("(c p) -> c p", p=128), in_=shst[0:14, 0:128]),
                          "shp", reads=["Xw"], is_output=True)
                if smp:
                    rwl = rw[:].rearrange("p c (b l) -> p c b l", l=9)
                    for cg in range(4):
                        ncs = 4 if cg < 3 else 2
                        for c4 in range(ncs):
                            c = cg * 4 + c4
                            P.op("pe", lambda e, c=c, c4=c4: e.transpose(out=ps[0:16, 7 * 512 + c4 * 128:7 * 512 + c4 * 128 + 128],
                                                                        in_=rwl[:, c, :, 8], identity=ident),
                                 reads=["rw", "ctab"], writes=[bk(7)], inc=(c4 == ncs - 1))
                        P.op("act", lambda e, ncs=ncs: e.copy(out=shst[0:16, 0:ncs * 128], in_=ps[0:16, 7 * 512:7 * 512 + ncs * 128]),
                             reads=[bk(7)], writes=["Xw"])
                        P.dma("sp", lambda e, cg=cg, ncs=ncs: e.dma_start(out=shifts_o[:, cg * 512:cg * 512 + ncs * 128], in_=shst[0:16, 0:ncs * 128]),
                              "shs", reads=["Xw"], is_output=True)
                P.op("act", lambda e: e.activation(out=th[0:64, :], in_=fm[0:64, 12, :], func=AF.Tanh), reads=["fm"], writes=["th"])
                for g in range(4):
                    P.op("pe", lambda e, g=g: e.matmul(bank(0, g * 128, g * 128 + 128), lhsT=w2a2[0:64, g * 128:(g + 1) * 128],
                                                       rhs=th[0:64, :], start=True, stop=True),
                         reads=["w2a2", "th"], writes=[bk(0)], inc=(g == 3))
                for g in range(4):
                    P.op("pe", lambda e, g=g: e.matmul(bank(1, g * 128, g * 128 + 128), lhsT=w2a2[64:128, g * 128:(g + 1) * 128],
                                                       rhs=fm[64:128, 12, :], start=True, stop=True),
                         reads=["w2a2", "fm"], writes=[bk(1)], inc=(g == 3))
                for g in range(4):
                    P.op("act", lambda e, g=g: e.activation(out=sgm[:, g, :], in_=bank(0, g * 128, g * 128 + 128), func=AF.Sigmoid,
                                                            bias=cols[:, C_W0 + g:C_W0 + g + 1]),
                         reads=[bk(0), "cols"], writes=["sgm"])
                P.op("act", lambda e: e.activation(out=ew[:], in_=sgm[:], func=AF.Exp, scale=-0.6065306597126334),
                     reads=["sgm"], writes=["ew"])
                for g in range(4):
                    P.op("act", lambda e, g=g: e.activation(out=aa[:, g, :], in_=bank(1, g * 128, g * 128 + 128), func=AF.Sigmoid,
                                                            bias=cols[:, C_A0 + g:C_A0 + g + 1]),
                         reads=[bk(1), "cols"], writes=["aa"])
                P.op("act", lambda e: e.activation(out=sigfg[:], in_=fm[:, 13, :], func=AF.Sigmoid), reads=["fm"], writes=["sigfg"])
                P.op("pe", lambda e: e.matmul(bank(0), lhsT=sigfg[:], rhs=g2[:], start=True, stop=True),
                     reads=["sigfg", "g2"], writes=[bk(0)])
                P.op("act", lambda e: e.copy(out=gate[:], in_=bank(0)), reads=[bk(0)], writes=["gate"])
                P.op("dve", lambda e: e.tensor_tensor(out=kk[:], in0=fm[:, 4:8, :], in1=bc(cols[:, C_KK:C_KK + 4], 128), op=ALU.mult),
                     reads=["fm", "cols"], writes=["kk"])
                P.op("act", lambda e: e.activation(out=sqk[:], in_=kk[:], func=AF.Square), reads=["kk"], writes=["prk"])
                P.op("pe", lambda e: e.matmul(bank(1), lhsT=blockmask, rhs=sqk[:].rearrange("p g t -> p (g t)"), start=True, stop=True),
                     reads=["prk", "ctab"], writes=[bk(1)])
                P.op("act", lambda e: e.activation(out=nrm[:].rearrange("p g t -> p (g t)"), in_=bank(1), func=AF.Sqrt),
                     reads=[bk(1)], writes=["nrm"])
                P.op("dve", lambda e: e.tensor_scalar(out=nrm[:], in0=nrm[:], scalar1=1e-12, scalar2=None, op0=ALU.max),
                     reads=["nrm"], writes=["nrm"])
                P.op("dve", lambda e: e.reciprocal(out=nrm[:], in_=nrm[:]), reads=["nrm"], writes=["nrm"])
                P.op("dve", lambda e: e.tensor_tensor(out=kk[:], in0=kk[:], in1=nrm[:], op=ALU.mult), reads=["kk", "nrm"], writes=["kk"])
                P.op("dve", lambda e: e.tensor_tensor(out=kp[:], in0=aa[:], in1=bc(cols[:, C_KA:C_KA + 4], 128), op=ALU.mult),
                     reads=["aa", "cols"], writes=["kp"])
                P.op("dve", lambda e: e.tensor_tensor(out=kp[:], in0=kp[:], in1=bc(cols[:, C_OMKA:C_OMKA + 4], 128), op=ALU.add),
                     reads=["kp", "cols"], writes=["kp"])
                P.op("dve", lambda e: e.tensor_tensor(out=kp[:], in0=kp[:], in1=fm[:, 4:8, :], op=ALU.mult), reads=["kp", "fm"], writes=["kp"])
                P.op("dve", lambda e: e.scalar_tensor_tensor(out=nkka[:].rearrange("p g t -> p (g t)"), in0=kk[:].rearrange("p g t -> p (g t)"),
                                                             scalar=-1.0, in1=aa[:].rearrange("p g t -> p (g t)"), op0=ALU.mult, op1=ALU.mult),
                     reads=["kk", "aa"], writes=["nkka"])
                P.op("dve", lambda e: e.tensor_tensor(out=prk[:], in0=fm[:, 0:4, :], in1=kp[:], op=ALU.mult), reads=["fm", "kp"], writes=["prk"])
                P.op("dve", lambda e: e.tensor_tensor(out=prk[:], in0=prk[:], in1=bc(cols[:, C_RK:C_RK + 4], 128), op=ALU.mult),
                     reads=["prk", "cols"], writes=["prk"])
                for g in range(4):
                    P.op("pe", lambda e, g=g: e.matmul(bank(7, 2 * g, 2 * g + 2), lhsT=prk[:, g, :], rhs=ct("halfmask2"),
                                                       start=True, stop=True),
                         reads=["prk", "ctab"], writes=[bk(7)], inc=(g == 3))
                P.op("act", lambda e: e.copy(out=bonus[:], in_=bank(7, 0, 8)), reads=[bk(7)], writes=["bonus"])
                for g in range(4):
                    P.op("pe", lambda e, g=g: e.transpose(out=bank(7, g * 128, g * 128 + 128), in_=fm[:, 8 + g, :], identity=ident),
                         reads=["fm", "ctab"], writes=[bk(7)], inc=(g == 3))
                P.op("act", lambda e: e.copy(out=vtok[:], in_=bank(7)), reads=[bk(7)], writes=["vtok"])
                P.op("act", lambda e: e.copy(out=vhi[:], in_=fm[:, 8:12, :]), reads=["fm"], writes=["vhi"])
                P.op("dve", lambda e: e.tensor_tensor(out=vlo[:], in0=fm[:, 8:12, :], in1=vhi[:], op=ALU.subtract),
                     reads=["fm", "vhi"], writes=["vlo"])

                ckpt("a7")
                for b in range(NB):
                    if smp:
                        wkv_state_in(b)
                    for l in range(L):
                        t = b * L + l
                        scan_step(t, t % NR, (t - 1) if l > 0 else None)
                    scan_y(b * L + L - 1, (b * L + L - 1) % NR)
                    if smp:
                        wkv_state_out(wkvs_o[b])
                if n == 15:
                    wkv_state_out(wkvp_o)

                ckpt("a8")
                P.op("act", lambda e: e.copy(out=xc2[:].rearrange("p (g hh) i -> p g hh i", hh=2),
                                             in_=bank(6).rearrange("p (hh g i) -> p g hh i", hh=2, g=4)),
                     reads=[bk(6)], writes=["xc"])
                head_norm(xc2[:], "xc", xc2[:], "xc", eps_t[:, 2:3], "w")
                xc2f = xc2[:].rearrange("p h d -> p (h d)")
                P.op("dve", lambda e: e.tensor_tensor(out=xc2f, in0=xc2f, in1=rowsA[:, 0:512], op=ALU.mult), reads=["xc", "rowsA"], writes=["xc"])
                P.op("dve", lambda e: e.tensor_tensor(out=xc2f, in0=xc2f, in1=rowsA[:, 512:1024], op=ALU.add), reads=["xc", "rowsA"], writes=["xc"])
                P.op("dve", lambda e: e.tensor_tensor(out=hn_sq[:].rearrange("p (h d) -> p h d", h=8),
                                                      in0=vtok[:].rearrange("p (h d) -> p h d", h=8), in1=bc(bonus[:], 64), op=ALU.mult),
                     reads=["vtok", "bonus"], writes=["hn_sq"])
                P.op("dve", lambda e: e.tensor_tensor(out=xc2f, in0=xc2f, in1=hn_sq[:], op=ALU.add), reads=["xc", "hn_sq"], writes=["xc"])
                P.op("dve", lambda e: e.tensor_tensor(out=oall[:, 512:1024], in0=xc2f, in1=gate[:], op=ALU.mult),
                     reads=["xc", "gate"], writes=["oall_w"])
                if n == 1:
                    dbg("oall", oall[:], [128, 1024], ["oall_r", "oall_w"])
                for hb, bnk in ((0, 2), (1, 3)):
                    for c4 in range(4):
                        c = hb * 4 + c4
                        P.op("pe", lambda e, c=c, c4=c4, bnk=bnk: e.transpose(out=bank(bnk, c4 * 128, c4 * 128 + 128),
                                                                             in_=oall[:, c * 128:(c + 1) * 128], identity=ident),
                             reads=["oall_r", "oall_w", "ctab"], writes=[bk(bnk)], inc=(c4 == 3))
                    P.op("act", lambda e, hb=hb, bnk=bnk: e.copy(out=oT[:, hb * 4:hb * 4 + 4, :].rearrange("p c t -> p (c t)"), in_=bank(bnk)),
                         reads=[bk(bnk)], writes=["oT"])
                for nb_ in range(2):
                    for c in range(8):
                        P.op("pe", lambda e, nb_=nb_, c=c: e.matmul(bank(nb_), lhsT=oT[:, c, :], rhs=wout_bf[:, c, nb_ * 512:(nb_ + 1) * 512],
                                                                    start=(c == 0), stop=(c == 7)),
                             reads=["oT", "wout_bf"], writes=[bk(nb_)], inc=(c == 7))
                    P.op("dve", lambda e, nb_=nb_: e.tensor_tensor(out=h1t[:, nb_ * 512:(nb_ + 1) * 512], in0=bank(nb_),
                                                                   in1=xt[:, nb_ * 512:(nb_ + 1) * 512], op=ALU.add),
                         reads=[bk(nb_), "xt"], writes=["xs"])
                P.dma("sp", lambda e, r0=r0, r1=r1: e.dma_start(out=h1_d[r0:r1, :], in_=h1t[:]), "h1o",
                      reads=["xs"], writes=["h1d%d" % n])
                if "h1" in debug:
                    if n == 0:
                        dbg_outs["h1"] = dout("dbg_h1", [TOK, D])
                    o_ = dbg_outs["h1"]
                    P.dma("sp", lambda e, r0=r0, r1=r1, o_=o_: e.dma_start(out=o_[r0:r1, :], in_=h1t[:]), "dbgh1",
                          reads=["xs"], is_output=True)
            P.barrier()
            print("arena A watermark", aoff[0], "of", ARENA_F32)
            aoff[0] = a_mark

        if "B" in phases:
            wq_bf = sb("wq_bf", [128, 8, 2048], BF16)
            pg_bf = sb("pg_bf", [128, 8, 1024], BF16)
            pp_bf = sb("pp_bf", [128, 2, 1024], BF16)
            keysT = sb("keysT", [128, 16, 128], BF16)
            keys_st = sb("keys_st", [128, 16, 128])
            B32 = sb("B32", [128, 32, 128], BF16)
            rowsB = sb("rowsB", [128, 2048])
            wq_v = peer_wq.rearrange("(c p) n -> p c n", p=128)
            for c in range(8):
                P.dma("pool", lambda e, c=c: e.dma_start(out=wq_bf[:, c, :], in_=wq_v[:, c, :]), "wload", writes=["wq_bf"])
            pg_v = ple_gate.rearrange("(c p) n -> p c n", p=128)
            for c in range(8):
                P.dma("pool", lambda e, c=c: e.dma_start(out=pg_bf[:, c, :], in_=pg_v[:, c, :]), "wload", writes=["pg_bf"])
            pp_v = ple_proj.rearrange("(c p) n -> p c n", p=128)
            for c in range(2):
                P.dma("pool", lambda e, c=c: e.dma_start(out=pp_bf[:, c, :], in_=pp_v[:, c, :]), "wload", writes=["pp_bf"])
            P.dma("act", lambda e: e.dma_start(out=keys_st[:], in_=peer_keys.rearrange("c n d -> n c d")), "keys", writes=["keys_st"])
            P.dma("act", lambda e: e.dma_start(out=rowsB[:], in_=rows_d[:, 1024:3072]), "rowsB", writes=["rowsB"])
            for c in range(16):
                bnk = 6 + (c // 4) % 2
                P.op("pe", lambda e, c=c, bnk=bnk: e.transpose(out=bank(bnk, (c % 4) * 128, (c % 4) * 128 + 128), in_=keys_st[:, c, :],
                                                               identity=ident),
                     reads=["keys_st", "ctab"], writes=[bk(bnk)], inc=(c % 4 == 3))
                if c % 4 == 3:
                    P.op("act", lambda e, c=c, bnk=bnk: e.copy(out=keysT[:, c - 3:c + 1, :].rearrange("p c n -> p (c n)"), in_=bank(bnk)),
                         reads=[bk(bnk)], writes=["keysT"])
            P.op("pool", lambda e: e.memset(B32[:], 1.0), writes=["B32"])
            for q in range(3):
                P.op("pool", lambda e, q=q: e.affine_select(out=B32[q * 32:(q + 1) * 32], in_=B32[q * 32:(q + 1) * 32],
                                                            pattern=[[1, 32], [0, 128]], compare_op=ALU.is_equal, fill=0.0,
                                                            base=0, channel_multiplier=-1),
                     reads=["B32"], writes=["B32"])

            h1bs = [sb("h1tB%d" % i, [128, 1024]) for i in range(2)]
            pts = [sb("pt%d" % i, [128, 256]) for i in range(2)]
            xs2 = sb("xs2", [128, 1024])
            xn2T = sb("xn2T", [128, 8, 128], BF16)
            xn2b = sb("xn2b", [128, 1024], BF16)
            xhi = sb("xhi", [32, 1024], BF16)
            qTb = sb("qTb", [128, 16, 128], BF16)
            Ssb = sb("Ssb", [128, 16, 128])
            S2 = sb("S2", [128, 256])
            sv = sb("sv", [128, 16, 16])
            siu = sb("siu", [128, 16, 16], U32)
            sif = sb("sif", [128, 16, 16])
            cand = sb("cand", [128, 8, 256])
            cv = sb("cv", [128, 8, 16])
            ciu = sb("ciu", [128, 8, 16], U32)
            abu = sb("abu", [128, 2, 128], U32)
            abf = sb("abf", [128, 2, 128])
            oh = sb("oh", [128, 8, 256])
            e01 = sb("e01", [128, 2, 128])
            ef = sb("ef", [128, 128])
            ex = sb("ex", [128, 8, 16])
            zs = sb("zs", [128, 8])
            gsm = sb("gsm", [128, 128])
            eTus = [sb("eTu%d" % i, [128, 128], U32) for i in range(2)]
            gsmT = sb("gsmT", [128, 128])
            hvT = sb("hvT", [128, 128])
            gl = sb("gl", [128, 128])
            actb = sb("actb", [128, 128], BF16)
            NU, NV, NA = 4, 4, 4
            Ug = [sb("Ug%d" % i, [128, 1024]) for i in range(NU)]
            Vg = [sb("Vg%d" % i, [128, 1024], BF16) for i in range(NV)]
            Awr = [sb("Awr%d" % i, [128, 255], BF16) for i in range(NA)]
            h2t = sb("h2t", [128, 1024])
            xn3T = sb("xn3T", [128, 8, 128], BF16)
            gs = sb("gs", [128, 1024])
            pT = sb("pT", [128, 2, 128], BF16)
            yt = sb("yt", [128, 1024])
            for i in range(NA):
                P.op("pool", lambda e, i=i: e.memset(Awr[i][:], 0.0), writes=["Awr%d" % i])

            def top16(src, src_key, dst_v, dst_i, width):
                P.op("dve", lambda e: e.max(out=dst_v[:, 0:8], in_=src), reads=[src_key], writes=["tk_v"])
                P.op("dve", lambda e: e.max_index(out=dst_i[:, 0:8], in_max=dst_v[:, 0:8], in_values=src), reads=[src_key, "tk_v"], writes=["tk_i"])
                P.op("dve", lambda e: e.match_replace(out=S2[:, 0:width], in_to_replace=dst_v[:, 0:8], in_values=src, imm_value=-1e30),
                     reads=[src_key, "tk_v"], writes=["S2"])
                P.op("dve", lambda e: e.max(out=dst_v[:, 8:16], in_=S2[:, 0:width]), reads=["S2"], writes=["tk_v"])
                P.op("dve", lambda e: e.max_index(out=dst_i[:, 8:16], in_max=dst_v[:, 8:16], in_values=S2[:, 0:width]),
                     reads=["S2", "tk_v"], writes=["tk_i"])

            def stage_F(n):
                r0, r1 = n * 128, (n + 1) * 128
                h1b = h1bs[n % 2]; eTu = eTus[n % 2]; pt = pts[n % 2]
                h1k = "h1t%d" % (n % 2); eTk = "eTu%d" % (n % 2); ptk = "pt%d" % (n % 2)
                P.dma("sp", lambda e, r0=r0, r1=r1: e.dma_start(out=h1b[:], in_=h1_d[r0:r1, :]), "h1i", reads=["h1d%d" % n], writes=[h1k])
                P.dma("sp", lambda e, r0=r0, r1=r1: e.dma_start(out=pt[:], in_=p_all[r0:r1, :]), ptk, writes=[ptk])
                rms_rstd(h1b[:], h1k, junk[:], "junk", ssq[:], rstd[:], "b")
                P.op("dve", lambda e: e.tensor_scalar(out=xs2[:], in0=h1b[:], scalar1=rstd[:, 0:1], scalar2=None, op0=ALU.mult),
                     reads=[h1k, "brstd"], writes=["xs2"])
                transposes8(xs2, "xs2", xn2T, "xn2T", C_GFFN, 6, 7)
                yield
                P.op("dve", lambda e: e.tensor_tensor(out=xn2b[:], in0=xs2[:], in1=rowsB[:, 1024:2048], op=ALU.mult),
                     reads=["xs2", "rowsB"], writes=["xn2b"])
                P.dma("act", lambda e: e.dma_start(out=xhi[:], in_=xn2b[96:128, :]), "xhi", reads=["xn2b"], writes=["xhi"])
                for cg in range(4):
                    bnk = 6 + cg % 2
                    for c4 in range(4):
                        ch = cg * 4 + c4
                        for dc in range(8):
                            P.op("pe", lambda e, ch=ch, c4=c4, dc=dc, bnk=bnk: e.matmul(
                                bank(bnk, c4 * 128, c4 * 128 + 128), lhsT=wq_bf[:, dc, ch * 128:(ch + 1) * 128], rhs=xn2T[:, dc, :],
                                start=(dc == 0), stop=(dc == 7)),
                                reads=["wq_bf", "xn2T"], writes=[bk(bnk)], inc=(dc == 7 and c4 == 3))
                    P.op("act", lambda e, cg=cg, bnk=bnk: e.copy(out=qTb[:, cg * 4:cg * 4 + 4, :].rearrange("p c t -> p (c t)"), in_=bank(bnk)),
                         reads=[bk(bnk)], writes=["qTb"])
                    yield
                for cg in range(4):
                    bnk = 6 + cg % 2
                    for c4 in range(4):
                        ch = cg * 4 + c4
                        P.op("pe", lambda e, ch=ch, c4=c4, bnk=bnk: e.matmul(bank(bnk, c4 * 128, c4 * 128 + 128), lhsT=qTb[:, ch, :],
                                                                             rhs=keysT[:, ch, :], start=True, stop=True),
                             reads=["qTb", "keysT"], writes=[bk(bnk)], inc=(c4 == 3))
                    P.op("act", lambda e, cg=cg, bnk=bnk: e.copy(out=Ssb[:, cg * 4:cg * 4 + 4, :].rearrange("p c t -> p (c t)"), in_=bank(bnk)),
                         reads=[bk(bnk)], writes=["Ssb"])
                    yield
                for ch in range(16):
                    top16(Ssb[:, ch, :], "Ssb", sv[:, ch, :], siu[:, ch, :], 128)
                    if ch % 4 == 3:
                        yield
                P.op("dve", lambda e: e.tensor_copy(out=sif[:], in_=siu[:]), reads=["tk_i"], writes=["sif"])
                sv4 = sv[:].rearrange("p (h c) k -> p h c k", c=2)
                P.op("dve", lambda e: e.tensor_tensor(out=cand[:].rearrange("p h (a b) -> p h a b", a=16),
                                                      in0=sv4[:, :, 0, :].unsqueeze(3).to_broadcast([128, 8, 16, 16]),
                                                      in1=sv4[:, :, 1, :].unsqueeze(2).to_broadcast([128, 8, 16, 16]), op=ALU.add),
                     reads=["tk_v"], writes=["cand"])
                yield
                for h in range(8):
                    top16(cand[:, h, :], "cand", cv[:, h, :], ciu[:, h, :], 256)
                    if h % 2 == 1:
                        yield
                ciu2 = ciu[:].rearrange("p h k -> p (h k)")
                P.op("dve", lambda e: e.tensor_single_scalar(out=abu[:, 0, :], in_=ciu2, scalar=4, op=ALU.logical_shift_right),
                     reads=["tk_i"], writes=["abu"])
                P.op("dve", lambda e: e.tensor_single_scalar(out=abu[:, 1, :], in_=ciu2, scalar=15, op=ALU.bitwise_and),
                     reads=["tk_i"], writes=["abu"])
                P.op("dve", lambda e: e.tensor_copy(out=abf[:], in_=abu[:]), reads=["abu"], writes=["abf"])
                sif4 = sif[:].rearrange("p (h c) k -> p h c k", c=2)
                iob = ct("iota16").unsqueeze(1).unsqueeze(2).to_broadcast([128, 8, 16, 16])
                oh4 = oh[:].rearrange("p h (k a) -> p h k a", k=16)
                for c in range(2):
                    P.op("dve", lambda e, c=c: e.tensor_tensor(
                        out=oh4, in0=iob, in1=abf[:, c, :].rearrange("p (h k) -> p h k", h=8).unsqueeze(3).to_broadcast([128, 8, 16, 16]),
                        op=ALU.is_equal), reads=["abf", "ctab"], writes=["oh"])
                    P.op("dve", lambda e, c=c: e.tensor_tensor(out=oh4, in0=oh4, in1=sif4[:, :, c, :].unsqueeze(2).to_broadcast([128, 8, 16, 16]),
                                                               op=ALU.mult), reads=["oh", "sif"], writes=["oh"])
                    P.op("dve", lambda e, c=c: e.tensor_reduce(out=e01[:, c, :].rearrange("p (h k) -> p h k", h=8), in_=oh4, axis=AX.X, op=ALU.add),
                         reads=["oh"], writes=["e01"])
                    yield
                P.op("dve", lambda e: e.scalar_tensor_tensor(out=ef[:], in0=e01[:, 0, :], scalar=128.0, in1=e01[:, 1, :], op0=ALU.mult, op1=ALU.add),
                     reads=["e01"], writes=["ef"])
                P.op("dve", lambda e: e.tensor_tensor(out=ex[:], in0=cv[:], in1=cv[:, :, 0:1].to_broadcast([128, 8, 16]), op=ALU.subtract),
                     reads=["tk_v"], writes=["ex"])
                P.op("act", lambda e: e.activation(out=ex[:], in_=ex[:], func=AF.Exp), reads=["ex"], writes=["ex"])
                P.op("dve", lambda e: e.tensor_reduce(out=zs[:], in_=ex[:], axis=AX.X, op=ALU.add), reads=["ex"], writes=["zs"])
                P.op("dve", lambda e: e.reciprocal(out=zs[:], in_=zs[:]), reads=["zs"], writes=["zs"])
                P.op("dve", lambda e: e.tensor_tensor(out=gsm[:].rearrange("p (h k) -> p h k", h=8), in0=ex[:], in1=bc(zs[:], 16), op=ALU.mult),
                     reads=["ex", "zs"], writes=["gsm"])
                P.op("pe", lambda e: e.transpose(out=bank(6, 0, 128), in_=ef[:], identity=ident), reads=["ef", "ctab"], writes=[bk(6)])
                P.op("dve", lambda e: e.tensor_copy(out=eTu[:], in_=bank(6, 0, 128)), reads=[bk(6)], writes=[eTk])
                P.op("pe", lambda e: e.transpose(out=bank(7, 0, 128), in_=gsm[:], identity=ident), reads=["gsm", "ctab"], writes=[bk(7)])
                P.op("act", lambda e: e.copy(out=gsmT[:], in_=bank(7, 0, 128)), reads=[bk(7)], writes=["gsmT"])
                if "eT" in debug and n == 0:
                    dbg("eT", ef[:], [128, 128], ["ef"])
                    dbg("gsm", gsm[:], [128, 128], ["gsm"])
                yield

            def stage_U(n, genT=None):
                r0, r1 = n * 128, (n + 1) * 128
                h1b = h1bs[n % 2]; eTu = eTus[n % 2]; pt = pts[n % 2]
                h1k = "h1t%d" % (n % 2); eTk = "eTu%d" % (n % 2); ptk = "pt%d" % (n % 2)
                for t in range(128):
                    su = t % NU
                    P.dma("pool", lambda e, t=t, su=su: e.indirect_dma_start(
                        out=Ug[su][:], out_offset=None, in_=peer_u, in_offset=bass.IndirectOffsetOnAxis(ap=eTu[:, t:t + 1], axis=0)),
                        "Ug%d" % su, reads=[eTk], writes=["Ug%d" % su])
                    q, rr = t // 32, t % 32
                    xb0 = (t % 2) * 2
                    for hf in range(2):
                        if q < 3:
                            P.op("pe", lambda e, q=q, rr=rr, hf=hf, xb0=xb0: e.matmul(
                                bank(xb0 + hf), lhsT=B32[q * 32:(q + 1) * 32, rr, :], rhs=xn2b[q * 32:(q + 1) * 32, hf * 512:(hf + 1) * 512],
                                start=True, stop=True),
                                reads=["B32", "xn2b"], writes=[bk(xb0 + hf)], inc=(hf == 1))
                        else:
                            P.op("pe", lambda e, rr=rr, hf=hf, xb0=xb0: e.matmul(
                                bank(xb0 + hf), lhsT=B32[0:32, rr, :], rhs=xhi[0:32, hf * 512:(hf + 1) * 512],
                                start=True, stop=True),
                                reads=["B32", "xhi"], writes=[bk(xb0 + hf)], inc=(hf == 1))
                    dummies(DUM_B, 6, wq_bf[:, 0, :], "wq_bf")
                    P.op("dve", lambda e, t=t, su=su, xb0=xb0: e.scalar_tensor_tensor(
                        out=oh[:].rearrange("p h x -> p (h x)")[:, 0:1024], in0=Ug[su][:], scalar=1.0, in1=ps[:, xb0 * 512:xb0 * 512 + 1024],
                        op0=ALU.mult, op1=ALU.mult, accum_out=hvT[:, t:t + 1]),
                        reads=["Ug%d" % su, bk(xb0), bk(xb0 + 1)], writes=["oh", "hvT"])
                    if genT is not None and t % 8 == 7:
                        next(genT, None)

            def stage_G(n):
                r0, r1 = n * 128, (n + 1) * 128
                h1b = h1bs[n % 2]; eTu = eTus[n % 2]; pt = pts[n % 2]
                h1k = "h1t%d" % (n % 2); eTk = "eTu%d" % (n % 2); ptk = "pt%d" % (n % 2)
                P.op("act", lambda e: e.activation(out=gl[:], in_=hvT[:], func=AF.Gelu), reads=["hvT"], writes=["gl"])
                P.op("dve", lambda e: e.tensor_tensor(out=actb[:], in0=gl[:], in1=gsmT[:], op=ALU.mult), reads=["gl", "gsmT"], writes=["actb"])

            def stage_V(n, gen):
                r0, r1 = n * 128, (n + 1) * 128
                h1b = h1bs[n % 2]; eTu = eTus[n % 2]; pt = pts[n % 2]
                h1k = "h1t%d" % (n % 2); eTk = "eTu%d" % (n % 2); ptk = "pt%d" % (n % 2)
                for t in range(128):
                    sv_ = t % NV
                    sa = t % NA
                    P.dma("pool", lambda e, t=t, sv_=sv_: e.indirect_dma_start(
                        out=Vg[sv_][:], out_offset=None, in_=peer_v, in_offset=bass.IndirectOffsetOnAxis(ap=eTu[:, t:t + 1], axis=0)),
                        "Vg%d" % sv_, reads=[eTk], writes=["Vg%d" % sv_])
                    P.op("act", lambda e, t=t, sa=sa: e.copy(out=Awr[sa][:, 127:128], in_=actb[:, t:t + 1]),
                         reads=["actb"], writes=["Awr%d" % sa])
                    for hf in range(2):
                        P.op("pe", lambda e, t=t, sv_=sv_, sa=sa, hf=hf: e.matmul(
                            bank(4 + hf), lhsT=Awr[sa][:, 127 - t:255 - t], rhs=Vg[sv_][:, hf * 512:(hf + 1) * 512],
                            start=(t == 0), stop=(t == 127), skip_group_check=True),
                            reads=["Awr%d" % sa, "Vg%d" % sv_], writes=[bk(4 + hf)], inc=(hf == 1))
                    dummies(DUM_B, 6, wq_bf[:, 0, :], "wq_bf")
                    if gen is not None and t % 6 == 5:
                        next(gen, None)
                if gen is not None:
                    for _ in gen:
                        pass

            def stage_T(n):
                r0, r1 = n * 128, (n + 1) * 128
                h1b = h1bs[n % 2]; eTu = eTus[n % 2]; pt = pts[n % 2]
                h1k = "h1t%d" % (n % 2); eTk = "eTu%d" % (n % 2); ptk = "pt%d" % (n % 2)
                for hf in range(2):
                    P.op("dve", lambda e, hf=hf: e.tensor_tensor(out=h2t[:, hf * 512:(hf + 1) * 512], in0=bank(4 + hf),
                                                                 in1=h1b[:, hf * 512:(hf + 1) * 512], op=ALU.add),
                         reads=[bk(4 + hf), h1k], writes=["h2t"])
                if "h2" in debug:
                    if n == 0:
                        dbg_outs["h2"] = dout("dbg_h2", [TOK, D])
                    o_ = dbg_outs["h2"]
                    P.dma("sp", lambda e, r0=r0, r1=r1, o_=o_: e.dma_start(out=o_[r0:r1, :], in_=h2t[:]), "dbgh2",
                          reads=["h2t"], is_output=True)
                yield
                rms_rstd(h2t[:], "h2t", junk[:], "junk", ssq[:], rstd[:], "c")
                P.op("dve", lambda e: e.tensor_scalar(out=xs2[:], in0=h2t[:], scalar1=rstd[:, 0:1], scalar2=None, op0=ALU.mult),
                     reads=["h2t", "crstd"], writes=["xs2"])
                transposes8(xs2, "xs2", xn3T, "xn3T", C_GPLE, 6, 7)
                yield
                for c in range(2):
                    P.op("pe", lambda e, c=c: e.transpose(out=bank(6, c * 128, c * 128 + 128), in_=pt[:, c * 128:(c + 1) * 128], identity=ident),
                         reads=[ptk, "ctab"], writes=[bk(6)], inc=(c == 1))
                P.op("act", lambda e: e.copy(out=pT[:].rearrange("p c t -> p (c t)"), in_=bank(6, 0, 256)), reads=[bk(6)], writes=["pT"])
                yield
                for hf in range(2):
                    for dc in range(8):
                        P.op("pe", lambda e, hf=hf, dc=dc: e.matmul(bank(6 + hf), lhsT=xn3T[:, dc, :], rhs=pg_bf[:, dc, hf * 512:(hf + 1) * 512],
                                                                    start=(dc == 0), stop=(dc == 7)),
                             reads=["xn3T", "pg_bf"], writes=[bk(6 + hf)], inc=(dc == 7))
                    P.op("act", lambda e, hf=hf: e.activation(out=gs[:, hf * 512:(hf + 1) * 512], in_=bank(6 + hf), func=AF.Sigmoid),
                         reads=[bk(6 + hf)], writes=["gs"])
                    yield
                for hf in range(2):
                    for c in range(2):
                        P.op("pe", lambda e, hf=hf, c=c: e.matmul(bank(6 + hf), lhsT=pT[:, c, :], rhs=pp_bf[:, c, hf * 512:(hf + 1) * 512],
                                                                  start=(c == 0), stop=(c == 1)),
                             reads=["pT", "pp_bf"], writes=[bk(6 + hf)], inc=(c == 1))
                    P.op("dve", lambda e, hf=hf: e.tensor_tensor(out=gs[:, hf * 512:(hf + 1) * 512], in0=bank(6 + hf),
                                                                 in1=gs[:, hf * 512:(hf + 1) * 512], op=ALU.mult),
                         reads=[bk(6 + hf), "gs"], writes=["gs"])
                    yield
                P.op("dve", lambda e: e.tensor_tensor(out=h2t[:], in0=h2t[:], in1=gs[:], op=ALU.add), reads=["h2t", "gs"], writes=["h2t"])
                rms_rstd(h2t[:], "h2t", junk[:], "junk", ssq[:], rstd[:], "d")
                P.op("dve", lambda e: e.tensor_scalar(out=yt[:], in0=h2t[:], scalar1=rstd[:, 0:1], scalar2=None, op0=ALU.mult),
                     reads=["h2t", "drstd"], writes=["yt"])
                P.op("dve", lambda e: e.tensor_tensor(out=yt[:], in0=yt[:], in1=rowsB[:, 0:1024], op=ALU.mult), reads=["yt", "rowsB"], writes=["yt"])
                P.dma("sp", lambda e, r0=r0, r1=r1: e.dma_start(out=y_o[r0:r1, :], in_=yt[:]), "yo", reads=["yt"], is_output=True)
                yield

            ntb = min(NT, nt_limit)
            for _ in stage_F(0):
                pass
            genT = None
            for n in range(ntb):
                stage_U(n, genT)
                if genT is not None:
                    for _ in genT:
                        pass
                stage_G(n)
                stage_V(n, stage_F(n + 1) if n + 1 < ntb else None)
                genT = stage_T(n)
            for _ in genT:
                pass

        print("arena end watermark", aoff[0], "of", ARENA_F32)
        P.finish()
        P.run_block(block)
    return nc, tabs, dbg_outs


_CACHE = {}


def make_in_maps(inp, tabs):
    f = np.float32
    g = lambda k: np.asarray(inp[k], dtype=f)

    def colsof(v, nch):
        return np.ascontiguousarray(v.reshape(nch, 128).T)

    cols = np.zeros((128, 64), f)
    cols[:, 0:8] = colsof(g("norm_mix")[0], 8)
    cols[:, 8:16] = colsof(g("norm_ffn")[0], 8)
    cols[:, 16:24] = colsof(g("ple_norm")[0], 8)
    cols[:, 24:38] = colsof(g("rw_mu")[0], 14)
    cols[:, 38:42] = colsof(g("rw_w0")[0], 4)
    cols[:, 42:46] = colsof(g("rw_a0")[0], 4)
    cols[:, 46:50] = colsof(g("rw_kk")[0], 4)
    cols[:, 50:54] = colsof(g("rw_ka")[0], 4)
    cols[:, 54:58] = colsof(g("rw_rk")[0].reshape(512), 4)
    rows = np.zeros((128, 3072), f)
    rows[:, 0:512] = g("rw_ln_g")[0][None, :]
    rows[:, 512:1024] = g("rw_ln_b")[0][None, :]
    rows[:, 1024:2048] = g("norm_final")[None, :]
    rows[:, 2048:3072] = g("norm_ffn")[0][None, :]
    ctab = np.concatenate([tabs[k] for k in CT_ORDER], axis=1).astype(f)
    w2a2 = np.concatenate([g("rw_w2")[0], g("rw_a2")[0]], axis=0)
    shared = dict(
        w_in=g("w_in")[0], w_out=g("w_out")[0], peer_wq=g("peer_wq")[0],
        peer_keys=g("peer_keys")[0].reshape(16, 128, 128), peer_u=g("peer_u")[0], peer_v=g("peer_v")[0],
        ple_gate=g("ple_gate")[0], ple_proj=g("ple_proj")[0], rw_w2a2=np.ascontiguousarray(w2a2), rw_g2=g("rw_g2")[0],
        cols=cols, rows=rows, ctab=ctab, cs=tabs["cs"],
    )
    xp, xs_ = g("x_prompt"), g("x_sample")
    pp, ps_ = g("p_prompt")[0], g("p_sample")[0]
    sr, sw, ss = g("state_ret")[0], g("state_wkv")[0], g("state_shift")[0]
    maps = []
    for c in range(NCORES):
        m = dict(shared)
        m["x_all"] = np.ascontiguousarray(np.concatenate([xp[c], xs_[16 * c:16 * c + 16].reshape(128, D)], axis=0))
        m["p_all"] = np.ascontiguousarray(np.concatenate([pp[c], ps_[16 * c:16 * c + 16].reshape(128, 256)], axis=0))
        m["s_ret"] = np.ascontiguousarray(sr[16 * c:16 * c + 16])
        m["s_wkv"] = np.ascontiguousarray(sw[16 * c:16 * c + 16])
        m["s_shift"] = np.ascontiguousarray(ss[16 * c:16 * c + 16])
        maps.append(m)
    return maps


def kernel(**inputs):
    if "prog" not in _CACHE:
        _CACHE["prog"] = build_program()
    nc, tabs, _ = _CACHE["prog"]
    maps = make_in_maps(inputs, tabs)
    res = run_bass_kernel_spmd(nc, maps, core_ids=list(range(NCORES)))
    R = res.results
    f = np.float32
    y_prompt = np.stack([R[c]["y"][0:2048] for c in range(NCORES)]).astype(f)
    y_sample = np.concatenate([R[c]["y"][2048:].reshape(16, 8, D) for c in range(NCORES)], axis=0).astype(f)
    ret_prompt = np.stack([R[c]["ret_p"] for c in range(NCORES)])[None].astype(f)
    wkv_prompt = np.stack([R[c]["wkv_p"] for c in range(NCORES)])[None].astype(f)
    shift_prompt = np.stack([R[c]["shift_p"] for c in range(NCORES)])[None].astype(f)
    ret_sample = np.concatenate([R[c]["ret_s"] for c in range(NCORES)], axis=0)[None].astype(f)
    wkv_sample = np.concatenate([R[c]["wkv_s"] for c in range(NCORES)], axis=0)[None].astype(f)
    shift_sample = np.concatenate([R[c]["shift_s"] for c in range(NCORES)], axis=0)[None].astype(f)
    return (y_prompt, y_sample, ret_prompt, wkv_prompt, shift_prompt, ret_sample, wkv_sample, shift_sample)
```

```python
import numpy as np
from contextlib import ExitStack
import concourse.bass as bass
import concourse.mybir as mybir
from concourse.bass_utils import run_bass_kernel_spmd

F32 = mybir.dt.float32
BF16 = mybir.dt.bfloat16
U32 = mybir.dt.uint32
AF = mybir.ActivationFunctionType
ALU = mybir.AluOpType
AX = mybir.AxisListType

NCORES = 8
NT = 17
TOK = NT * 128
D = 1024
RMS_EPS = 1e-6
GN_EPS = 1e-5
RWKV_LN_EPS = 64e-5
SAME_ENGINE_SYNC = True
DUM_A = 0
DUM_B = 0


class Prog:
    ENG = ("pe", "dve", "act", "pool", "sp")

    def __init__(self, nc, n_dma_sems=64):
        self.nc = nc
        self.lists = {k: [] for k in self.ENG}
        self.cnt = {k: 0 for k in self.ENG}
        self.waited = {k: {} for k in self.ENG}
        self.bufs = {}
        self.pending = {k: ([], []) for k in self.ENG}
        self.sems = {}
        self.n_dma_sems = n_dma_sems
        self.chan_sem = {}
        self.chan_cnt = {}
        self.chan_last = {}
        self.free_dma = list(range(n_dma_sems))
        self.out_events = []
        self.dead = False

    def alloc_sems(self, stack):
        for k in self.ENG:
            self.sems["prog_" + k] = stack.enter_context(self.nc.semaphore("prog_" + k))
        for i in range(self.n_dma_sems):
            self.sems["dma%d" % i] = stack.enter_context(self.nc.semaphore("dma%d" % i))

    def _chan(self, chan):
        if chan not in self.chan_sem:
            if not self.free_dma:
                raise RuntimeError("out of DMA semaphores")
            self.chan_sem[chan] = "dma%d" % self.free_dma.pop(0)
            self.chan_cnt[chan] = 0
        return self.chan_sem[chan]

    def _b(self, key):
        if key not in self.bufs:
            self.bufs[key] = {"w": None, "r": []}
        return self.bufs[key]

    def _deps(self, reads, writes):
        deps = []
        for k in reads:
            b = self._b(k)
            if b["w"] is not None:
                deps.append(b["w"])
        for k in writes:
            b = self._b(k)
            if b["w"] is not None:
                deps.append(b["w"])
            deps.extend(b["r"])
        return deps

    def _emit_waits(self, e, deps, ss=True):
        own = "prog_" + e
        for (s, v) in deps:
            if s == own and (e == "pe" or not SAME_ENGINE_SYNC):
                continue
            if self.waited[e].get(s, 0) < v:
                self.waited[e][s] = v
                self.lists[e].append(("wait", s, v))

    def _commit(self, ev, reads, writes):
        for k in reads:
            self._b(k)["r"].append(ev)
        for k in writes:
            self.bufs[k] = {"w": ev, "r": []}

    def op(self, e, fn, reads=(), writes=(), inc=True, ss=True):
        if self.dead:
            return None
        reads = list(reads); writes = list(writes)
        self._emit_waits(e, self._deps(reads, writes), ss)
        if not inc:
            self.pending[e][0].extend(reads)
            self.pending[e][1].extend(writes)
            self.lists[e].append(("op", fn, None))
            return None
        self.cnt[e] += 1
        ev = ("prog_" + e, self.cnt[e])
        self.lists[e].append(("op", fn, "prog_" + e))
        pr, pw = self.pending[e]
        self._commit(ev, reads + pr, writes + pw)
        self.pending[e] = ([], [])
        return ev

    def dma(self, q, fn, chan, reads=(), writes=(), is_output=False):
        if self.dead:
            return None
        reads = list(reads); writes = list(writes)
        sem = self._chan(chan)
        deps = self._deps(reads, writes)
        if chan in self.chan_last:
            deps.append(self.chan_last[chan])
        self._emit_waits(q, deps)
        self.chan_cnt[chan] += 16
        ev = (sem, self.chan_cnt[chan])
        self.chan_last[chan] = ev
        self.lists[q].append(("dma", fn, sem))
        self._commit(ev, reads, writes)
        if is_output:
            self.out_events.append(ev)
        return ev

    def barrier(self):
        deps = [ev for ev in self.chan_last.values()]
        for e in self.ENG:
            if self.cnt[e] > 0:
                deps.append(("prog_" + e, self.cnt[e]))
        for e in self.ENG:
            self._emit_waits(e, deps)

    def finish(self):
        deps = list(self.out_events) + [ev for ev in self.chan_last.values()]
        for e in self.ENG:
            if e != "sp" and self.cnt[e] > 0:
                deps.append(("prog_" + e, self.cnt[e]))
        self._emit_waits("sp", deps)

    def replay(self, e, engine):
        for item in self.lists[e]:
            if item[0] == "wait":
                engine.wait_ge(self.sems[item[1]], item[2])
            elif item[0] == "op":
                ins = item[1](engine)
                if item[2] is not None:
                    ins.then_inc(self.sems[item[2]], 1)
            else:
                ins = item[1](engine)
                ins.then_inc(self.sems[item[2]], 16)

    def run_block(self, block):
        p = self

        @block.sync
        def _(eng):
            p.replay("sp", eng)

        @block.tensor
        def _(eng):
            p.replay("pe", eng)

        @block.vector
        def _(eng):
            p.replay("dve", eng)

        @block.scalar
        def _(eng):
            p.replay("act", eng)

        @block.gpsimd
        def _(eng):
            p.replay("pool", eng)


def _const_tables():
    f = np.float32
    H = 8
    lg = np.log(f(1.0) - f(2.0) ** (-5.0 - np.arange(H, dtype=f))).astype(f)
    t = {}

    def decay(e):
        return np.exp(e[None].astype(f) * lg.reshape((H,) + (1,) * e.ndim)).astype(f)

    idx = np.arange(128, dtype=f)
    diff = idx[None, :] - idx[:, None]
    dp = np.where(diff >= 0, decay(np.maximum(diff, 0)), 0).astype(f)
    t["DT_p"] = np.ascontiguousarray(dp.transpose(1, 0, 2)).reshape(128, H * 128)
    b = np.arange(128) // 8
    l = (np.arange(128) % 8).astype(f)
    same = (b[:, None] == b[None, :])
    dl = l[None, :] - l[:, None]
    ds = np.where(same[None] & (dl[None] >= 0), decay(np.maximum(dl, 0)), 0).astype(f)
    t["DT_s"] = np.ascontiguousarray(ds.transpose(1, 0, 2)).reshape(128, H * 128)
    def xi_tab(e):
        x = decay(e)
        o = np.zeros((128, 4, 128), f)
        for g in range(4):
            for hh in range(2):
                o[hh * 64:(hh + 1) * 64, g, :] = x[2 * g + hh][None, :]
        return o.reshape(128, 512)
    t["XI_p"] = xi_tab(idx + 1.0)
    t["XI_s"] = xi_tab(l + 1.0)
    t["Z_p"] = (decay(127.0 - idx).T * f(0.125)).astype(f)
    t["Z_s"] = (decay(7.0 - l).T * f(0.125)).astype(f)
    def gc_tab(C):
        g_ = np.exp(f(C) * lg).astype(f)
        o = np.zeros((128, 4), f)
        for g in range(4):
            for hh in range(2):
                o[hh * 64:(hh + 1) * 64, g] = g_[2 * g + hh]
        return o
    t["GC_p"] = gc_tab(128.0)
    t["GC_s"] = gc_tab(8.0)
    t["Mk2"] = (b[:, None] == np.arange(16)[None, :]).astype(f)
    bm = np.zeros((128, 128), f)
    bm[:64, :64] = 1; bm[64:, 64:] = 1
    t["blockmask"] = bm
    hm = np.zeros((128, 2), f)
    hm[:64, 0] = 1; hm[64:, 1] = 1
    t["halfmask2"] = hm
    aw = np.zeros((128, 2, 255), f)
    aw[:64, 0, 127] = 1; aw[64:, 1, 127] = 1
    t["Awin"] = aw.reshape(128, 510)
    i2 = np.zeros((128, 64), f)
    i2[np.arange(128), np.arange(128) % 64] = 1
    t["I2"] = i2
    t["ident"] = np.eye(128, dtype=f)
    t["iota16"] = np.tile(np.arange(16, dtype=f)[None, :], (128, 1))
    half = 32
    inv = (f(10000.0) ** (-np.arange(half, dtype=f) / f(half))).astype(f)
    cs = np.zeros((NT, 128, 64), f)
    for n in range(NT):
        if n < 16:
            pos = (n * 128 + np.arange(128)).astype(f)
        else:
            pos = (16384 + (np.arange(128) % 8)).astype(f)
        ang = (pos[:, None] * inv[None, :]).astype(f)
        cs[n, :, :32] = np.cos(ang.astype(np.float64)).astype(f)
        cs[n, :, 32:] = np.sin(ang.astype(np.float64)).astype(f)
    t["cs"] = cs
    return t


CT_ORDER = ["DT_p", "DT_s", "XI_p", "XI_s", "Z_p", "Z_s", "GC_p", "GC_s", "Mk2", "blockmask",
            "halfmask2", "Awin", "I2", "ident", "iota16"]


def build_program(phases=("A", "B"), debug=(), nt_limit=NT):
    nc = bass.Bass("TRN2", target_bir_lowering=False)
    P = Prog(nc)
    tabs = _const_tables()
    ct_off = {}
    off = 0
    for k in CT_ORDER:
        ct_off[k] = (off, tabs[k].shape[1])
        off += tabs[k].shape[1]
    NCT = off

    def din(name, shape, dt=F32):
        return nc.dram_tensor(name, list(shape), dt, kind="ExternalInput").ap()

    def dout(name, shape, dt=F32):
        return nc.dram_tensor(name, list(shape), dt, kind="ExternalOutput").ap()

    x_all = din("x_all", [TOK, D])
    p_all = din("p_all", [TOK, 256])
    s_ret = din("s_ret", [16, 8, 64, 64])
    s_wkv = din("s_wkv", [16, 8, 64, 64])
    s_shift = din("s_shift", [16, 1792])
    w_in = din("w_in", [D, 3840])
    w_out = din("w_out", [D, D])
    peer_wq = din("peer_wq", [D, 2048])
    peer_keys = din("peer_keys", [16, 128, 128])
    peer_u = din("peer_u", [16384, D])
    peer_v = din("peer_v", [16384, D])
    ple_gate = din("ple_gate", [D, D])
    ple_proj = din("ple_proj", [256, D])
    rw_w2a2 = din("rw_w2a2", [128, 512])
    rw_g2 = din("rw_g2", [128, 512])
    cols_d = din("cols", [128, 64])
    rows_d = din("rows", [128, 3072])
    ctab_d = din("ctab", [128, NCT])
    cs_d = din("cs", [NT, 128, 64])

    y_o = dout("y", [TOK, D])
    retp_o = dout("ret_p", [8, 64, 64])
    wkvp_o = dout("wkv_p", [8, 64, 64])
    shiftp_o = dout("shift_p", [1792])
    rets_o = dout("ret_s", [16, 8, 64, 64])
    wkvs_o = dout("wkv_s", [16, 8, 64, 64])
    shifts_o = dout("shift_s", [16, 1792])
    h1_d = dout("h1_scr", [TOK, D])
    dbg_outs = {}

    with ExitStack() as st:
        ARENA_F32 = 53200
        arena = st.enter_context(nc.sbuf_tensor("arena", [128, ARENA_F32], F32))
        aoff = [0]

        def sb(name, shape, dt=F32):
            nel = 1
            for d_ in shape[1:]:
                nel *= d_
            esz = 4 if dt in (F32, U32) else 2
            n4 = ((nel * esz + 31) // 32) * 8
            o = aoff[0]
            aoff[0] += n4
            assert aoff[0] <= ARENA_F32, ("arena overflow", name, aoff[0])
            v = arena[0:shape[0], o:o + n4]
            if dt != F32:
                v = v.bitcast(dt)
            v = v[:, 0:nel]
            if len(shape) == 3:
                v = v.rearrange("p (a b) -> p a b", a=shape[1])
            elif len(shape) == 4:
                v = v.rearrange("p (a b c) -> p a b c", a=shape[1], b=shape[2])
            elif len(shape) == 5:
                v = v.rearrange("p (a b c d) -> p a b c d", a=shape[1], b=shape[2], c=shape[3])
            return v

        ps = st.enter_context(nc.psum_tensor("ps", [128, 4096], F32))

        def bank(b, lo=0, hi=512):
            return ps[:, b * 512 + lo:b * 512 + hi]

        def bk(b):
            return "ps%d" % b

        NDUM = [0]

        def dummies(k, bnk, src, src_key):
            for _ in range(k):
                P.op("pe", lambda e: e.matmul(bank(bnk), lhsT=src[:, 0:128], rhs=src[:, 0:512], start=True, stop=True),
                     reads=[src_key], writes=[bk(bnk)], inc=False)

        cols = sb("cols", [128, 64])
        ctab = sb("ctab", [128, NCT])
        C_GMIX, C_GFFN, C_GPLE, C_MU, C_W0, C_A0, C_KK, C_KA, C_RK, C_OMKA = 0, 8, 16, 24, 38, 42, 46, 50, 54, 58

        def ct(name):
            o, w = ct_off[name]
            return ctab[:, o:o + w]

        ident = ct("ident")
        P.alloc_sems(st)
        block = st.enter_context(nc.Block())

        def ckpt(name):
            if ("cut_" + name) in debug:
                P.dead = True

        def dbg(name, ap, shape, reads):
            if name not in debug:
                return
            o = dout("dbg_" + name, shape)
            dbg_outs[name] = o
            P.dma("sp", lambda e: e.dma_start(out=o, in_=ap), "dbg_" + name, reads=reads, is_output=True)

        P.dma("sp", lambda e: e.dma_start(out=cols[:], in_=cols_d), "cols", writes=["cols"])
        P.dma("sp", lambda e: e.dma_start(out=ctab[:], in_=ctab_d), "ctab", writes=["ctab"])
        P.op("dve", lambda e: e.tensor_scalar(out=cols[:, C_OMKA:C_OMKA + 4], in0=cols[:, C_KA:C_KA + 4],
                                              scalar1=-1.0, scalar2=1.0, op0=ALU.mult, op1=ALU.add),
             reads=["cols"], writes=["cols"])

        def bc(ap2, n, axis_last=True):
            k = ap2.shape[1]
            return ap2.unsqueeze(2).to_broadcast([ap2.shape[0], k, n])

        def rms_rstd(src, src_key, junk, junk_key, ssq, rstd, tag):
            P.op("act", lambda e: e.activation(out=junk, in_=src, func=AF.Square, accum_out=ssq),
                 reads=[src_key], writes=[junk_key, tag + "ssq"])
            P.op("act", lambda e: e.activation(out=rstd, in_=ssq, func=AF.Sqrt, scale=1.0 / D, bias=eps_t[:, 0:1]),
                 reads=[tag + "ssq", "eps"], writes=[tag + "rstd"])
            P.op("dve", lambda e: e.reciprocal(out=rstd, in_=rstd), reads=[tag + "rstd"], writes=[tag + "rstd"])

        def transposes8(src, src_key, dstT, dst_key, gcol_off, b0, b1):
            for hb, bnk in ((0, b0), (1, b1)):
                for c4 in range(4):
                    c = hb * 4 + c4
                    P.op("pe", lambda e, c=c, c4=c4, bnk=bnk: e.transpose(out=bank(bnk, c4 * 128, c4 * 128 + 128),
                                                                         in_=src[:, c * 128:(c + 1) * 128], identity=ident),
                         reads=[src_key, "ctab"], writes=[bk(bnk)], inc=(c4 == 3))
                if gcol_off is None:
                    P.op("act", lambda e, hb=hb, bnk=bnk: e.copy(out=dstT[:, hb * 4:hb * 4 + 4, :],
                                                                 in_=bank(bnk).rearrange("p (c t) -> p c t", c=4)),
                         reads=[bk(bnk)], writes=[dst_key])
                else:
                    P.op("dve", lambda e, hb=hb, bnk=bnk: e.tensor_tensor(
                        out=dstT[:, hb * 4:hb * 4 + 4, :], in0=bank(bnk).rearrange("p (c t) -> p c t", c=4),
                        in1=bc(cols[:, gcol_off + hb * 4:gcol_off + hb * 4 + 4], 128), op=ALU.mult),
                        reads=[bk(bnk), "cols"], writes=[dst_key])

        def head_norm(src3, src_key, xc, xc_key, eps_ap, tag):
            P.op("dve", lambda e: e.tensor_reduce(out=hn_m[:], in_=src3, axis=AX.X, op=ALU.add),
                 reads=[src_key], writes=["hn_m"])
            P.op("dve", lambda e: e.tensor_scalar(out=hn_m[:], in0=hn_m[:], scalar1=-1.0 / 64, scalar2=None, op0=ALU.mult),
                 reads=["hn_m"], writes=["hn_m"])
            P.op("dve", lambda e: e.tensor_tensor(out=xc, in0=src3, in1=bc(hn_m[:], 64), op=ALU.add),
                 reads=[src_key, "hn_m"], writes=[xc_key])
            P.op("act", lambda e: e.activation(out=hn_sq[:], in_=xc.rearrange("p h d -> p (h d)"), func=AF.Square),
                 reads=[xc_key], writes=["hn_sq"])
            P.op("dve", lambda e: e.tensor_reduce(out=hn_v[:], in_=hn_sq[:].rearrange("p (h d) -> p h d", h=8), axis=AX.X, op=ALU.add),
                 reads=["hn_sq"], writes=["hn_v"])
            P.op("act", lambda e: e.activation(out=hn_v[:], in_=hn_v[:], func=AF.Sqrt, scale=1.0 / 64, bias=eps_ap),
                 reads=["hn_v", "eps"], writes=["hn_v"])
            P.op("dve", lambda e: e.reciprocal(out=hn_v[:], in_=hn_v[:]), reads=["hn_v"], writes=["hn_v"])
            P.op("dve", lambda e: e.tensor_tensor(out=xc, in0=xc, in1=bc(hn_v[:], 64), op=ALU.mult),
                 reads=[xc_key, "hn_v"], writes=[xc_key])

        eps_t = sb("eps_t", [128, 4])
        P.op("pool", lambda e: e.memset(eps_t[:, 0:1], RMS_EPS), writes=["eps"], inc=False)
        P.op("pool", lambda e: e.memset(eps_t[:, 1:2], GN_EPS), writes=["eps"], inc=False)
        P.op("pool", lambda e: e.memset(eps_t[:, 2:3], RWKV_LN_EPS), writes=["eps"], inc=False)
        P.op("pool", lambda e: e.memset(eps_t[:, 3:4], 0.0), writes=["eps"])
        hn_m = sb("hn_m", [128, 8])
        hn_v = sb("hn_v", [128, 8])
        hn_sq = sb("hn_sq", [128, 512])
        ssq = sb("ssq", [128, 1])
        rstd = sb("rstd", [128, 1])
        junk = sb("junk", [128, 1024])

        if "A" in phases:
            a_mark = aoff[0]
            sa_ = sb

            win_bf = sa_("win_bf", [128, 8, 3840], BF16)
            wout_bf = sa_("wout_bf", [128, 8, 1024], BF16)
            w2a2 = sa_("w2a2", [128, 512])
            g2 = sa_("g2", [128, 512])
            rowsA = sa_("rowsA", [128, 1024])
            w_in_v = w_in.rearrange("(c p) n -> p c n", p=128)
            for c in range(8):
                for hf in range(2):
                    P.dma("pool", lambda e, c=c, hf=hf: e.dma_start(out=win_bf[:, c, hf * 1920:(hf + 1) * 1920],
                                                                    in_=w_in_v[:, c, hf * 1920:(hf + 1) * 1920]),
                          "wload%d" % (c % 4), writes=["win_bf"])
            w_out_v = w_out.rearrange("(c p) n -> p c n", p=128)
            for c in range(8):
                P.dma("pool", lambda e, c=c: e.dma_start(out=wout_bf[:, c, :], in_=w_out_v[:, c, :]),
                      "wload%d" % (c % 4), writes=["wout_bf"])
            P.dma("act", lambda e: e.dma_start(out=w2a2[:], in_=rw_w2a2), "w2a2", writes=["w2a2"])
            P.dma("act", lambda e: e.dma_start(out=g2[:], in_=rw_g2), "g2", writes=["g2"])
            P.dma("act", lambda e: e.dma_start(out=rowsA[:], in_=rows_d[:, 0:1024]), "rowsA", writes=["rowsA"])

            xt = sa_("xt", [128, 1024])
            xs = sa_("xs", [128, 1024])
            xnT = sa_("xnT", [128, 8, 128], BF16)
            cst = sa_("cst", [128, 64])
            qkr = sa_("qkr", [128, 16, 64])
            rt1 = sa_("rt1", [128, 16, 32])
            rt2 = sa_("rt2", [128, 16, 32])
            vtb = sa_("vtb", [128, 512], BF16)
            sgl = sa_("sgl", [128, 512])
            kz = sa_("kz", [128, 8, 64], BF16)
            qbd = sa_("qbd", [128, 4, 2, 128], BF16)
            kT = sa_("kT", [128, 4, 128], BF16)
            qxT = sa_("qxT", [128, 4, 128], BF16)
            PT = sa_("PT", [128, 8, 128], BF16)
            Sst = sa_("Sst", [128, 4, 64])
            Sbd = sa_("Sbd", [128, 4, 2, 64], BF16)
            S0b = sa_("S0b", [128, 2, 4, 64])
            S0bd = sa_("S0bd", [128, 2, 4, 2, 64], BF16)
            kzm = sa_("kzm", [128, 2, 512], BF16)
            xc = sa_("xc", [128, 8, 64])
            oall = sa_("oall", [128, 1024])
            rw = sa_("rw", [128, 14, 144])
            fm = sa_("fm", [128, 14, 128])
            th = sa_("th", [128, 128])
            sgm = sa_("sgm", [128, 4, 128])
            ew = sa_("ew", [128, 4, 128])
            aa = sa_("aa", [128, 4, 128])
            sigfg = sa_("sigfg", [128, 128])
            gate = sa_("gate", [128, 512])
            kk = sa_("kk", [128, 4, 128])
            nrm = sa_("nrm", [128, 4, 128])
            kp = sa_("kp", [128, 4, 128])
            nkka = sa_("nkka", [128, 4, 128])
            prk = sa_("prk", [128, 4, 128])
            sqk = prk
            bonus = sa_("bonus", [128, 8])
            vtok = sa_("vtok", [128, 512])
            ST = sa_("ST", [128, 256])
            NR = 2
            RH = [sa_("RH%d" % i, [128, 256], BF16) for i in range(NR)]
            T2 = [sa_("T2_%d" % i, [128, 256], BF16) for i in range(NR)]
            DV = [sa_("DV%d" % i, [128, 2, 4, 256], BF16) for i in range(2)]
            Qr = [sa_("Qr%d" % i, [128, 256]) for i in range(NR)]
            Pr = [sa_("Pr%d" % i, [128, 256]) for i in range(NR)]
            Mm = sa_("Mm", [128, 256])
            vhi = sa_("vhi", [128, 4, 128], BF16)
            vlo = sa_("vlo", [128, 4, 128], BF16)
            I2h = sa_("I2h", [128, 64], BF16)
            bmh = sa_("bmh", [128, 128], BF16)
            Awh = sa_("Awh", [128, 2, 255], BF16)
            Xw = sa_("Xw", [64, 8, 64])
            shst = Xw[0:16].rearrange("p h j -> p (h j)")
            xc2 = xc
            oT = sa_("oT", [128, 8, 128], BF16)
            h1t = xs

            P.op("pool", lambda e: e.memset(Sst[:], 0.0), writes=["Sst"])
            P.op("pool", lambda e: e.memset(Sbd[:], 0.0), writes=["Sbd"])
            P.op("pool", lambda e: e.memset(qbd[:], 0.0), writes=["qbd"])
            P.op("pool", lambda e: e.memset(S0bd[:], 0.0), writes=["S0bd0", "S0bd1"])
            P.op("pool", lambda e: e.memset(ST[:], 0.0), writes=["ST0", "ST1", "ST2", "ST3"])
            P.op("pool", lambda e: e.memset(rw[:], 0.0), writes=["rw"])

            ST3 = ST[:].rearrange("p (g i) -> p g i", g=4)
            P.op("act", lambda e: e.copy(out=I2h[:], in_=ct("I2")), reads=["ctab"], writes=["I2h"])
            P.op("act", lambda e: e.copy(out=bmh[:], in_=ct("blockmask")), reads=["ctab"], writes=["bmh"])
            P.op("act", lambda e: e.copy(out=Awh[:].rearrange("p h w -> p (h w)"), in_=ct("Awin")), reads=["ctab"], writes=["Awh"])
            I2hb = I2h[:].unsqueeze(1).to_broadcast([128, 4, 64])
            blockmask = ct("blockmask")
            Aw = ct("Awin").rearrange("p (h w) -> p h w", h=2)

            def colb(t_, tl):
                return t_[:, :, tl:tl + 1].to_broadcast([128, 4, 64])

            def wkv_state_in(b):
                P.dma("sp", lambda e: e.dma_start(out=Xw[:], in_=s_wkv[b].rearrange("h i j -> i h j")), "Xw",
                      writes=["Xw"])
                for g in range(4):
                    P.op("pe", lambda e, g=g: e.transpose(out=bank(7, g * 64, g * 64 + 64),
                                                          in_=Xw[:, 2 * g:2 * g + 2, :].rearrange("i h j -> i (h j)"),
                                                          identity=ident[0:64, 0:64]),
                         reads=["Xw", "ctab"], writes=[bk(7)], inc=(g == 3))
                P.op("act", lambda e: e.copy(out=ST[:], in_=bank(7, 0, 256)), reads=[bk(7)], writes=["ST0", "ST1", "ST2", "ST3"])

            def wkv_state_out(dst):
                for g in range(4):
                    P.op("pe", lambda e, g=g: e.transpose(out=ps[0:64, 7 * 512 + g * 128:7 * 512 + g * 128 + 128],
                                                          in_=ST[:, g * 64:(g + 1) * 64], identity=ident),
                         reads=["ST0", "ST1", "ST2", "ST3", "ctab"], writes=[bk(7)], inc=(g == 3))
                P.op("act", lambda e: e.copy(out=Xw[:].rearrange("i h j -> i (h j)"), in_=ps[0:64, 7 * 512:7 * 512 + 512]),
                     reads=[bk(7)], writes=["Xw"])
                P.dma("sp", lambda e: e.dma_start(out=dst.rearrange("h i j -> i h j"), in_=Xw[:]), "Xwo",
                      reads=["Xw"], is_output=True)

            def scan_pre(t, r):
                vb = 2 + (t % 2)
                sl = t % 2
                for g in range(4):
                    P.op("act", lambda e, g=g: e.activation(out=DV[sl][:, 0, 0, g * 64:(g + 1) * 64], in_=I2h[:],
                                                            func=AF.Copy, scale=fm[:, 8 + g, t:t + 1]),
                         reads=["fm", "I2h"], writes=["DV%d" % sl], inc=(g == 3))
                P.op("pe", lambda e: e.matmul(bank(vb, 0, 256), lhsT=bmh[:], rhs=DV[sl][:, 0, 0, :], start=True, stop=True),
                     reads=["DV%d" % sl, "bmh"], writes=[bk(vb)])

            def scan_y(t, r):
                t2 = T2[r]
                P.op("dve", lambda e: e.tensor_tensor(out=t2[:].rearrange("p (g i) -> p g i", g=4), in0=ST3,
                                                      in1=colb(fm[:, 0:4, :], t), op=ALU.mult),
                     reads=["ST0", "ST1", "ST2", "ST3", "fm"], writes=["T2_%d" % r])
                for hh in range(2):
                    P.op("pe", lambda e, hh=hh: e.matmul(
                        bank(6, hh * 256, hh * 256 + 256),
                        lhsT=Awh[:, hh, 127 - t:255 - t], rhs=t2[:], start=(t == 0 and hh == 0), stop=(t == 127 and hh == 1),
                        skip_group_check=True),
                        reads=["T2_%d" % r, "Awh"], writes=[bk(6)], inc=(hh == 1))

            def scan_step(t, r, prev_t):
                rh = RH[r]
                pb = 4 + (t % 2)
                scan_pre(t, r)
                P.op("dve", lambda e: e.tensor_tensor(out=rh[:].rearrange("p (g i) -> p g i", g=4), in0=ST3,
                                                      in1=colb(kk, t), op=ALU.mult),
                     reads=["ST0", "ST1", "ST2", "ST3", "kk"], writes=["RH%d" % r])
                if prev_t is not None:
                    scan_y(prev_t, prev_t % NR)
                P.op("pe", lambda e: e.matmul(bank(pb, 0, 256), lhsT=bmh[:], rhs=rh[:], start=True, stop=True),
                     reads=["RH%d" % r, "bmh"], writes=[bk(pb)])
                dummies(DUM_A, 7, wout_bf[:, 0, :], "wout_bf")
                vb = 2 + (t % 2)
                P.op("pool", lambda e: e.tensor_tensor(out=Pr[r][:].rearrange("p (g i) -> p g i", g=4), in0=ST3,
                                                       in1=colb(ew, t), op=ALU.mult),
                     reads=["ST0", "ST1", "ST2", "ST3", "ew"], writes=["P%d" % r])
                for g in range(4):
                    gs_ = slice(g * 64, (g + 1) * 64)
                    P.op("dve", lambda e, g=g, gs_=gs_: e.scalar_tensor_tensor(
                        out=Qr[r][:, gs_], in0=bank(vb, g * 64, g * 64 + 64), scalar=kp[:, g, t:t + 1], in1=Pr[r][:, gs_],
                        op0=ALU.mult, op1=ALU.add),
                        reads=[bk(vb), "kp", "P%d" % r], writes=["Q%d_%d" % (r, g)])
                for g in range(4):
                    gs_ = slice(g * 64, (g + 1) * 64)
                    P.op("dve", lambda e, g=g, gs_=gs_: e.scalar_tensor_tensor(
                        out=ST[:, gs_], in0=bank(pb, g * 64, g * 64 + 64), scalar=nkka[:, g, t:t + 1], in1=Qr[r][:, gs_],
                        op0=ALU.mult, op1=ALU.add),
                        reads=[bk(pb), "nkka", "Q%d_%d" % (r, g)], writes=["ST%d" % g])

            for n in range(min(NT, nt_limit)):
                smp = (n == 16)
                sfx = "_s" if smp else "_p"
                NB, L = (16, 8) if smp else (1, 128)
                r0, r1 = n * 128, (n + 1) * 128
                P.dma("sp", lambda e, r0=r0, r1=r1: e.dma_start(out=xt[:], in_=x_all[r0:r1, :]), "xt", writes=["xt"])
                P.dma("sp", lambda e, n=n: e.dma_start(out=cst[:], in_=cs_d[n]), "cst", writes=["cst"])
                rms_rstd(xt[:], "xt", junk[:], "junk", ssq[:], rstd[:], "a")
                P.op("dve", lambda e: e.tensor_scalar(out=xs[:], in0=xt[:], scalar1=rstd[:, 0:1], scalar2=None, op0=ALU.mult),
                     reads=["xt", "arstd"], writes=["xs"])
                transposes8(xs, "xs", xnT, "xnT", C_GMIX, 6, 7)
                ckpt("a1")
                for blk in range(4):
                    for c in range(8):
                        P.op("pe", lambda e, blk=blk, c=c: e.matmul(bank(blk), lhsT=xnT[:, c, :],
                                                                    rhs=win_bf[:, c, blk * 512:(blk + 1) * 512],
                                                                    start=(c == 0), stop=(c == 7)),
                             reads=["xnT", "win_bf"], writes=[bk(blk)], inc=(c == 7))
                if n > 0 and not smp:
                    P.op("pool", lambda e: e.tensor_copy(out=rw[:, :, 0:1], in_=rw[:, :, 128:129]), reads=["rw"], writes=["rw"])
                if smp:
                    ssh = fm[0:16, :, :].rearrange("p c t -> p (c t)")
                    P.dma("sp", lambda e: e.dma_start(out=ssh, in_=s_shift), "ssh", writes=["fm"])
                    for c in range(14):
                        P.op("pe", lambda e, c=c: e.transpose(out=bank(6, c * 16, c * 16 + 16), in_=ssh[:, c * 128:(c + 1) * 128],
                                                              identity=ident[0:16, 0:16]),
                             reads=["fm", "ctab"], writes=[bk(6)], inc=(c == 13))
                    P.op("act", lambda e: e.copy(out=rw[:].rearrange("p c (b l) -> p c b l", l=9)[:, :, :, 0:1],
                                                 in_=bank(6, 0, 224).rearrange("p (c b o) -> p c b o", c=14, o=1)),
                         reads=[bk(6)], writes=["rw"])
                for cg in range(4):
                    bnk = 4 + cg % 2
                    ncs = 4 if cg < 3 else 2
                    for c4 in range(ncs):
                        c = cg * 4 + c4
                        for dc in range(8):
                            P.op("pe", lambda e, c=c, c4=c4, dc=dc, bnk=bnk: e.matmul(
                                bank(bnk, c4 * 128, c4 * 128 + 128), lhsT=win_bf[:, dc, 2048 + c * 128:2048 + (c + 1) * 128],
                                rhs=xnT[:, dc, :], start=(dc == 0), stop=(dc == 7)),
                                reads=["xnT", "win_bf"], writes=[bk(bnk)], inc=(dc == 7 and c4 == ncs - 1))
                    if smp:
                        P.op("act", lambda e, cg=cg, ncs=ncs, bnk=bnk: e.copy(
                            out=rw[:, cg * 4:cg * 4 + ncs, :].rearrange("p c (b l) -> p c b l", l=9)[:, :, :, 1:9],
                            in_=bank(bnk, 0, ncs * 128).rearrange("p (c b l) -> p c b l", c=ncs, l=8)),
                            reads=[bk(bnk)], writes=["rw"])
                    else:
                        P.op("act", lambda e, cg=cg, ncs=ncs, bnk=bnk: e.copy(
                            out=rw[:, cg * 4:cg * 4 + ncs, 1:129],
                            in_=bank(bnk, 0, ncs * 128).rearrange("p (c t) -> p c t", c=ncs)),
                            reads=[bk(bnk)], writes=["rw"])
                ckpt("a2")
                qk3 = ps[:, 0:1024].rearrange("p (h d) -> p h d", d=64)
                cosb = cst[:, 0:32].unsqueeze(1).to_broadcast([128, 16, 32])
                sinb = cst[:, 32:64].unsqueeze(1).to_broadcast([128, 16, 32])
                P.op("dve", lambda e: e.tensor_tensor(out=rt1[:], in0=qk3[:, :, 0:32], in1=cosb, op=ALU.mult),
                     reads=[bk(0), bk(1), "cst"], writes=["rt1"])
                P.op("dve", lambda e: e.tensor_tensor(out=rt2[:], in0=qk3[:, :, 32:64], in1=sinb, op=ALU.mult),
                     reads=[bk(0), bk(1), "cst"], writes=["rt2"])
                P.op("pool", lambda e: e.tensor_tensor(out=qkr[:, :, 0:32], in0=rt1[:], in1=rt2[:], op=ALU.subtract),
                     reads=["rt1", "rt2"], writes=["qkr"])
                P.op("dve", lambda e: e.tensor_tensor(out=rt1[:], in0=qk3[:, :, 32:64], in1=cosb, op=ALU.mult),
                     reads=[bk(0), bk(1), "cst"], writes=["rt1"])
                P.op("dve", lambda e: e.tensor_tensor(out=rt2[:], in0=qk3[:, :, 0:32], in1=sinb, op=ALU.mult),
                     reads=[bk(0), bk(1), "cst"], writes=["rt2"])
                P.op("pool", lambda e: e.tensor_tensor(out=qkr[:, :, 32:64], in0=rt1[:], in1=rt2[:], op=ALU.add),
                     reads=["rt1", "rt2"], writes=["qkr"])
                P.op("act", lambda e: e.copy(out=vtb[:], in_=bank(2)), reads=[bk(2)], writes=["vtb"])
                P.op("act", lambda e: e.activation(out=sgl[:], in_=bank(3), func=AF.Silu), reads=[bk(3)], writes=["sgl"])
                Zt = ct("Z" + sfx)
                P.op("dve", lambda e, Zt=Zt: e.tensor_tensor(out=kz[:], in0=qkr[:, 8:16, :], in1=bc(Zt, 64), op=ALU.mult),
                     reads=["qkr", "ctab"], writes=["kz"])
                ckpt("a3")
                qkr2 = qkr[:].rearrange("p (a b) d -> p a (b d)", b=2)
                for pr in range(8):
                    bnk = 2 + pr // 4
                    P.op("pe", lambda e, pr=pr, bnk=bnk: e.transpose(out=bank(bnk, (pr % 4) * 128, (pr % 4) * 128 + 128),
                                                                    in_=qkr2[:, pr, :], identity=ident),
                         reads=["qkr", "ctab"], writes=[bk(bnk)], inc=(pr % 4 == 3))
                for hh in range(2):
                    P.op("act", lambda e, hh=hh: e.copy(out=qbd[hh * 64:(hh + 1) * 64, :, hh, :],
                                                        in_=ps[hh * 64:(hh + 1) * 64, 1024:1536].rearrange("p (g t) -> p g t", g=4)),
                         reads=[bk(2)], writes=["qbd"])
                P.op("act", lambda e: e.activation(out=kT[:].rearrange("p g t -> p (g t)"), in_=bank(3), func=AF.Copy, scale=0.125),
                     reads=[bk(3)], writes=["kT"])
                XIt = ct("XI" + sfx)
                P.op("dve", lambda e, XIt=XIt: e.tensor_tensor(out=qxT[:].rearrange("p g t -> p (g t)"), in0=bank(2), in1=XIt, op=ALU.mult),
                     reads=[bk(2), "ctab"], writes=["qxT"])
                ckpt("a4")
                for g in range(4):
                    bnk = g // 2
                    P.op("pe", lambda e, g=g, bnk=bnk: e.matmul(
                        bank(bnk, (g % 2) * 256, (g % 2) * 256 + 256), lhsT=kT[:, g, :],
                        rhs=qbd[:, g, :, :].rearrange("p a t -> p (a t)"), start=True, stop=True),
                        reads=["kT", "qbd"], writes=[bk(bnk)], inc=(g % 2 == 1))
                DTt = ct("DT" + sfx)
                for hb in range(2):
                    P.op("dve", lambda e, hb=hb, DTt=DTt: e.tensor_tensor(
                        out=PT[:, hb * 4:hb * 4 + 4, :].rearrange("p h t -> p (h t)"), in0=bank(hb),
                        in1=DTt[:, hb * 512:(hb + 1) * 512], op=ALU.mult),
                        reads=[bk(hb), "ctab"], writes=["PT"])
                GCt = ct("GC" + sfx)
                ckpt("a5")

                def state_upd(dst, dst_key, lhs_of_g, lhs_key, ub, GCt=GCt):
                    for g in range(4):
                        P.op("pe", lambda e, g=g: e.matmul(bank(ub, g * 128, g * 128 + 128), lhsT=lhs_of_g(g),
                                                           rhs=vtb[:, g * 128:(g + 1) * 128], start=True, stop=True),
                             reads=[lhs_key, "vtb"], writes=[bk(ub)], inc=(g == 3))
                    P.op("dve", lambda e: e.tensor_tensor(out=dst, in0=dst, in1=bc(GCt, 64), op=ALU.mult),
                         reads=[dst_key, "ctab"], writes=[dst_key])
                    for hh in range(2):
                        P.op("dve", lambda e, hh=hh: e.tensor_tensor(
                            out=dst[hh * 64:(hh + 1) * 64], in0=dst[hh * 64:(hh + 1) * 64],
                            in1=ps[hh * 64:(hh + 1) * 64, ub * 512:(ub + 1) * 512].rearrange("p (g x) -> p g x", g=4)[:, :, hh * 64:hh * 64 + 64],
                            op=ALU.add),
                            reads=[dst_key, bk(ub)], writes=[dst_key])

                if not smp:
                    for g in range(4):
                        for hh in range(2):
                            h = 2 * g + hh
                            P.op("pe", lambda e, h=h, hh=hh: e.matmul(bank(2, h * 64, h * 64 + 64), lhsT=PT[:, h, :],
                                                                      rhs=vtb[:, h * 64:(h + 1) * 64], start=(hh == 0), stop=False,
                                                                      skip_group_check=True),
                                 reads=["PT", "vtb"], writes=[bk(2)], inc=False)
                        P.op("pe", lambda e, g=g: e.matmul(bank(2, g * 128, g * 128 + 128), lhsT=qxT[:, g, :],
                                                           rhs=Sbd[:, g, :, :].rearrange("p a v -> p (a v)"), start=False, stop=True,
                                                           skip_group_check=True),
                             reads=["qxT", "Sbd"], writes=[bk(2)], inc=(g == 3))
                    state_upd(Sst[:], "Sst", lambda g: kz[:, 2 * g:2 * g + 2, :].rearrange("p a d -> p (a d)"), "kz", 3)
                    for hh in range(2):
                        P.op("act", lambda e, hh=hh: e.copy(out=Sbd[hh * 64:(hh + 1) * 64, :, hh, :], in_=Sst[hh * 64:(hh + 1) * 64, :, :]),
                             reads=["Sst"], writes=["Sbd"])
                    if n == 15:
                        for hh in range(2):
                            P.dma("sp", lambda e, hh=hh: e.dma_start(
                                out=retp_o.rearrange("(g hh) d v -> hh d g v", hh=2)[hh], in_=Sst[hh * 64:(hh + 1) * 64, :, :]),
                                "retp", reads=["Sst"], is_output=True)
                    head_norm(bank(2).rearrange("p (h d) -> p h d", h=8), bk(2), xc[:], "xc", eps_t[:, 1:2], "r")
                else:
                    for h in range(8):
                        P.op("pe", lambda e, h=h: e.matmul(bank(2, h * 64, h * 64 + 64), lhsT=PT[:, h, :],
                                                           rhs=vtb[:, h * 64:(h + 1) * 64], start=True, stop=True),
                             reads=["PT", "vtb"], writes=[bk(2)], inc=(h == 7))
                    kzf = kz[:].rearrange("p h d -> p (h d)")
                    for b in range(16):
                        sl = b % 2
                        for hh in range(2):
                            P.dma("sp", lambda e, b=b, hh=hh, sl=sl: e.dma_start(
                                out=S0b[hh * 64:(hh + 1) * 64, sl, :, :],
                                in_=s_ret[b].rearrange("(g hh) d v -> hh d g v", hh=2)[hh]),
                                "S0b%d" % sl, writes=["S0b%d" % sl])
                        for hh in range(2):
                            P.op("act", lambda e, sl=sl, hh=hh: e.copy(out=S0bd[hh * 64:(hh + 1) * 64, sl, :, hh, :],
                                                                       in_=S0b[hh * 64:(hh + 1) * 64, sl, :, :]),
                                 reads=["S0b%d" % sl], writes=["S0bd%d" % sl])
                        for g in range(4):
                            P.op("pe", lambda e, g=g, b=b, sl=sl: e.matmul(
                                bank(3, g * 128 + b * 8, g * 128 + b * 8 + 8), lhsT=S0bd[:, sl, g, :, :].rearrange("p a v -> p (a v)"),
                                rhs=qxT[:, g, b * 8:b * 8 + 8], start=True, stop=True),
                                reads=["S0bd%d" % sl, "qxT"], writes=[bk(3)], inc=(g == 3))
                        P.op("dve", lambda e, b=b, sl=sl: e.tensor_scalar(out=kzm[:, sl, :], in0=kzf, scalar1=ct("Mk2")[:, b:b + 1],
                                                                          scalar2=None, op0=ALU.mult),
                             reads=["kz", "ctab"], writes=["kzm%d" % sl])
                        state_upd(S0b[:, sl], "S0b%d" % sl, lambda g, sl=sl: kzm[:, sl, g * 128:(g + 1) * 128], "kzm%d" % sl, sl)
                        for hh in range(2):
                            P.dma("sp", lambda e, b=b, hh=hh, sl=sl: e.dma_start(
                                out=rets_o[b].rearrange("(g hh) d v -> hh d g v", hh=2)[hh],
                                in_=S0b[hh * 64:(hh + 1) * 64, sl, :, :]),
                                "S0o%d" % sl, reads=["S0b%d" % sl], is_output=True)
                    P.op("act", lambda e: e.copy(out=junk[:, 0:512], in_=bank(3)), reads=[bk(3)], writes=["junk"])
                    for g in range(4):
                        P.op("pe", lambda e, g=g: e.transpose(out=bank(3, g * 128, g * 128 + 128), in_=junk[:, g * 128:(g + 1) * 128],
                                                              identity=ident),
                             reads=["junk", "ctab"], writes=[bk(3)], inc=(g == 3))
                    P.op("act", lambda e: e.copy(out=junk[:, 512:1024], in_=bank(3)), reads=[bk(3)], writes=["junk"])
                    P.op("dve", lambda e: e.tensor_tensor(out=xc[:].rearrange("p h d -> p (h d)"), in0=bank(2), in1=junk[:, 512:1024], op=ALU.add),
                         reads=[bk(2), "junk"], writes=["xc"])
                    head_norm(xc[:], "xc", xc[:], "xc", eps_t[:, 1:2], "r")
                P.op("dve", lambda e: e.tensor_tensor(out=oall[:, 0:512], in0=xc[:].rearrange("p h d -> p (h d)"), in1=sgl[:], op=ALU.mult),
                     reads=["xc", "sgl"], writes=["oall_r"])

                ckpt("a6")
                if smp:
                    rwv = rw[:].rearrange("p c (b l) -> p c b l", l=9)
                    prev, cur = rwv[:, :, :, 0:8], rwv[:, :, :, 1:9]
                    fmv = fm[:].rearrange("p c (b l) -> p c b l", l=8)
                    mub = cols[:, C_MU:C_MU + 14].unsqueeze(2).unsqueeze(3).to_broadcast([128, 14, 16, 8])
                else:
                    prev, cur = rw[:, :, 0:128], rw[:, :, 1:129]
                    fmv = fm[:]
                    mub = bc(cols[:, C_MU:C_MU + 14], 128)
                P.op("dve", lambda e, prev=prev, cur=cur, fmv=fmv: e.tensor_tensor(out=fmv, in0=prev, in1=cur, op=ALU.subtract),
                     reads=["rw"], writes=["fm"])
                P.op("dve", lambda e, fmv=fmv, mub=mub: e.tensor_tensor(out=fmv, in0=fmv, in1=mub, op=ALU.mult),
                     reads=["fm", "cols"], writes=["fm"])
                P.op("dve", lambda e, fmv=fmv, cur=cur: e.tensor_tensor(out=fmv, in0=fmv, in1=cur, op=ALU.add),
                     reads=["fm", "rw"], writes=["fm"])
                if n == 15:
                    P.op("pe", lambda e: e.transpose(out=ps[0:14, 7 * 512:7 * 512 + 128], in_=rw[:, :, 128], identity=ident),
                         reads=["rw", "ctab"], writes=[bk(7)])
                    P.op("act", lambda e: e.copy(out=shst[0:14, 0:128], in_=ps[0:14, 7 * 512:7 * 512 + 128]), reads=[bk(7)], writes=["Xw"])
                    P.dma("sp", lambda e: e.dma_start(out=shiftp_o.rearrange("(c p) -> c p", p=128), in_=shst[0:14, 0:128]),
                          "shp", reads=["Xw"], is_output=True)
                if smp:
                    rwl = rw[:].rearrange("p c (b l) -> p c b l", l=9)
                    for cg in range(4):
                        ncs = 4 if cg < 3 else 2
                        for c4 in range(ncs):
                            c = cg * 4 + c4
                            P.op("pe", lambda e, c=c, c4=c4: e.transpose(out=ps[0:16, 7 * 512 + c4 * 128:7 * 512 + c4 * 128 + 128],
                                                                        in_=rwl[:, c, :, 8], identity=ident),
                                 reads=["rw", "ctab"], writes=[bk(7)], inc=(c4 == ncs - 1))
                        P.op("act", lambda e, ncs=ncs: e.copy(out=shst[0:16, 0:ncs * 128], in_=ps[0:16, 7 * 512:7 * 512 + ncs * 128]),
                             reads=[bk(7)], writes=["Xw"])
                        P.dma("sp", lambda e, cg=cg, ncs=ncs: e.dma_start(out=shifts_o[:, cg * 512:cg * 512 + ncs * 128], in_=shst[0:16, 0:ncs * 128]),
                              "shs", reads=["Xw"], is_output=True)
                P.op("act", lambda e: e.activation(out=th[0:64, :], in_=fm[0:64, 12, :], func=AF.Tanh), reads=["fm"], writes=["th"])
                for g in range(4):
                    P.op("pe", lambda e, g=g: e.matmul(bank(0, g * 128, g * 128 + 128), lhsT=w2a2[0:64, g * 128:(g + 1) * 128],
                                                       rhs=th[0:64, :], start=True, stop=True),
                         reads=["w2a2", "th"], writes=[bk(0)], inc=(g == 3))
                for g in range(4):
                    P.op("pe", lambda e, g=g: e.matmul(bank(1, g * 128, g * 128 + 128), lhsT=w2a2[64:128, g * 128:(g + 1) * 128],
                                                       rhs=fm[64:128, 12, :], start=True, stop=True),
                         reads=["w2a2", "fm"], writes=[bk(1)], inc=(g == 3))
                for g in range(4):
                    P.op("act", lambda e, g=g: e.activation(out=sgm[:, g, :], in_=bank(0, g * 128, g * 128 + 128), func=AF.Sigmoid,
                                                            bias=cols[:, C_W0 + g:C_W0 + g + 1]),
                         reads=[bk(0), "cols"], writes=["sgm"])
                P.op("act", lambda e: e.activation(out=ew[:], in_=sgm[:], func=AF.Exp, scale=-0.6065306597126334),
                     reads=["sgm"], writes=["ew"])
                for g in range(4):
                    P.op("act", lambda e, g=g: e.activation(out=aa[:, g, :], in_=bank(1, g * 128, g * 128 + 128), func=AF.Sigmoid,
                                                            bias=cols[:, C_A0 + g:C_A0 + g + 1]),
                         reads=[bk(1), "cols"], writes=["aa"])
                P.op("act", lambda e: e.activation(out=sigfg[:], in_=fm[:, 13, :], func=AF.Sigmoid), reads=["fm"], writes=["sigfg"])
                P.op("pe", lambda e: e.matmul(bank(0), lhsT=sigfg[:], rhs=g2[:], start=True, stop=True),
                     reads=["sigfg", "g2"], writes=[bk(0)])
                P.op("act", lambda e: e.copy(out=gate[:], in_=bank(0)), reads=[bk(0)], writes=["gate"])
                P.op("dve", lambda e: e.tensor_tensor(out=kk[:], in0=fm[:, 4:8, :], in1=bc(cols[:, C_KK:C_KK + 4], 128), op=ALU.mult),
                     reads=["fm", "cols"], writes=["kk"])
                P.op("act", lambda e: e.activation(out=sqk[:], in_=kk[:], func=AF.Square), reads=["kk"], writes=["prk"])
                P.op("pe", lambda e: e.matmul(bank(1), lhsT=blockmask, rhs=sqk[:].rearrange("p g t -> p (g t)"), start=True, stop=True),
                     reads=["prk", "ctab"], writes=[bk(1)])
                P.op("act", lambda e: e.activation(out=nrm[:].rearrange("p g t -> p (g t)"), in_=bank(1), func=AF.Sqrt),
                     reads=[bk(1)], writes=["nrm"])
                P.op("dve", lambda e: e.tensor_scalar(out=nrm[:], in0=nrm[:], scalar1=1e-12, scalar2=None, op0=ALU.max),
                     reads=["nrm"], writes=["nrm"])
                P.op("dve", lambda e: e.reciprocal(out=nrm[:], in_=nrm[:]), reads=["nrm"], writes=["nrm"])
                P.op("dve", lambda e: e.tensor_tensor(out=kk[:], in0=kk[:], in1=nrm[:], op=ALU.mult), reads=["kk", "nrm"], writes=["kk"])
                P.op("dve", lambda e: e.tensor_tensor(out=kp[:], in0=aa[:], in1=bc(cols[:, C_KA:C_KA + 4], 128), op=ALU.mult),
                     reads=["aa", "cols"], writes=["kp"])
                P.op("dve", lambda e: e.tensor_tensor(out=kp[:], in0=kp[:], in1=bc(cols[:, C_OMKA:C_OMKA + 4], 128), op=ALU.add),
                     reads=["kp", "cols"], writes=["kp"])
                P.op("dve", lambda e: e.tensor_tensor(out=kp[:], in0=kp[:], in1=fm[:, 4:8, :], op=ALU.mult), reads=["kp", "fm"], writes=["kp"])
                P.op("dve", lambda e: e.scalar_tensor_tensor(out=nkka[:].rearrange("p g t -> p (g t)"), in0=kk[:].rearrange("p g t -> p (g t)"),
                                                             scalar=-1.0, in1=aa[:].rearrange("p g t -> p (g t)"), op0=ALU.mult, op1=ALU.mult),
                     reads=["kk", "aa"], writes=["nkka"])
                P.op("dve", lambda e: e.tensor_tensor(out=prk[:], in0=fm[:, 0:4, :], in1=kp[:], op=ALU.mult), reads=["fm", "kp"], writes=["prk"])
                P.op("dve", lambda e: e.tensor_tensor(out=prk[:], in0=prk[:], in1=bc(cols[:, C_RK:C_RK + 4], 128), op=ALU.mult),
                     reads=["prk", "cols"], writes=["prk"])
                for g in range(4):
                    P.op("pe", lambda e, g=g: e.matmul(bank(7, 2 * g, 2 * g + 2), lhsT=prk[:, g, :], rhs=ct("halfmask2"),
                                                       start=True, stop=True),
                         reads=["prk", "ctab"], writes=[bk(7)], inc=(g == 3))
                P.op("act", lambda e: e.copy(out=bonus[:], in_=bank(7, 0, 8)), reads=[bk(7)], writes=["bonus"])
                for g in range(4):
                    P.op("pe", lambda e, g=g: e.transpose(out=bank(7, g * 128, g * 128 + 128), in_=fm[:, 8 + g, :], identity=ident),
                         reads=["fm", "ctab"], writes=[bk(7)], inc=(g == 3))
                P.op("act", lambda e: e.copy(out=vtok[:], in_=bank(7)), reads=[bk(7)], writes=["vtok"])
                P.op("act", lambda e: e.copy(out=vhi[:], in_=fm[:, 8:12, :]), reads=["fm"], writes=["vhi"])
                P.op("dve", lambda e: e.tensor_tensor(out=vlo[:], in0=fm[:, 8:12, :], in1=vhi[:], op=ALU.subtract),
                     reads=["fm", "vhi"], writes=["vlo"])

                ckpt("a7")
                for b in range(NB):
                    if smp:
                        wkv_state_in(b)
                    for l in range(L):
                        t = b * L + l
                        scan_step(t, t % NR, (t - 1) if l > 0 else None)
                    scan_y(b * L + L - 1, (b * L + L - 1) % NR)
                    if smp:
                        wkv_state_out(wkvs_o[b])
                if n == 15:
                    wkv_state_out(wkvp_o)

                ckpt("a8")
                P.op("act", lambda e: e.copy(out=xc2[:].rearrange("p (g hh) i -> p g hh i", hh=2),
                                             in_=bank(6).rearrange("p (hh g i) -> p g hh i", hh=2, g=4)),
                     reads=[bk(6)], writes=["xc"])
                head_norm(xc2[:], "xc", xc2[:], "xc", eps_t[:, 2:3], "w")
                xc2f = xc2[:].rearrange("p h d -> p (h d)")
                P.op("dve", lambda e: e.tensor_tensor(out=xc2f, in0=xc2f, in1=rowsA[:, 0:512], op=ALU.mult), reads=["xc", "rowsA"], writes=["xc"])
                P.op("dve", lambda e: e.tensor_tensor(out=xc2f, in0=xc2f, in1=rowsA[:, 512:1024], op=ALU.add), reads=["xc", "rowsA"], writes=["xc"])
                P.op("dve", lambda e: e.tensor_tensor(out=hn_sq[:].rearrange("p (h d) -> p h d", h=8),
                                                      in0=vtok[:].rearrange("p (h d) -> p h d", h=8), in1=bc(bonus[:], 64), op=ALU.mult),
                     reads=["vtok", "bonus"], writes=["hn_sq"])
                P.op("dve", lambda e: e.tensor_tensor(out=xc2f, in0=xc2f, in1=hn_sq[:], op=ALU.add), reads=["xc", "hn_sq"], writes=["xc"])
                P.op("dve", lambda e: e.tensor_tensor(out=oall[:, 512:1024], in0=xc2f, in1=gate[:], op=ALU.mult),
                     reads=["xc", "gate"], writes=["oall_w"])
                if n == 1:
                    dbg("oall", oall[:], [128, 1024], ["oall_r", "oall_w"])
                for hb, bnk in ((0, 2), (1, 3)):
                    for c4 in range(4):
                        c = hb * 4 + c4
                        P.op("pe", lambda e, c=c, c4=c4, bnk=bnk: e.transpose(out=bank(bnk, c4 * 128, c4 * 128 + 128),
                                                                             in_=oall[:, c * 128:(c + 1) * 128], identity=ident),
                             reads=["oall_r", "oall_w", "ctab"], writes=[bk(bnk)], inc=(c4 == 3))
                    P.op("act", lambda e, hb=hb, bnk=bnk: e.copy(out=oT[:, hb * 4:hb * 4 + 4, :].rearrange("p c t -> p (c t)"), in_=bank(bnk)),
                         reads=[bk(bnk)], writes=["oT"])
                for nb_ in range(2):
                    for c in range(8):
                        P.op("pe", lambda e, nb_=nb_, c=c: e.matmul(bank(nb_), lhsT=oT[:, c, :], rhs=wout_bf[:, c, nb_ * 512:(nb_ + 1) * 512],
                                                                    start=(c == 0), stop=(c == 7)),
                             reads=["oT", "wout_bf"], writes=[bk(nb_)], inc=(c == 7))
                    P.op("dve", lambda e, nb_=nb_: e.tensor_tensor(out=h1t[:, nb_ * 512:(nb_ + 1) * 512], in0=bank(nb_),
                                                                   in1=xt[:, nb_ * 512:(nb_ + 1) * 512], op=ALU.add),
                         reads=[bk(nb_), "xt"], writes=["xs"])
                P.dma("sp", lambda e, r0=r0, r1=r1: e.dma_start(out=h1_d[r0:r1, :], in_=h1t[:]), "h1o",
                      reads=["xs"], writes=["h1d%d" % n])
                if "h1" in debug:
                    if n == 0:
                        dbg_outs["h1"] = dout("dbg_h1", [TOK, D])
                    o_ = dbg_outs["h1"]
                    P.dma("sp", lambda e, r0=r0, r1=r1, o_=o_: e.dma_start(out=o_[r0:r1, :], in_=h1t[:]), "dbgh1",
                          reads=["xs"], is_output=True)
            P.barrier()
            print("arena A watermark", aoff[0], "of", ARENA_F32)
            aoff[0] = a_mark

        if "B" in phases:
            wq_bf = sb("wq_bf", [128, 8, 2048], BF16)
            pg_bf = sb("pg_bf", [128, 8, 1024], BF16)
            pp_bf = sb("pp_bf", [128, 2, 1024], BF16)
            keysT = sb("keysT", [128, 16, 128], BF16)
            keys_st = sb("keys_st", [128, 16, 128])
            B32 = sb("B32", [128, 32, 128], BF16)
            rowsB = sb("rowsB", [128, 2048])
            wq_v = peer_wq.rearrange("(c p) n -> p c n", p=128)
            for c in range(8):
                P.dma("pool", lambda e, c=c: e.dma_start(out=wq_bf[:, c, :], in_=wq_v[:, c, :]), "wload%d" % (c % 4), writes=["wq_bf"])
            pg_v = ple_gate.rearrange("(c p) n -> p c n", p=128)
            for c in range(8):
                P.dma("pool", lambda e, c=c: e.dma_start(out=pg_bf[:, c, :], in_=pg_v[:, c, :]), "wload%d" % (c % 4), writes=["pg_bf"])
            pp_v = ple_proj.rearrange("(c p) n -> p c n", p=128)
            for c in range(2):
                P.dma("pool", lambda e, c=c: e.dma_start(out=pp_bf[:, c, :], in_=pp_v[:, c, :]), "wload", writes=["pp_bf"])
            P.dma("act", lambda e: e.dma_start(out=keys_st[:], in_=peer_keys.rearrange("c n d -> n c d")), "keys", writes=["keys_st"])
            P.dma("act", lambda e: e.dma_start(out=rowsB[:], in_=rows_d[:, 1024:3072]), "rowsB", writes=["rowsB"])
            for c in range(16):
                bnk = 6 + (c // 4) % 2
                P.op("pe", lambda e, c=c, bnk=bnk: e.transpose(out=bank(bnk, (c % 4) * 128, (c % 4) * 128 + 128), in_=keys_st[:, c, :],
                                                               identity=ident),
                     reads=["keys_st", "ctab"], writes=[bk(bnk)], inc=(c % 4 == 3))
                if c % 4 == 3:
                    P.op("act", lambda e, c=c, bnk=bnk: e.copy(out=keysT[:, c - 3:c + 1, :].rearrange("p c n -> p (c n)"), in_=bank(bnk)),
                         reads=[bk(bnk)], writes=["keysT"])
            P.op("pool", lambda e: e.memset(B32[:], 1.0), writes=["B32"])
            for q in range(3):
                P.op("pool", lambda e, q=q: e.affine_select(out=B32[q * 32:(q + 1) * 32], in_=B32[q * 32:(q + 1) * 32],
                                                            pattern=[[1, 32], [0, 128]], compare_op=ALU.is_equal, fill=0.0,
                                                            base=0, channel_multiplier=-1),
                     reads=["B32"], writes=["B32"])

            h1bs = [sb("h1tB%d" % i, [128, 1024]) for i in range(2)]
            pts = [sb("pt%d" % i, [128, 256]) for i in range(2)]
            xs2 = sb("xs2", [128, 1024])
            xn2T = sb("xn2T", [128, 8, 128], BF16)
            xn2b = sb("xn2b", [128, 1024], BF16)
            xhi = sb("xhi", [32, 1024], BF16)
            qTb = sb("qTb", [128, 16, 128], BF16)
            Ssb = sb("Ssb", [128, 16, 128])
            S2 = sb("S2", [128, 256])
            sv = sb("sv", [128, 16, 16])
            siu = sb("siu", [128, 16, 16], U32)
            sif = sb("sif", [128, 16, 16])
            cand = sb("cand", [128, 8, 256])
            cv = sb("cv", [128, 8, 16])
            ciu = sb("ciu", [128, 8, 16], U32)
            abu = sb("abu", [128, 2, 128], U32)
            abf = sb("abf", [128, 2, 128])
            oh = sb("oh", [128, 8, 256])
            e01 = sb("e01", [128, 2, 128])
            ef = sb("ef", [128, 128])
            ex = sb("ex", [128, 8, 16])
            zs = sb("zs", [128, 8])
            gsm = sb("gsm", [128, 128])
            eTus = [sb("eTu%d" % i, [128, 128], U32) for i in range(2)]
            gsmT = sb("gsmT", [128, 128])
            hvT = sb("hvT", [128, 128])
            gl = sb("gl", [128, 128])
            actb = sb("actb", [128, 128], BF16)
            NU, NV, NA = 4, 4, 4
            Ug = [sb("Ug%d" % i, [128, 1024]) for i in range(NU)]
            Vg = [sb("Vg%d" % i, [128, 1024], BF16) for i in range(NV)]
            Awr = [sb("Awr%d" % i, [128, 255], BF16) for i in range(NA)]
            h2t = sb("h2t", [128, 1024])
            xn3T = sb("xn3T", [128, 8, 128], BF16)
            gs = sb("gs", [128, 1024])
            pT = sb("pT", [128, 2, 128], BF16)
            yt = sb("yt", [128, 1024])
            for i in range(NA):
                P.op("pool", lambda e, i=i: e.memset(Awr[i][:], 0.0), writes=["Awr%d" % i])

            def top16(src, src_key, dst_v, dst_i, width):
                P.op("dve", lambda e: e.max(out=dst_v[:, 0:8], in_=src), reads=[src_key], writes=["tk_v"])
                P.op("dve", lambda e: e.max_index(out=dst_i[:, 0:8], in_max=dst_v[:, 0:8], in_values=src), reads=[src_key, "tk_v"], writes=["tk_i"])
                P.op("dve", lambda e: e.match_replace(out=S2[:, 0:width], in_to_replace=dst_v[:, 0:8], in_values=src, imm_value=-1e30),
                     reads=[src_key, "tk_v"], writes=["S2"])
                P.op("dve", lambda e: e.max(out=dst_v[:, 8:16], in_=S2[:, 0:width]), reads=["S2"], writes=["tk_v"])
                P.op("dve", lambda e: e.max_index(out=dst_i[:, 8:16], in_max=dst_v[:, 8:16], in_values=S2[:, 0:width]),
                     reads=["S2", "tk_v"], writes=["tk_i"])

            def stage_F(n):
                r0, r1 = n * 128, (n + 1) * 128
                h1b = h1bs[n % 2]; eTu = eTus[n % 2]; pt = pts[n % 2]
                h1k = "h1t%d" % (n % 2); eTk = "eTu%d" % (n % 2); ptk = "pt%d" % (n % 2)
                P.dma("sp", lambda e, r0=r0, r1=r1: e.dma_start(out=h1b[:], in_=h1_d[r0:r1, :]), "h1i", reads=["h1d%d" % n], writes=[h1k])
                P.dma("sp", lambda e, r0=r0, r1=r1: e.dma_start(out=pt[:], in_=p_all[r0:r1, :]), ptk, writes=[ptk])
                rms_rstd(h1b[:], h1k, junk[:], "junk", ssq[:], rstd[:], "b")
                P.op("dve", lambda e: e.tensor_scalar(out=xs2[:], in0=h1b[:], scalar1=rstd[:, 0:1], scalar2=None, op0=ALU.mult),
                     reads=[h1k, "brstd"], writes=["xs2"])
                transposes8(xs2, "xs2", xn2T, "xn2T", C_GFFN, 6, 7)
                yield
                P.op("dve", lambda e: e.tensor_tensor(out=xn2b[:], in0=xs2[:], in1=rowsB[:, 1024:2048], op=ALU.mult),
                     reads=["xs2", "rowsB"], writes=["xn2b"])
                P.dma("act", lambda e: e.dma_start(out=xhi[:], in_=xn2b[96:128, :]), "xhi", reads=["xn2b"], writes=["xhi"])
                for cg in range(4):
                    bnk = 6 + cg % 2
                    for c4 in range(4):
                        ch = cg * 4 + c4
                        for dc in range(8):
                            P.op("pe", lambda e, ch=ch, c4=c4, dc=dc, bnk=bnk: e.matmul(
                                bank(bnk, c4 * 128, c4 * 128 + 128), lhsT=wq_bf[:, dc, ch * 128:(ch + 1) * 128], rhs=xn2T[:, dc, :],
                                start=(dc == 0), stop=(dc == 7)),
                                reads=["wq_bf", "xn2T"], writes=[bk(bnk)], inc=(dc == 7 and c4 == 3))
                    P.op("act", lambda e, cg=cg, bnk=bnk: e.copy(out=qTb[:, cg * 4:cg * 4 + 4, :].rearrange("p c t -> p (c t)"), in_=bank(bnk)),
                         reads=[bk(bnk)], writes=["qTb"])
                    yield
                for cg in range(4):
                    bnk = 6 + cg % 2
                    for c4 in range(4):
                        ch = cg * 4 + c4
                        P.op("pe", lambda e, ch=ch, c4=c4, bnk=bnk: e.matmul(bank(bnk, c4 * 128, c4 * 128 + 128), lhsT=qTb[:, ch, :],
                                                                             rhs=keysT[:, ch, :], start=True, stop=True),
                             reads=["qTb", "keysT"], writes=[bk(bnk)], inc=(c4 == 3))
                    P.op("act", lambda e, cg=cg, bnk=bnk: e.copy(out=Ssb[:, cg * 4:cg * 4 + 4, :].rearrange("p c t -> p (c t)"), in_=bank(bnk)),
                         reads=[bk(bnk)], writes=["Ssb"])
                    yield
                for ch in range(16):
                    top16(Ssb[:, ch, :], "Ssb", sv[:, ch, :], siu[:, ch, :], 128)
                    if ch % 4 == 3:
                        yield
                P.op("dve", lambda e: e.tensor_copy(out=sif[:], in_=siu[:]), reads=["tk_i"], writes=["sif"])
                sv4 = sv[:].rearrange("p (h c) k -> p h c k", c=2)
                P.op("dve", lambda e: e.tensor_tensor(out=cand[:].rearrange("p h (a b) -> p h a b", a=16),
                                                      in0=sv4[:, :, 0, :].unsqueeze(3).to_broadcast([128, 8, 16, 16]),
                                                      in1=sv4[:, :, 1, :].unsqueeze(2).to_broadcast([128, 8, 16, 16]), op=ALU.add),
                     reads=["tk_v"], writes=["cand"])
                yield
                for h in range(8):
                    top16(cand[:, h, :], "cand", cv[:, h, :], ciu[:, h, :], 256)
                    if h % 2 == 1:
                        yield
                ciu2 = ciu[:].rearrange("p h k -> p (h k)")
                P.op("dve", lambda e: e.tensor_single_scalar(out=abu[:, 0, :], in_=ciu2, scalar=4, op=ALU.logical_shift_right),
                     reads=["tk_i"], writes=["abu"])
                P.op("dve", lambda e: e.tensor_single_scalar(out=abu[:, 1, :], in_=ciu2, scalar=15, op=ALU.bitwise_and),
                     reads=["tk_i"], writes=["abu"])
                P.op("dve", lambda e: e.tensor_copy(out=abf[:], in_=abu[:]), reads=["abu"], writes=["abf"])
                sif4 = sif[:].rearrange("p (h c) k -> p h c k", c=2)
                iob = ct("iota16").unsqueeze(1).unsqueeze(2).to_broadcast([128, 8, 16, 16])
                oh4 = oh[:].rearrange("p h (k a) -> p h k a", k=16)
                for c in range(2):
                    P.op("dve", lambda e, c=c: e.tensor_tensor(
                        out=oh4, in0=iob, in1=abf[:, c, :].rearrange("p (h k) -> p h k", h=8).unsqueeze(3).to_broadcast([128, 8, 16, 16]),
                        op=ALU.is_equal), reads=["abf", "ctab"], writes=["oh"])
                    P.op("dve", lambda e, c=c: e.tensor_tensor(out=oh4, in0=oh4, in1=sif4[:, :, c, :].unsqueeze(2).to_broadcast([128, 8, 16, 16]),
                                                               op=ALU.mult), reads=["oh", "sif"], writes=["oh"])
                    P.op("dve", lambda e, c=c: e.tensor_reduce(out=e01[:, c, :].rearrange("p (h k) -> p h k", h=8), in_=oh4, axis=AX.X, op=ALU.add),
                         reads=["oh"], writes=["e01"])
                    yield
                P.op("dve", lambda e: e.scalar_tensor_tensor(out=ef[:], in0=e01[:, 0, :], scalar=128.0, in1=e01[:, 1, :], op0=ALU.mult, op1=ALU.add),
                     reads=["e01"], writes=["ef"])
                P.op("dve", lambda e: e.tensor_tensor(out=ex[:], in0=cv[:], in1=cv[:, :, 0:1].to_broadcast([128, 8, 16]), op=ALU.subtract),
                     reads=["tk_v"], writes=["ex"])
                P.op("act", lambda e: e.activation(out=ex[:], in_=ex[:], func=AF.Exp), reads=["ex"], writes=["ex"])
                P.op("dve", lambda e: e.tensor_reduce(out=zs[:], in_=ex[:], axis=AX.X, op=ALU.add), reads=["ex"], writes=["zs"])
                P.op("dve", lambda e: e.reciprocal(out=zs[:], in_=zs[:]), reads=["zs"], writes=["zs"])
                P.op("dve", lambda e: e.tensor_tensor(out=gsm[:].rearrange("p (h k) -> p h k", h=8), in0=ex[:], in1=bc(zs[:], 16), op=ALU.mult),
                     reads=["ex", "zs"], writes=["gsm"])
                P.op("pe", lambda e: e.transpose(out=bank(6, 0, 128), in_=ef[:], identity=ident), reads=["ef", "ctab"], writes=[bk(6)])
                P.op("dve", lambda e: e.tensor_copy(out=eTu[:], in_=bank(6, 0, 128)), reads=[bk(6)], writes=[eTk])
                P.op("pe", lambda e: e.transpose(out=bank(7, 0, 128), in_=gsm[:], identity=ident), reads=["gsm", "ctab"], writes=[bk(7)])
                P.op("act", lambda e: e.copy(out=gsmT[:], in_=bank(7, 0, 128)), reads=[bk(7)], writes=["gsmT"])
                if "eT" in debug and n == 0:
                    dbg("eT", ef[:], [128, 128], ["ef"])
                    dbg("gsm", gsm[:], [128, 128], ["gsm"])
                yield

            def stage_U(n):
                r0, r1 = n * 128, (n + 1) * 128
                h1b = h1bs[n % 2]; eTu = eTus[n % 2]; pt = pts[n % 2]
                h1k = "h1t%d" % (n % 2); eTk = "eTu%d" % (n % 2); ptk = "pt%d" % (n % 2)
                for t in range(128):
                    su = t % NU
                    P.dma("pool", lambda e, t=t, su=su: e.indirect_dma_start(
                        out=Ug[su][:], out_offset=None, in_=peer_u, in_offset=bass.IndirectOffsetOnAxis(ap=eTu[:, t:t + 1], axis=0)),
                        "Ug%d" % su, reads=[eTk], writes=["Ug%d" % su])
                    q, rr = t // 32, t % 32
                    xb0 = (t % 2) * 2
                    for hf in range(2):
                        if q < 3:
                            P.op("pe", lambda e, q=q, rr=rr, hf=hf, xb0=xb0: e.matmul(
                                bank(xb0 + hf), lhsT=B32[q * 32:(q + 1) * 32, rr, :], rhs=xn2b[q * 32:(q + 1) * 32, hf * 512:(hf + 1) * 512],
                                start=True, stop=True),
                                reads=["B32", "xn2b"], writes=[bk(xb0 + hf)], inc=(hf == 1))
                        else:
                            P.op("pe", lambda e, rr=rr, hf=hf, xb0=xb0: e.matmul(
                                bank(xb0 + hf), lhsT=B32[0:32, rr, :], rhs=xhi[0:32, hf * 512:(hf + 1) * 512],
                                start=True, stop=True),
                                reads=["B32", "xhi"], writes=[bk(xb0 + hf)], inc=(hf == 1))
                    dummies(DUM_B, 6, wq_bf[:, 0, :], "wq_bf")
                    P.op("dve", lambda e, t=t, su=su, xb0=xb0: e.scalar_tensor_tensor(
                        out=oh[:].rearrange("p h x -> p (h x)")[:, 0:1024], in0=Ug[su][:], scalar=1.0, in1=ps[:, xb0 * 512:xb0 * 512 + 1024],
                        op0=ALU.mult, op1=ALU.mult, accum_out=hvT[:, t:t + 1]),
                        reads=["Ug%d" % su, bk(xb0), bk(xb0 + 1)], writes=["oh", "hvT"])

            def stage_G(n):
                r0, r1 = n * 128, (n + 1) * 128
                h1b = h1bs[n % 2]; eTu = eTus[n % 2]; pt = pts[n % 2]
                h1k = "h1t%d" % (n % 2); eTk = "eTu%d" % (n % 2); ptk = "pt%d" % (n % 2)
                P.op("act", lambda e: e.activation(out=gl[:], in_=hvT[:], func=AF.Gelu), reads=["hvT"], writes=["gl"])
                P.op("dve", lambda e: e.tensor_tensor(out=actb[:], in0=gl[:], in1=gsmT[:], op=ALU.mult), reads=["gl", "gsmT"], writes=["actb"])

            def stage_V(n, gen):
                r0, r1 = n * 128, (n + 1) * 128
                h1b = h1bs[n % 2]; eTu = eTus[n % 2]; pt = pts[n % 2]
                h1k = "h1t%d" % (n % 2); eTk = "eTu%d" % (n % 2); ptk = "pt%d" % (n % 2)
                for t in range(128):
                    sv_ = t % NV
                    sa = t % NA
                    P.dma("pool", lambda e, t=t, sv_=sv_: e.indirect_dma_start(
                        out=Vg[sv_][:], out_offset=None, in_=peer_v, in_offset=bass.IndirectOffsetOnAxis(ap=eTu[:, t:t + 1], axis=0)),
                        "Vg%d" % sv_, reads=[eTk], writes=["Vg%d" % sv_])
                    P.op("act", lambda e, t=t, sa=sa: e.copy(out=Awr[sa][:, 127:128], in_=actb[:, t:t + 1]),
                         reads=["actb"], writes=["Awr%d" % sa])
                    for hf in range(2):
                        P.op("pe", lambda e, t=t, sv_=sv_, sa=sa, hf=hf: e.matmul(
                            bank(4 + hf), lhsT=Awr[sa][:, 127 - t:255 - t], rhs=Vg[sv_][:, hf * 512:(hf + 1) * 512],
                            start=(t == 0), stop=(t == 127), skip_group_check=True),
                            reads=["Awr%d" % sa, "Vg%d" % sv_], writes=[bk(4 + hf)], inc=(hf == 1))
                    dummies(DUM_B, 6, wq_bf[:, 0, :], "wq_bf")
                    if gen is not None and t % 6 == 5:
                        next(gen, None)
                if gen is not None:
                    for _ in gen:
                        pass

            def stage_T(n):
                r0, r1 = n * 128, (n + 1) * 128
                h1b = h1bs[n % 2]; eTu = eTus[n % 2]; pt = pts[n % 2]
                h1k = "h1t%d" % (n % 2); eTk = "eTu%d" % (n % 2); ptk = "pt%d" % (n % 2)
                for hf in range(2):
                    P.op("dve", lambda e, hf=hf: e.tensor_tensor(out=h2t[:, hf * 512:(hf + 1) * 512], in0=bank(4 + hf),
                                                                 in1=h1b[:, hf * 512:(hf + 1) * 512], op=ALU.add),
                         reads=[bk(4 + hf), h1k], writes=["h2t"])
                if "h2" in debug:
                    if n == 0:
                        dbg_outs["h2"] = dout("dbg_h2", [TOK, D])
                    o_ = dbg_outs["h2"]
                    P.dma("sp", lambda e, r0=r0, r1=r1, o_=o_: e.dma_start(out=o_[r0:r1, :], in_=h2t[:]), "dbgh2",
                          reads=["h2t"], is_output=True)
                rms_rstd(h2t[:], "h2t", junk[:], "junk", ssq[:], rstd[:], "c")
                P.op("dve", lambda e: e.tensor_scalar(out=xs2[:], in0=h2t[:], scalar1=rstd[:, 0:1], scalar2=None, op0=ALU.mult),
                     reads=["h2t", "crstd"], writes=["xs2"])
                transposes8(xs2, "xs2", xn3T, "xn3T", C_GPLE, 6, 7)
                for c in range(2):
                    P.op("pe", lambda e, c=c: e.transpose(out=bank(6, c * 128, c * 128 + 128), in_=pt[:, c * 128:(c + 1) * 128], identity=ident),
                         reads=[ptk, "ctab"], writes=[bk(6)], inc=(c == 1))
                P.op("act", lambda e: e.copy(out=pT[:].rearrange("p c t -> p (c t)"), in_=bank(6, 0, 256)), reads=[bk(6)], writes=["pT"])
                for hf in range(2):
                    for dc in range(8):
                        P.op("pe", lambda e, hf=hf, dc=dc: e.matmul(bank(hf), lhsT=xn3T[:, dc, :], rhs=pg_bf[:, dc, hf * 512:(hf + 1) * 512],
                                                                    start=(dc == 0), stop=(dc == 7)),
                             reads=["xn3T", "pg_bf"], writes=[bk(hf)], inc=(dc == 7))
                    P.op("act", lambda e, hf=hf: e.activation(out=gs[:, hf * 512:(hf + 1) * 512], in_=bank(hf), func=AF.Sigmoid),
                         reads=[bk(hf)], writes=["gs"])
                for hf in range(2):
                    for c in range(2):
                        P.op("pe", lambda e, hf=hf, c=c: e.matmul(bank(2 + hf), lhsT=pT[:, c, :], rhs=pp_bf[:, c, hf * 512:(hf + 1) * 512],
                                                                  start=(c == 0), stop=(c == 1)),
                             reads=["pT", "pp_bf"], writes=[bk(2 + hf)], inc=(c == 1))
                    P.op("dve", lambda e, hf=hf: e.tensor_tensor(out=gs[:, hf * 512:(hf + 1) * 512], in0=bank(2 + hf),
                                                                 in1=gs[:, hf * 512:(hf + 1) * 512], op=ALU.mult),
                         reads=[bk(2 + hf), "gs"], writes=["gs"])
                P.op("dve", lambda e: e.tensor_tensor(out=h2t[:], in0=h2t[:], in1=gs[:], op=ALU.add), reads=["h2t", "gs"], writes=["h2t"])
                rms_rstd(h2t[:], "h2t", junk[:], "junk", ssq[:], rstd[:], "d")
                P.op("dve", lambda e: e.tensor_scalar(out=yt[:], in0=h2t[:], scalar1=rstd[:, 0:1], scalar2=None, op0=ALU.mult),
                     reads=["h2t", "drstd"], writes=["yt"])
                P.op("dve", lambda e: e.tensor_tensor(out=yt[:], in0=yt[:], in1=rowsB[:, 0:1024], op=ALU.mult), reads=["yt", "rowsB"], writes=["yt"])
                P.dma("sp", lambda e, r0=r0, r1=r1: e.dma_start(out=y_o[r0:r1, :], in_=yt[:]), "yo", reads=["yt"], is_output=True)

            ntb = min(NT, nt_limit)
            for _ in stage_F(0):
                pass
            for n in range(ntb):
                stage_U(n)
                stage_G(n)
                stage_V(n, stage_F(n + 1) if n + 1 < ntb else None)
                stage_T(n)

        print("arena end watermark", aoff[0], "of", ARENA_F32)
        P.finish()
        P.run_block(block)
    return nc, tabs, dbg_outs


_CACHE = {}


def make_in_maps(inp, tabs):
    f = np.float32
    g = lambda k: np.asarray(inp[k], dtype=f)

    def colsof(v, nch):
        return np.ascontiguousarray(v.reshape(nch, 128).T)

    cols = np.zeros((128, 64), f)
    cols[:, 0:8] = colsof(g("norm_mix")[0], 8)
    cols[:, 8:16] = colsof(g("norm_ffn")[0], 8)
    cols[:, 16:24] = colsof(g("ple_norm")[0], 8)
    cols[:, 24:38] = colsof(g("rw_mu")[0], 14)
    cols[:, 38:42] = colsof(g("rw_w0")[0], 4)
    cols[:, 42:46] = colsof(g("rw_a0")[0], 4)
    cols[:, 46:50] = colsof(g("rw_kk")[0], 4)
    cols[:, 50:54] = colsof(g("rw_ka")[0], 4)
    cols[:, 54:58] = colsof(g("rw_rk")[0].reshape(512), 4)
    rows = np.zeros((128, 3072), f)
    rows[:, 0:512] = g("rw_ln_g")[0][None, :]
    rows[:, 512:1024] = g("rw_ln_b")[0][None, :]
    rows[:, 1024:2048] = g("norm_final")[None, :]
    rows[:, 2048:3072] = g("norm_ffn")[0][None, :]
    ctab = np.concatenate([tabs[k] for k in CT_ORDER], axis=1).astype(f)
    w2a2 = np.concatenate([g("rw_w2")[0], g("rw_a2")[0]], axis=0)
    shared = dict(
        w_in=g("w_in")[0], w_out=g("w_out")[0], peer_wq=g("peer_wq")[0],
        peer_keys=g("peer_keys")[0].reshape(16, 128, 128), peer_u=g("peer_u")[0], peer_v=g("peer_v")[0],
        ple_gate=g("ple_gate")[0], ple_proj=g("ple_proj")[0], rw_w2a2=np.ascontiguousarray(w2a2), rw_g2=g("rw_g2")[0],
        cols=cols, rows=rows, ctab=ctab, cs=tabs["cs"],
    )
    xp, xs_ = g("x_prompt"), g("x_sample")
    pp, ps_ = g("p_prompt")[0], g("p_sample")[0]
    sr, sw, ss = g("state_ret")[0], g("state_wkv")[0], g("state_shift")[0]
    maps = []
    for c in range(NCORES):
        m = dict(shared)
        m["x_all"] = np.ascontiguousarray(np.concatenate([xp[c], xs_[16 * c:16 * c + 16].reshape(128, D)], axis=0))
        m["p_all"] = np.ascontiguousarray(np.concatenate([pp[c], ps_[16 * c:16 * c + 16].reshape(128, 256)], axis=0))
        m["s_ret"] = np.ascontiguousarray(sr[16 * c:16 * c + 16])
        m["s_wkv"] = np.ascontiguousarray(sw[16 * c:16 * c + 16])
        m["s_shift"] = np.ascontiguousarray(ss[16 * c:16 * c + 16])
        maps.append(m)
    return maps


def kernel(**inputs):
    if "prog" not in _CACHE:
        _CACHE["prog"] = build_program()
    nc, tabs, _ = _CACHE["prog"]
    maps = make_in_maps(inputs, tabs)
    res = run_bass_kernel_spmd(nc, maps, core_ids=list(range(NCORES)))
    R = res.results
    f = np.float32
    y_prompt = np.stack([R[c]["y"][0:2048] for c in range(NCORES)]).astype(f)
    y_sample = np.concatenate([R[c]["y"][2048:].reshape(16, 8, D) for c in range(NCORES)], axis=0).astype(f)
    ret_prompt = np.stack([R[c]["ret_p"] for c in range(NCORES)])[None].astype(f)
    wkv_prompt = np.stack([R[c]["wkv_p"] for c in range(NCORES)])[None].astype(f)
    shift_prompt = np.stack([R[c]["shift_p"] for c in range(NCORES)])[None].astype(f)
    ret_sample = np.concatenate([R[c]["ret_s"] for c in range(NCORES)], axis=0)[None].astype(f)
    wkv_sample = np.concatenate([R[c]["wkv_s"] for c in range(NCORES)], axis=0)[None].astype(f)
    shift_sample = np.concatenate([R[c]["shift_s"] for c in range(NCORES)], axis=0)[None].astype(f)
    return (y_prompt, y_sample, ret_prompt, wkv_prompt, shift_prompt, ret_sample, wkv_sample, shift_sample)
```

```python
import numpy as np
from contextlib import ExitStack
import concourse.bass as bass
import concourse.mybir as mybir
from concourse.bass_utils import run_bass_kernel_spmd

F32 = mybir.dt.float32
BF16 = mybir.dt.bfloat16
U32 = mybir.dt.uint32
AF = mybir.ActivationFunctionType
ALU = mybir.AluOpType
AX = mybir.AxisListType

NCORES = 8
NT = 17
TOK = NT * 128
D = 1024
RMS_EPS = 1e-6
GN_EPS = 1e-5
RWKV_LN_EPS = 64e-5
SAME_ENGINE_SYNC = True
DUM_A = 0
DUM_B = 0


class Prog:
    ENG = ("pe", "dve", "act", "pool", "sp")

    def __init__(self, nc, n_dma_sems=64):
        self.nc = nc
        self.lists = {k: [] for k in self.ENG}
        self.cnt = {k: 0 for k in self.ENG}
        self.waited = {k: {} for k in self.ENG}
        self.bufs = {}
        self.pending = {k: ([], []) for k in self.ENG}
        self.sems = {}
        self.n_dma_sems = n_dma_sems
        self.chan_sem = {}
        self.chan_cnt = {}
        self.chan_last = {}
        self.free_dma = list(range(n_dma_sems))
        self.out_events = []
        self.dead = False

    def alloc_sems(self, stack):
        for k in self.ENG:
            self.sems["prog_" + k] = stack.enter_context(self.nc.semaphore("prog_" + k))
        for i in range(self.n_dma_sems):
            self.sems["dma%d" % i] = stack.enter_context(self.nc.semaphore("dma%d" % i))

    def _chan(self, chan):
        if chan not in self.chan_sem:
            if not self.free_dma:
                raise RuntimeError("out of DMA semaphores")
            self.chan_sem[chan] = "dma%d" % self.free_dma.pop(0)
            self.chan_cnt[chan] = 0
        return self.chan_sem[chan]

    def _b(self, key):
        if key not in self.bufs:
            self.bufs[key] = {"w": None, "r": []}
        return self.bufs[key]

    def _deps(self, reads, writes):
        deps = []
        for k in reads:
            b = self._b(k)
            if b["w"] is not None:
                deps.append(b["w"])
        for k in writes:
            b = self._b(k)
            if b["w"] is not None:
                deps.append(b["w"])
            deps.extend(b["r"])
        return deps

    def _emit_waits(self, e, deps, ss=True):
        own = "prog_" + e
        for (s, v) in deps:
            if s == own and (e == "pe" or not SAME_ENGINE_SYNC):
                continue
            if self.waited[e].get(s, 0) < v:
                self.waited[e][s] = v
                self.lists[e].append(("wait", s, v))

    def _commit(self, ev, reads, writes):
        for k in reads:
            self._b(k)["r"].append(ev)
        for k in writes:
            self.bufs[k] = {"w": ev, "r": []}

    def op(self, e, fn, reads=(), writes=(), inc=True, ss=True):
        if self.dead:
            return None
        reads = list(reads); writes = list(writes)
        self._emit_waits(e, self._deps(reads, writes), ss)
        if not inc:
            self.pending[e][0].extend(reads)
            self.pending[e][1].extend(writes)
            self.lists[e].append(("op", fn, None))
            return None
        self.cnt[e] += 1
        ev = ("prog_" + e, self.cnt[e])
        self.lists[e].append(("op", fn, "prog_" + e))
        pr, pw = self.pending[e]
        self._commit(ev, reads + pr, writes + pw)
        self.pending[e] = ([], [])
        return ev

    def dma(self, q, fn, chan, reads=(), writes=(), is_output=False):
        if self.dead:
            return None
        reads = list(reads); writes = list(writes)
        sem = self._chan(chan)
        deps = self._deps(reads, writes)
        if chan in self.chan_last:
            deps.append(self.chan_last[chan])
        self._emit_waits(q, deps)
        self.chan_cnt[chan] += 16
        ev = (sem, self.chan_cnt[chan])
        self.chan_last[chan] = ev
        self.lists[q].append(("dma", fn, sem))
        self._commit(ev, reads, writes)
        if is_output:
            self.out_events.append(ev)
        return ev

    def barrier(self):
        deps = [ev for ev in self.chan_last.values()]
        for e in self.ENG:
            if self.cnt[e] > 0:
                deps.append(("prog_" + e, self.cnt[e]))
        for e in self.ENG:
            self._emit_waits(e, deps)

    def finish(self):
        deps = list(self.out_events) + [ev for ev in self.chan_last.values()]
        for e in self.ENG:
            if e != "sp" and self.cnt[e] > 0:
                deps.append(("prog_" + e, self.cnt[e]))
        self._emit_waits("sp", deps)

    def replay(self, e, engine):
        for item in self.lists[e]:
            if item[0] == "wait":
                engine.wait_ge(self.sems[item[1]], item[2])
            elif item[0] == "op":
                ins = item[1](engine)
                if item[2] is not None:
                    ins.then_inc(self.sems[item[2]], 1)
            else:
                ins = item[1](engine)
                ins.then_inc(self.sems[item[2]], 16)

    def run_block(self, block):
        p = self

        @block.sync
        def _(eng):
            p.replay("sp", eng)

        @block.tensor
        def _(eng):
            p.replay("pe", eng)

        @block.vector
        def _(eng):
            p.replay("dve", eng)

        @block.scalar
        def _(eng):
            p.replay("act", eng)

        @block.gpsimd
        def _(eng):
            p.replay("pool", eng)


def _const_tables():
    f = np.float32
    H = 8
    lg = np.log(f(1.0) - f(2.0) ** (-5.0 - np.arange(H, dtype=f))).astype(f)
    t = {}

    def decay(e):
        return np.exp(e[None].astype(f) * lg.reshape((H,) + (1,) * e.ndim)).astype(f)

    idx = np.arange(128, dtype=f)
    diff = idx[None, :] - idx[:, None]
    dp = np.where(diff >= 0, decay(np.maximum(diff, 0)), 0).astype(f)
    t["DT_p"] = np.ascontiguousarray(dp.transpose(1, 0, 2)).reshape(128, H * 128)
    b = np.arange(128) // 8
    l = (np.arange(128) % 8).astype(f)
    same = (b[:, None] == b[None, :])
    dl = l[None, :] - l[:, None]
    ds = np.where(same[None] & (dl[None] >= 0), decay(np.maximum(dl, 0)), 0).astype(f)
    t["DT_s"] = np.ascontiguousarray(ds.transpose(1, 0, 2)).reshape(128, H * 128)
    def xi_tab(e):
        x = decay(e)
        o = np.zeros((128, 4, 128), f)
        for g in range(4):
            for hh in range(2):
                o[hh * 64:(hh + 1) * 64, g, :] = x[2 * g + hh][None, :]
        return o.reshape(128, 512)
    t["XI_p"] = xi_tab(idx + 1.0)
    t["XI_s"] = xi_tab(l + 1.0)
    t["Z_p"] = (decay(127.0 - idx).T * f(0.125)).astype(f)
    t["Z_s"] = (decay(7.0 - l).T * f(0.125)).astype(f)
    def gc_tab(C):
        g_ = np.exp(f(C) * lg).astype(f)
        o = np.zeros((128, 4), f)
        for g in range(4):
            for hh in range(2):
                o[hh * 64:(hh + 1) * 64, g] = g_[2 * g + hh]
        return o
    t["GC_p"] = gc_tab(128.0)
    t["GC_s"] = gc_tab(8.0)
    t["Mk2"] = (b[:, None] == np.arange(16)[None, :]).astype(f)
    bm = np.zeros((128, 128), f)
    bm[:64, :64] = 1; bm[64:, 64:] = 1
    t["blockmask"] = bm
    hm = np.zeros((128, 2), f)
    hm[:64, 0] = 1; hm[64:, 1] = 1
    t["halfmask2"] = hm
    aw = np.zeros((128, 2, 255), f)
    aw[:64, 0, 127] = 1; aw[64:, 1, 127] = 1
    t["Awin"] = aw.reshape(128, 510)
    i2 = np.zeros((128, 64), f)
    i2[np.arange(128), np.arange(128) % 64] = 1
    t["I2"] = i2
    t["ident"] = np.eye(128, dtype=f)
    t["iota16"] = np.tile(np.arange(16, dtype=f)[None, :], (128, 1))
    half = 32
    inv = (f(10000.0) ** (-np.arange(half, dtype=f) / f(half))).astype(f)
    cs = np.zeros((NT, 128, 64), f)
    for n in range(NT):
        if n < 16:
            pos = (n * 128 + np.arange(128)).astype(f)
        else:
            pos = (16384 + (np.arange(128) % 8)).astype(f)
        ang = (pos[:, None] * inv[None, :]).astype(f)
        cs[n, :, :32] = np.cos(ang.astype(np.float64)).astype(f)
        cs[n, :, 32:] = np.sin(ang.astype(np.float64)).astype(f)
    t["cs"] = cs
    return t


CT_ORDER = ["DT_p", "DT_s", "XI_p", "XI_s", "Z_p", "Z_s", "GC_p", "GC_s", "Mk2", "blockmask",
            "halfmask2", "Awin", "I2", "ident", "iota16"]


def build_program(phases=("A", "B"), debug=(), nt_limit=NT):
    nc = bass.Bass("TRN2", target_bir_lowering=False)
    P = Prog(nc)
    tabs = _const_tables()
    ct_off = {}
    off = 0
    for k in CT_ORDER:
        ct_off[k] = (off, tabs[k].shape[1])
        off += tabs[k].shape[1]
    NCT = off

    def din(name, shape, dt=F32):
        return nc.dram_tensor(name, list(shape), dt, kind="ExternalInput").ap()

    def dout(name, shape, dt=F32):
        return nc.dram_tensor(name, list(shape), dt, kind="ExternalOutput").ap()

    x_all = din("x_all", [TOK, D])
    p_all = din("p_all", [TOK, 256])
    s_ret = din("s_ret", [16, 8, 64, 64])
    s_wkv = din("s_wkv", [16, 8, 64, 64])
    s_shift = din("s_shift", [16, 1792])
    w_in = din("w_in", [D, 3840])
    w_out = din("w_out", [D, D])
    peer_wq = din("peer_wq", [D, 2048])
    peer_keys = din("peer_keys", [16, 128, 128])
    peer_u = din("peer_u", [16384, D])
    peer_v = din("peer_v", [16384, D])
    ple_gate = din("ple_gate", [D, D])
    ple_proj = din("ple_proj", [256, D])
    rw_w2a2 = din("rw_w2a2", [128, 512])
    rw_g2 = din("rw_g2", [128, 512])
    cols_d = din("cols", [128, 64])
    rows_d = din("rows", [128, 3072])
    ctab_d = din("ctab", [128, NCT])
    cs_d = din("cs", [NT, 128, 64])

    y_o = dout("y", [TOK, D])
    retp_o = dout("ret_p", [8, 64, 64])
    wkvp_o = dout("wkv_p", [8, 64, 64])
    shiftp_o = dout("shift_p", [1792])
    rets_o = dout("ret_s", [16, 8, 64, 64])
    wkvs_o = dout("wkv_s", [16, 8, 64, 64])
    shifts_o = dout("shift_s", [16, 1792])
    h1_d = dout("h1_scr", [TOK, D])
    dbg_outs = {}

    with ExitStack() as st:
        ARENA_F32 = 53200
        arena = st.enter_context(nc.sbuf_tensor("arena", [128, ARENA_F32], F32))
        aoff = [0]

        def sb(name, shape, dt=F32):
            nel = 1
            for d_ in shape[1:]:
                nel *= d_
            esz = 4 if dt in (F32, U32) else 2
            n4 = ((nel * esz + 31) // 32) * 8
            o = aoff[0]
            aoff[0] += n4
            assert aoff[0] <= ARENA_F32, ("arena overflow", name, aoff[0])
            v = arena[0:shape[0], o:o + n4]
            if dt != F32:
                v = v.bitcast(dt)
            v = v[:, 0:nel]
            if len(shape) == 3:
                v = v.rearrange("p (a b) -> p a b", a=shape[1])
            elif len(shape) == 4:
                v = v.rearrange("p (a b c) -> p a b c", a=shape[1], b=shape[2])
            elif len(shape) == 5:
                v = v.rearrange("p (a b c d) -> p a b c d", a=shape[1], b=shape[2], c=shape[3])
            return v

        ps = st.enter_context(nc.psum_tensor("ps", [128, 4096], F32))

        def bank(b, lo=0, hi=512):
            return ps[:, b * 512 + lo:b * 512 + hi]

        def bk(b):
            return "ps%d" % b

        NDUM = [0]

        def dummies(k, bnk, src, src_key):
            for _ in range(k):
                P.op("pe", lambda e: e.matmul(bank(bnk), lhsT=src[:, 0:128], rhs=src[:, 0:512], start=True, stop=True),
                     reads=[src_key], writes=[bk(bnk)], inc=False)

        cols = sb("cols", [128, 64])
        ctab = sb("ctab", [128, NCT])
        C_GMIX, C_GFFN, C_GPLE, C_MU, C_W0, C_A0, C_KK, C_KA, C_RK, C_OMKA = 0, 8, 16, 24, 38, 42, 46, 50, 54, 58

        def ct(name):
            o, w = ct_off[name]
            return ctab[:, o:o + w]

        ident = ct("ident")
        P.alloc_sems(st)
        block = st.enter_context(nc.Block())

        def ckpt(name):
            if ("cut_" + name) in debug:
                P.dead = True

        def dbg(name, ap, shape, reads):
            if name not in debug:
                return
            o = dout("dbg_" + name, shape)
            dbg_outs[name] = o
            P.dma("sp", lambda e: e.dma_start(out=o, in_=ap), "dbg_" + name, reads=reads, is_output=True)

        P.dma("sp", lambda e: e.dma_start(out=cols[:], in_=cols_d), "cols", writes=["cols"])
        P.dma("sp", lambda e: e.dma_start(out=ctab[:], in_=ctab_d), "ctab", writes=["ctab"])
        P.op("dve", lambda e: e.tensor_scalar(out=cols[:, C_OMKA:C_OMKA + 4], in0=cols[:, C_KA:C_KA + 4],
                                              scalar1=-1.0, scalar2=1.0, op0=ALU.mult, op1=ALU.add),
             reads=["cols"], writes=["cols"])

        def bc(ap2, n, axis_last=True):
            k = ap2.shape[1]
            return ap2.unsqueeze(2).to_broadcast([ap2.shape[0], k, n])

        def rms_rstd(src, src_key, junk, junk_key, ssq, rstd, tag):
            P.op("act", lambda e: e.activation(out=junk, in_=src, func=AF.Square, accum_out=ssq),
                 reads=[src_key], writes=[junk_key, tag + "ssq"])
            P.op("act", lambda e: e.activation(out=rstd, in_=ssq, func=AF.Sqrt, scale=1.0 / D, bias=eps_t[:, 0:1]),
                 reads=[tag + "ssq", "eps"], writes=[tag + "rstd"])
            P.op("dve", lambda e: e.reciprocal(out=rstd, in_=rstd), reads=[tag + "rstd"], writes=[tag + "rstd"])

        def transposes8(src, src_key, dstT, dst_key, gcol_off, b0, b1):
            for hb, bnk in ((0, b0), (1, b1)):
                for c4 in range(4):
                    c = hb * 4 + c4
                    P.op("pe", lambda e, c=c, c4=c4, bnk=bnk: e.transpose(out=bank(bnk, c4 * 128, c4 * 128 + 128),
                                                                         in_=src[:, c * 128:(c + 1) * 128], identity=ident),
                         reads=[src_key, "ctab"], writes=[bk(bnk)], inc=(c4 == 3))
                if gcol_off is None:
                    P.op("act", lambda e, hb=hb, bnk=bnk: e.copy(out=dstT[:, hb * 4:hb * 4 + 4, :],
                                                                 in_=bank(bnk).rearrange("p (c t) -> p c t", c=4)),
                         reads=[bk(bnk)], writes=[dst_key])
                else:
                    P.op("dve", lambda e, hb=hb, bnk=bnk: e.tensor_tensor(
                        out=dstT[:, hb * 4:hb * 4 + 4, :], in0=bank(bnk).rearrange("p (c t) -> p c t", c=4),
                        in1=bc(cols[:, gcol_off + hb * 4:gcol_off + hb * 4 + 4], 128), op=ALU.mult),
                        reads=[bk(bnk), "cols"], writes=[dst_key])

        def head_norm(src3, src_key, xc, xc_key, eps_ap, tag):
            P.op("dve", lambda e: e.tensor_reduce(out=hn_m[:], in_=src3, axis=AX.X, op=ALU.add),
                 reads=[src_key], writes=["hn_m"])
            P.op("dve", lambda e: e.tensor_scalar(out=hn_m[:], in0=hn_m[:], scalar1=-1.0 / 64, scalar2=None, op0=ALU.mult),
                 reads=["hn_m"], writes=["hn_m"])
            P.op("dve", lambda e: e.tensor_tensor(out=xc, in0=src3, in1=bc(hn_m[:], 64), op=ALU.add),
                 reads=[src_key, "hn_m"], writes=[xc_key])
            P.op("act", lambda e: e.activation(out=hn_sq[:], in_=xc.rearrange("p h d -> p (h d)"), func=AF.Square),
                 reads=[xc_key], writes=["hn_sq"])
            P.op("dve", lambda e: e.tensor_reduce(out=hn_v[:], in_=hn_sq[:].rearrange("p (h d) -> p h d", h=8), axis=AX.X, op=ALU.add),
                 reads=["hn_sq"], writes=["hn_v"])
            P.op("act", lambda e: e.activation(out=hn_v[:], in_=hn_v[:], func=AF.Sqrt, scale=1.0 / 64, bias=eps_ap),
                 reads=["hn_v", "eps"], writes=["hn_v"])
            P.op("dve", lambda e: e.reciprocal(out=hn_v[:], in_=hn_v[:]), reads=["hn_v"], writes=["hn_v"])
            P.op("dve", lambda e: e.tensor_tensor(out=xc, in0=xc, in1=bc(hn_v[:], 64), op=ALU.mult),
                 reads=[xc_key, "hn_v"], writes=[xc_key])

        eps_t = sb("eps_t", [128, 4])
        P.op("pool", lambda e: e.memset(eps_t[:, 0:1], RMS_EPS), writes=["eps"], inc=False)
        P.op("pool", lambda e: e.memset(eps_t[:, 1:2], GN_EPS), writes=["eps"], inc=False)
        P.op("pool", lambda e: e.memset(eps_t[:, 2:3], RWKV_LN_EPS), writes=["eps"], inc=False)
        P.op("pool", lambda e: e.memset(eps_t[:, 3:4], 0.0), writes=["eps"])
        hn_m = sb("hn_m", [128, 8])
        hn_v = sb("hn_v", [128, 8])
        hn_sq = sb("hn_sq", [128, 512])
        ssq = sb("ssq", [128, 1])
        rstd = sb("rstd", [128, 1])
        junk = sb("junk", [128, 1024])

        if "A" in phases:
            a_mark = aoff[0]
            sa_ = sb

            win_bf = sa_("win_bf", [128, 8, 3840], BF16)
            wout_bf = sa_("wout_bf", [128, 8, 1024], BF16)
            w2a2 = sa_("w2a2", [128, 512])
            g2 = sa_("g2", [128, 512])
            rowsA = sa_("rowsA", [128, 1024])
            w_in_v = w_in.rearrange("(c p) n -> p c n", p=128)
            for c in range(8):
                for hf in range(2):
                    P.dma("pool", lambda e, c=c, hf=hf: e.dma_start(out=win_bf[:, c, hf * 1920:(hf + 1) * 1920],
                                                                    in_=w_in_v[:, c, hf * 1920:(hf + 1) * 1920]),
                          "wload%d" % (c % 4), writes=["win_bf"])
            w_out_v = w_out.rearrange("(c p) n -> p c n", p=128)
            for c in range(8):
                P.dma("pool", lambda e, c=c: e.dma_start(out=wout_bf[:, c, :], in_=w_out_v[:, c, :]),
                      "wload%d" % (c % 4), writes=["wout_bf"])
            P.dma("act", lambda e: e.dma_start(out=w2a2[:], in_=rw_w2a2), "w2a2", writes=["w2a2"])
            P.dma("act", lambda e: e.dma_start(out=g2[:], in_=rw_g2), "g2", writes=["g2"])
            P.dma("act", lambda e: e.dma_start(out=rowsA[:], in_=rows_d[:, 0:1024]), "rowsA", writes=["rowsA"])

            xt = sa_("xt", [128, 1024])
            xs = sa_("xs", [128, 1024])
            xnT = sa_("xnT", [128, 8, 128], BF16)
            cst = sa_("cst", [128, 64])
            qkr = sa_("qkr", [128, 16, 64])
            rt1 = sa_("rt1", [128, 16, 32])
            rt2 = sa_("rt2", [128, 16, 32])
            vtb = sa_("vtb", [128, 512], BF16)
            sgl = sa_("sgl", [128, 512])
            kz = sa_("kz", [128, 8, 64], BF16)
            qbd = sa_("qbd", [128, 4, 2, 128], BF16)
            kT = sa_("kT", [128, 4, 128], BF16)
            qxT = sa_("qxT", [128, 4, 128], BF16)
            PT = sa_("PT", [128, 8, 128], BF16)
            Sst = sa_("Sst", [128, 4, 64])
            Sbd = sa_("Sbd", [128, 4, 2, 64], BF16)
            S0b = sa_("S0b", [128, 2, 4, 64])
            S0bd = sa_("S0bd", [128, 2, 4, 2, 64], BF16)
            kzm = sa_("kzm", [128, 2, 512], BF16)
            xc = sa_("xc", [128, 8, 64])
            oall = sa_("oall", [128, 1024])
            rw = sa_("rw", [128, 14, 144])
            fm = sa_("fm", [128, 14, 128])
            th = sa_("th", [128, 128])
            sgm = sa_("sgm", [128, 4, 128])
            ew = sa_("ew", [128, 4, 128])
            aa = sa_("aa", [128, 4, 128])
            sigfg = sa_("sigfg", [128, 128])
            gate = sa_("gate", [128, 512])
            kk = sa_("kk", [128, 4, 128])
            nrm = sa_("nrm", [128, 4, 128])
            kp = sa_("kp", [128, 4, 128])
            nkka = sa_("nkka", [128, 4, 128])
            prk = sa_("prk", [128, 4, 128])
            sqk = prk
            bonus = sa_("bonus", [128, 8])
            vtok = sa_("vtok", [128, 512])
            ST = sa_("ST", [128, 256])
            NR = 2
            RH = [sa_("RH%d" % i, [128, 256], BF16) for i in range(NR)]
            T2 = [sa_("T2_%d" % i, [128, 256], BF16) for i in range(NR)]
            DV = [sa_("DV%d" % i, [128, 2, 4, 256], BF16) for i in range(2)]
            Qr = [sa_("Qr%d" % i, [128, 256]) for i in range(NR)]
            Pr = [sa_("Pr%d" % i, [128, 256]) for i in range(NR)]
            Mm = sa_("Mm", [128, 256])
            vhi = sa_("vhi", [128, 4, 128], BF16)
            vlo = sa_("vlo", [128, 4, 128], BF16)
            I2h = sa_("I2h", [128, 64], BF16)
            bmh = sa_("bmh", [128, 128], BF16)
            Awh = sa_("Awh", [128, 2, 255], BF16)
            Xw = sa_("Xw", [64, 8, 64])
            shst = Xw[0:16].rearrange("p h j -> p (h j)")
            xc2 = xc
            oT = sa_("oT", [128, 8, 128], BF16)
            h1t = xs

            P.op("pool", lambda e: e.memset(Sst[:], 0.0), writes=["Sst"])
            P.op("pool", lambda e: e.memset(Sbd[:], 0.0), writes=["Sbd"])
            P.op("pool", lambda e: e.memset(qbd[:], 0.0), writes=["qbd"])
            P.op("pool", lambda e: e.memset(S0bd[:], 0.0), writes=["S0bd0", "S0bd1"])
            P.op("pool", lambda e: e.memset(ST[:], 0.0), writes=["ST0", "ST1", "ST2", "ST3"])
            P.op("pool", lambda e: e.memset(rw[:], 0.0), writes=["rw"])

            ST3 = ST[:].rearrange("p (g i) -> p g i", g=4)
            P.op("act", lambda e: e.copy(out=I2h[:], in_=ct("I2")), reads=["ctab"], writes=["I2h"])
            P.op("act", lambda e: e.copy(out=bmh[:], in_=ct("blockmask")), reads=["ctab"], writes=["bmh"])
            P.op("act", lambda e: e.copy(out=Awh[:].rearrange("p h w -> p (h w)"), in_=ct("Awin")), reads=["ctab"], writes=["Awh"])
            I2hb = I2h[:].unsqueeze(1).to_broadcast([128, 4, 64])
            blockmask = ct("blockmask")
            Aw = ct("Awin").rearrange("p (h w) -> p h w", h=2)

            def colb(t_, tl):
                return t_[:, :, tl:tl + 1].to_broadcast([128, 4, 64])

            def wkv_state_in(b):
                P.dma("sp", lambda e: e.dma_start(out=Xw[:], in_=s_wkv[b].rearrange("h i j -> i h j")), "Xw",
                      writes=["Xw"])
                for g in range(4):
                    P.op("pe", lambda e, g=g: e.transpose(out=bank(7, g * 64, g * 64 + 64),
                                                          in_=Xw[:, 2 * g:2 * g + 2, :].rearrange("i h j -> i (h j)"),
                                                          identity=ident[0:64, 0:64]),
                         reads=["Xw", "ctab"], writes=[bk(7)], inc=(g == 3))
                P.op("act", lambda e: e.copy(out=ST[:], in_=bank(7, 0, 256)), reads=[bk(7)], writes=["ST0", "ST1", "ST2", "ST3"])

            def wkv_state_out(dst):
                for g in range(4):
                    P.op("pe", lambda e, g=g: e.transpose(out=ps[0:64, 7 * 512 + g * 128:7 * 512 + g * 128 + 128],
                                                          in_=ST[:, g * 64:(g + 1) * 64], identity=ident),
                         reads=["ST0", "ST1", "ST2", "ST3", "ctab"], writes=[bk(7)], inc=(g == 3))
                P.op("act", lambda e: e.copy(out=Xw[:].rearrange("i h j -> i (h j)"), in_=ps[0:64, 7 * 512:7 * 512 + 512]),
                     reads=[bk(7)], writes=["Xw"])
                P.dma("sp", lambda e: e.dma_start(out=dst.rearrange("h i j -> i h j"), in_=Xw[:]), "Xwo",
                      reads=["Xw"], is_output=True)

            def scan_pre(t, r):
                vb = 2 + (t % 2)
                sl = t % 2
                for g in range(4):
                    P.op("act", lambda e, g=g: e.activation(out=DV[sl][:, 0, 0, g * 64:(g + 1) * 64], in_=I2h[:],
                                                            func=AF.Copy, scale=fm[:, 8 + g, t:t + 1]),
                         reads=["fm", "I2h"], writes=["DV%d" % sl], inc=(g == 3))
                P.op("pe", lambda e: e.matmul(bank(vb, 0, 256), lhsT=bmh[:], rhs=DV[sl][:, 0, 0, :], start=True, stop=True),
                     reads=["DV%d" % sl, "bmh"], writes=[bk(vb)])

            def scan_y(t, r):
                t2 = T2[r]
                P.op("dve", lambda e: e.tensor_tensor(out=t2[:].rearrange("p (g i) -> p g i", g=4), in0=ST3,
                                                      in1=colb(fm[:, 0:4, :], t), op=ALU.mult),
                     reads=["ST0", "ST1", "ST2", "ST3", "fm"], writes=["T2_%d" % r])
                for hh in range(2):
                    P.op("pe", lambda e, hh=hh: e.matmul(
                        bank(6, hh * 256, hh * 256 + 256),
                        lhsT=Awh[:, hh, 127 - t:255 - t], rhs=t2[:], start=(t == 0 and hh == 0), stop=(t == 127 and hh == 1),
                        skip_group_check=True),
                        reads=["T2_%d" % r, "Awh"], writes=[bk(6)], inc=(hh == 1))

            def scan_step(t, r, prev_t):
                rh = RH[r]
                pb = 4 + (t % 2)
                scan_pre(t, r)
                P.op("dve", lambda e: e.tensor_tensor(out=rh[:].rearrange("p (g i) -> p g i", g=4), in0=ST3,
                                                      in1=colb(kk, t), op=ALU.mult),
                     reads=["ST0", "ST1", "ST2", "ST3", "kk"], writes=["RH%d" % r])
                if prev_t is not None:
                    scan_y(prev_t, prev_t % NR)
                P.op("pe", lambda e: e.matmul(bank(pb, 0, 256), lhsT=bmh[:], rhs=rh[:], start=True, stop=True),
                     reads=["RH%d" % r, "bmh"], writes=[bk(pb)])
                dummies(DUM_A, 7, wout_bf[:, 0, :], "wout_bf")
                vb = 2 + (t % 2)
                P.op("pool", lambda e: e.tensor_tensor(out=Pr[r][:].rearrange("p (g i) -> p g i", g=4), in0=ST3,
                                                       in1=colb(ew, t), op=ALU.mult),
                     reads=["ST0", "ST1", "ST2", "ST3", "ew"], writes=["P%d" % r])
                for g in range(4):
                    gs_ = slice(g * 64, (g + 1) * 64)
                    P.op("dve", lambda e, g=g, gs_=gs_: e.scalar_tensor_tensor(
                        out=Qr[r][:, gs_], in0=bank(vb, g * 64, g * 64 + 64), scalar=kp[:, g, t:t + 1], in1=Pr[r][:, gs_],
                        op0=ALU.mult, op1=ALU.add),
                        reads=[bk(vb), "kp", "P%d" % r], writes=["Q%d_%d" % (r, g)])
                for g in range(4):
                    gs_ = slice(g * 64, (g + 1) * 64)
                    P.op("dve", lambda e, g=g, gs_=gs_: e.scalar_tensor_tensor(
                        out=ST[:, gs_], in0=bank(pb, g * 64, g * 64 + 64), scalar=nkka[:, g, t:t + 1], in1=Qr[r][:, gs_],
                        op0=ALU.mult, op1=ALU.add),
                        reads=[bk(pb), "nkka", "Q%d_%d" % (r, g)], writes=["ST%d" % g])

            for n in range(min(NT, nt_limit)):
                smp = (n == 16)
                sfx = "_s" if smp else "_p"
                NB, L = (16, 8) if smp else (1, 128)
                r0, r1 = n * 128, (n + 1) * 128
                P.dma("sp", lambda e, r0=r0, r1=r1: e.dma_start(out=xt[:], in_=x_all[r0:r1, :]), "xt", writes=["xt"])
                P.dma("sp", lambda e, n=n: e.dma_start(out=cst[:], in_=cs_d[n]), "cst", writes=["cst"])
                rms_rstd(xt[:], "xt", junk[:], "junk", ssq[:], rstd[:], "a")
                P.op("dve", lambda e: e.tensor_scalar(out=xs[:], in0=xt[:], scalar1=rstd[:, 0:1], scalar2=None, op0=ALU.mult),
                     reads=["xt", "arstd"], writes=["xs"])
                transposes8(xs, "xs", xnT, "xnT", C_GMIX, 6, 7)
                ckpt("a1")
                for blk in range(4):
                    for c in range(8):
                        P.op("pe", lambda e, blk=blk, c=c: e.matmul(bank(blk), lhsT=xnT[:, c, :],
                                                                    rhs=win_bf[:, c, blk * 512:(blk + 1) * 512],
                                                                    start=(c == 0), stop=(c == 7)),
                             reads=["xnT", "win_bf"], writes=[bk(blk)], inc=(c == 7))
                if n > 0 and not smp:
                    P.op("pool", lambda e: e.tensor_copy(out=rw[:, :, 0:1], in_=rw[:, :, 128:129]), reads=["rw"], writes=["rw"])
                if smp:
                    ssh = fm[0:16, :, :].rearrange("p c t -> p (c t)")
                    P.dma("sp", lambda e: e.dma_start(out=ssh, in_=s_shift), "ssh", writes=["fm"])
                    for c in range(14):
                        P.op("pe", lambda e, c=c: e.transpose(out=bank(6, c * 16, c * 16 + 16), in_=ssh[:, c * 128:(c + 1) * 128],
                                                              identity=ident[0:16, 0:16]),
                             reads=["fm", "ctab"], writes=[bk(6)], inc=(c == 13))
                    P.op("act", lambda e: e.copy(out=rw[:].rearrange("p c (b l) -> p c b l", l=9)[:, :, :, 0:1],
                                                 in_=bank(6, 0, 224).rearrange("p (c b o) -> p c b o", c=14, o=1)),
                         reads=[bk(6)], writes=["rw"])
                for cg in range(4):
                    bnk = 4 + cg % 2
                    ncs = 4 if cg < 3 else 2
                    for c4 in range(ncs):
                        c = cg * 4 + c4
                        for dc in range(8):
                            P.op("pe", lambda e, c=c, c4=c4, dc=dc, bnk=bnk: e.matmul(
                                bank(bnk, c4 * 128, c4 * 128 + 128), lhsT=win_bf[:, dc, 2048 + c * 128:2048 + (c + 1) * 128],
                                rhs=xnT[:, dc, :], start=(dc == 0), stop=(dc == 7)),
                                reads=["xnT", "win_bf"], writes=[bk(bnk)], inc=(dc == 7 and c4 == ncs - 1))
                    if smp:
                        P.op("act", lambda e, cg=cg, ncs=ncs, bnk=bnk: e.copy(
                            out=rw[:, cg * 4:cg * 4 + ncs, :].rearrange("p c (b l) -> p c b l", l=9)[:, :, :, 1:9],
                            in_=bank(bnk, 0, ncs * 128).rearrange("p (c b l) -> p c b l", c=ncs, l=8)),
                            reads=[bk(bnk)], writes=["rw"])
                    else:
                        P.op("act", lambda e, cg=cg, ncs=ncs, bnk=bnk: e.copy(
                            out=rw[:, cg * 4:cg * 4 + ncs, 1:129],
                            in_=bank(bnk, 0, ncs * 128).rearrange("p (c t) -> p c t", c=ncs)),
                            reads=[bk(bnk)], writes=["rw"])
                ckpt("a2")
                qk3 = ps[:, 0:1024].rearrange("p (h d) -> p h d", d=64)
                cosb = cst[:, 0:32].unsqueeze(1).to_broadcast([128, 16, 32])
                sinb = cst[:, 32:64].unsqueeze(1).to_broadcast([128, 16, 32])
                P.op("dve", lambda e: e.tensor_tensor(out=rt1[:], in0=qk3[:, :, 0:32], in1=cosb, op=ALU.mult),
                     reads=[bk(0), bk(1), "cst"], writes=["rt1"])
                P.op("dve", lambda e: e.tensor_tensor(out=rt2[:], in0=qk3[:, :, 32:64], in1=sinb, op=ALU.mult),
                     reads=[bk(0), bk(1), "cst"], writes=["rt2"])
                P.op("pool", lambda e: e.tensor_tensor(out=qkr[:, :, 0:32], in0=rt1[:], in1=rt2[:], op=ALU.subtract),
                     reads=["rt1", "rt2"], writes=["qkr"])
                P.op("dve", lambda e: e.tensor_tensor(out=rt1[:], in0=qk3[:, :, 32:64], in1=cosb, op=ALU.mult),
                     reads=[bk(0), bk(1), "cst"], writes=["rt1"])
                P.op("dve", lambda e: e.tensor_tensor(out=rt2[:], in0=qk3[:, :, 0:32], in1=sinb, op=ALU.mult),
                     reads=[bk(0), bk(1), "cst"], writes=["rt2"])
                P.op("pool", lambda e: e.tensor_tensor(out=qkr[:, :, 32:64], in0=rt1[:], in1=rt2[:], op=ALU.add),
                     reads=["rt1", "rt2"], writes=["qkr"])
                P.op("act", lambda e: e.copy(out=vtb[:], in_=bank(2)), reads=[bk(2)], writes=["vtb"])
                P.op("act", lambda e: e.activation(out=sgl[:], in_=bank(3), func=AF.Silu), reads=[bk(3)], writes=["sgl"])
                Zt = ct("Z" + sfx)
                P.op("dve", lambda e, Zt=Zt: e.tensor_tensor(out=kz[:], in0=qkr[:, 8:16, :], in1=bc(Zt, 64), op=ALU.mult),
                     reads=["qkr", "ctab"], writes=["kz"])
                ckpt("a3")
                qkr2 = qkr[:].rearrange("p (a b) d -> p a (b d)", b=2)
                for pr in range(8):
                    bnk = 2 + pr // 4
                    P.op("pe", lambda e, pr=pr, bnk=bnk: e.transpose(out=bank(bnk, (pr % 4) * 128, (pr % 4) * 128 + 128),
                                                                    in_=qkr2[:, pr, :], identity=ident),
                         reads=["qkr", "ctab"], writes=[bk(bnk)], inc=(pr % 4 == 3))
                for hh in range(2):
                    P.op("act", lambda e, hh=hh: e.copy(out=qbd[hh * 64:(hh + 1) * 64, :, hh, :],
                                                        in_=ps[hh * 64:(hh + 1) * 64, 1024:1536].rearrange("p (g t) -> p g t", g=4)),
                         reads=[bk(2)], writes=["qbd"])
                P.op("act", lambda e: e.activation(out=kT[:].rearrange("p g t -> p (g t)"), in_=bank(3), func=AF.Copy, scale=0.125),
                     reads=[bk(3)], writes=["kT"])
                XIt = ct("XI" + sfx)
                P.op("dve", lambda e, XIt=XIt: e.tensor_tensor(out=qxT[:].rearrange("p g t -> p (g t)"), in0=bank(2), in1=XIt, op=ALU.mult),
                     reads=[bk(2), "ctab"], writes=["qxT"])
                ckpt("a4")
                for g in range(4):
                    bnk = g // 2
                    P.op("pe", lambda e, g=g, bnk=bnk: e.matmul(
                        bank(bnk, (g % 2) * 256, (g % 2) * 256 + 256), lhsT=kT[:, g, :],
                        rhs=qbd[:, g, :, :].rearrange("p a t -> p (a t)"), start=True, stop=True),
                        reads=["kT", "qbd"], writes=[bk(bnk)], inc=(g % 2 == 1))
                DTt = ct("DT" + sfx)
                for hb in range(2):
                    P.op("dve", lambda e, hb=hb, DTt=DTt: e.tensor_tensor(
                        out=PT[:, hb * 4:hb * 4 + 4, :].rearrange("p h t -> p (h t)"), in0=bank(hb),
                        in1=DTt[:, hb * 512:(hb + 1) * 512], op=ALU.mult),
                        reads=[bk(hb), "ctab"], writes=["PT"])
                GCt = ct("GC" + sfx)
                ckpt("a5")

                def state_upd(dst, dst_key, lhs_of_g, lhs_key, ub, GCt=GCt):
                    for g in range(4):
                        P.op("pe", lambda e, g=g: e.matmul(bank(ub, g * 128, g * 128 + 128), lhsT=lhs_of_g(g),
                                                           rhs=vtb[:, g * 128:(g + 1) * 128], start=True, stop=True),
                             reads=[lhs_key, "vtb"], writes=[bk(ub)], inc=(g == 3))
                    P.op("dve", lambda e: e.tensor_tensor(out=dst, in0=dst, in1=bc(GCt, 64), op=ALU.mult),
                         reads=[dst_key, "ctab"], writes=[dst_key])
                    for hh in range(2):
                        P.op("dve", lambda e, hh=hh: e.tensor_tensor(
                            out=dst[hh * 64:(hh + 1) * 64], in0=dst[hh * 64:(hh + 1) * 64],
                            in1=ps[hh * 64:(hh + 1) * 64, ub * 512:(ub + 1) * 512].rearrange("p (g x) -> p g x", g=4)[:, :, hh * 64:hh * 64 + 64],
                            op=ALU.add),
                            reads=[dst_key, bk(ub)], writes=[dst_key])

                if not smp:
                    for g in range(4):
                        for hh in range(2):
                            h = 2 * g + hh
                            P.op("pe", lambda e, h=h, hh=hh: e.matmul(bank(2, h * 64, h * 64 + 64), lhsT=PT[:, h, :],
                                                                      rhs=vtb[:, h * 64:(h + 1) * 64], start=(hh == 0), stop=False,
                                                                      skip_group_check=True),
                                 reads=["PT", "vtb"], writes=[bk(2)], inc=False)
                        P.op("pe", lambda e, g=g: e.matmul(bank(2, g * 128, g * 128 + 128), lhsT=qxT[:, g, :],
                                                           rhs=Sbd[:, g, :, :].rearrange("p a v -> p (a v)"), start=False, stop=True,
                                                           skip_group_check=True),
                             reads=["qxT", "Sbd"], writes=[bk(2)], inc=(g == 3))
                    state_upd(Sst[:], "Sst", lambda g: kz[:, 2 * g:2 * g + 2, :].rearrange("p a d -> p (a d)"), "kz", 3)
                    for hh in range(2):
                        P.op("act", lambda e, hh=hh: e.copy(out=Sbd[hh * 64:(hh + 1) * 64, :, hh, :], in_=Sst[hh * 64:(hh + 1) * 64, :, :]),
                             reads=["Sst"], writes=["Sbd"])
                    if n == 15:
                        for hh in range(2):
                            P.dma("sp", lambda e, hh=hh: e.dma_start(
                                out=retp_o.rearrange("(g hh) d v -> hh d g v", hh=2)[hh], in_=Sst[hh * 64:(hh + 1) * 64, :, :]),
                                "retp", reads=["Sst"], is_output=True)
                    head_norm(bank(2).rearrange("p (h d) -> p h d", h=8), bk(2), xc[:], "xc", eps_t[:, 1:2], "r")
                else:
                    for h in range(8):
                        P.op("pe", lambda e, h=h: e.matmul(bank(2, h * 64, h * 64 + 64), lhsT=PT[:, h, :],
                                                           rhs=vtb[:, h * 64:(h + 1) * 64], start=True, stop=True),
                             reads=["PT", "vtb"], writes=[bk(2)], inc=(h == 7))
                    kzf = kz[:].rearrange("p h d -> p (h d)")
                    for b in range(16):
                        sl = b % 2
                        for hh in range(2):
                            P.dma("sp", lambda e, b=b, hh=hh, sl=sl: e.dma_start(
                                out=S0b[hh * 64:(hh + 1) * 64, sl, :, :],
                                in_=s_ret[b].rearrange("(g hh) d v -> hh d g v", hh=2)[hh]),
                                "S0b%d" % sl, writes=["S0b%d" % sl])
                        for hh in range(2):
                            P.op("act", lambda e, sl=sl, hh=hh: e.copy(out=S0bd[hh * 64:(hh + 1) * 64, sl, :, hh, :],
                                                                       in_=S0b[hh * 64:(hh + 1) * 64, sl, :, :]),
                                 reads=["S0b%d" % sl], writes=["S0bd%d" % sl])
                        for g in range(4):
                            P.op("pe", lambda e, g=g, b=b, sl=sl: e.matmul(
                                bank(3, g * 128 + b * 8, g * 128 + b * 8 + 8), lhsT=S0bd[:, sl, g, :, :].rearrange("p a v -> p (a v)"),
                                rhs=qxT[:, g, b * 8:b * 8 + 8], start=True, stop=True),
                                reads=["S0bd%d" % sl, "qxT"], writes=[bk(3)], inc=(g == 3))
                        P.op("dve", lambda e, b=b, sl=sl: e.tensor_scalar(out=kzm[:, sl, :], in0=kzf, scalar1=ct("Mk2")[:, b:b + 1],
                                                                          scalar2=None, op0=ALU.mult),
                             reads=["kz", "ctab"], writes=["kzm%d" % sl])
                        state_upd(S0b[:, sl], "S0b%d" % sl, lambda g, sl=sl: kzm[:, sl, g * 128:(g + 1) * 128], "kzm%d" % sl, sl)
                        for hh in range(2):
                            P.dma("sp", lambda e, b=b, hh=hh, sl=sl: e.dma_start(
                                out=rets_o[b].rearrange("(g hh) d v -> hh d g v", hh=2)[hh],
                                in_=S0b[hh * 64:(hh + 1) * 64, sl, :, :]),
                                "S0o%d" % sl, reads=["S0b%d" % sl], is_output=True)
                    P.op("act", lambda e: e.copy(out=junk[:, 0:512], in_=bank(3)), reads=[bk(3)], writes=["junk"])
                    for g in range(4):
                        P.op("pe", lambda e, g=g: e.transpose(out=bank(3, g * 128, g * 128 + 128), in_=junk[:, g * 128:(g + 1) * 128],
                                                              identity=ident),
                             reads=["junk", "ctab"], writes=[bk(3)], inc=(g == 3))
                    P.op("act", lambda e: e.copy(out=junk[:, 512:1024], in_=bank(3)), reads=[bk(3)], writes=["junk"])
                    P.op("dve", lambda e: e.tensor_tensor(out=xc[:].rearrange("p h d -> p (h d)"), in0=bank(2), in1=junk[:, 512:1024], op=ALU.add),
                         reads=[bk(2), "junk"], writes=["xc"])
                    head_norm(xc[:], "xc", xc[:], "xc", eps_t[:, 1:2], "r")
                P.op("dve", lambda e: e.tensor_tensor(out=oall[:, 0:512], in0=xc[:].rearrange("p h d -> p (h d)"), in1=sgl[:], op=ALU.mult),
                     reads=["xc", "sgl"], writes=["oall_r"])

                ckpt("a6")
                if smp:
                    rwv = rw[:].rearrange("p c (b l) -> p c b l", l=9)
                    prev, cur = rwv[:, :, :, 0:8], rwv[:, :, :, 1:9]
                    fmv = fm[:].rearrange("p c (b l) -> p c b l", l=8)
                    mub = cols[:, C_MU:C_MU + 14].unsqueeze(2).unsqueeze(3).to_broadcast([128, 14, 16, 8])
                else:
                    prev, cur = rw[:, :, 0:128], rw[:, :, 1:129]
                    fmv = fm[:]
                    mub = bc(cols[:, C_MU:C_MU + 14], 128)
                P.op("dve", lambda e, prev=prev, cur=cur, fmv=fmv: e.tensor_tensor(out=fmv, in0=prev, in1=cur, op=ALU.subtract),
                     reads=["rw"], writes=["fm"])
                P.op("dve", lambda e, fmv=fmv, mub=mub: e.tensor_tensor(out=fmv, in0=fmv, in1=mub, op=ALU.mult),
                     reads=["fm", "cols"], writes=["fm"])
                P.op("dve", lambda e, fmv=fmv, cur=cur: e.tensor_tensor(out=fmv, in0=fmv, in1=cur, op=ALU.add),
                     reads=["fm", "rw"], writes=["fm"])
                if n == 15:
                    P.op("pe", lambda e: e.transpose(out=ps[0:14, 7 * 512:7 * 512 + 128], in_=rw[:, :, 128], identity=ident),
                         reads=["rw", "ctab"], writes=[bk(7)])
                    P.op("act", lambda e: e.copy(out=shst[0:14, 0:128], in_=ps[0:14, 7 * 512:7 * 512 + 128]), reads=[bk(7)], writes=["Xw"])
                    P.dma("sp", lambda e: e.dma_start(out=shiftp_o.rearrange("(c p) -> c p", p=128), in_=shst[0:14, 0:128]),
                          "shp", reads=["Xw"], is_output=True)
                if smp:
                    rwl = rw[:].rearrange("p c (b l) -> p c b l", l=9)
                    for cg in range(4):
                        ncs = 4 if cg < 3 else 2
                        for c4 in range(ncs):
                            c = cg * 4 + c4
                            P.op("pe", lambda e, c=c, c4=c4: e.transpose(out=ps[0:16, 7 * 512 + c4 * 128:7 * 512 + c4 * 128 + 128],
                                                                        in_=rwl[:, c, :, 8], identity=ident),
                                 reads=["rw", "ctab"], writes=[bk(7)], inc=(c4 == ncs - 1))
                        P.op("act", lambda e, ncs=ncs: e.copy(out=shst[0:16, 0:ncs * 128], in_=ps[0:16, 7 * 512:7 * 512 + ncs * 128]),
                             reads=[bk(7)], writes=["Xw"])
                        P.dma("sp", lambda e, cg=cg, ncs=ncs: e.dma_start(out=shifts_o[:, cg * 512:cg * 512 + ncs * 128], in_=shst[0:16, 0:ncs * 128]),
                              "shs", reads=["Xw"], is_output=True)
                P.op("act", lambda e: e.activation(out=th[0:64, :], in_=fm[0:64, 12, :], func=AF.Tanh), reads=["fm"], writes=["th"])
                for g in range(4):
                    P.op("pe", lambda e, g=g: e.matmul(bank(0, g * 128, g * 128 + 128), lhsT=w2a2[0:64, g * 128:(g + 1) * 128],
                                                       rhs=th[0:64, :], start=True, stop=True),
                         reads=["w2a2", "th"], writes=[bk(0)], inc=(g == 3))
                for g in range(4):
                    P.op("pe", lambda e, g=g: e.matmul(bank(1, g * 128, g * 128 + 128), lhsT=w2a2[64:128, g * 128:(g + 1) * 128],
                                                       rhs=fm[64:128, 12, :], start=True, stop=True),
                         reads=["w2a2", "fm"], writes=[bk(1)], inc=(g == 3))
                for g in range(4):
                    P.op("act", lambda e, g=g: e.activation(out=sgm[:, g, :], in_=bank(0, g * 128, g * 128 + 128), func=AF.Sigmoid,
                                                            bias=cols[:, C_W0 + g:C_W0 + g + 1]),
                         reads=[bk(0), "cols"], writes=["sgm"])
                P.op("act", lambda e: e.activation(out=ew[:], in_=sgm[:], func=AF.Exp, scale=-0.6065306597126334),
                     reads=["sgm"], writes=["ew"])
                for g in range(4):
                    P.op("act", lambda e, g=g: e.activation(out=aa[:, g, :], in_=bank(1, g * 128, g * 128 + 128), func=AF.Sigmoid,
                                                            bias=cols[:, C_A0 + g:C_A0 + g + 1]),
                         reads=[bk(1), "cols"], writes=["aa"])
                P.op("act", lambda e: e.activation(out=sigfg[:], in_=fm[:, 13, :], func=AF.Sigmoid), reads=["fm"], writes=["sigfg"])
                P.op("pe", lambda e: e.matmul(bank(0), lhsT=sigfg[:], rhs=g2[:], start=True, stop=True),
                     reads=["sigfg", "g2"], writes=[bk(0)])
                P.op("act", lambda e: e.copy(out=gate[:], in_=bank(0)), reads=[bk(0)], writes=["gate"])
                P.op("dve", lambda e: e.tensor_tensor(out=kk[:], in0=fm[:, 4:8, :], in1=bc(cols[:, C_KK:C_KK + 4], 128), op=ALU.mult),
                     reads=["fm", "cols"], writes=["kk"])
                P.op("act", lambda e: e.activation(out=sqk[:], in_=kk[:], func=AF.Square), reads=["kk"], writes=["prk"])
                P.op("pe", lambda e: e.matmul(bank(1), lhsT=blockmask, rhs=sqk[:].rearrange("p g t -> p (g t)"), start=True, stop=True),
                     reads=["prk", "ctab"], writes=[bk(1)])
                P.op("act", lambda e: e.activation(out=nrm[:].rearrange("p g t -> p (g t)"), in_=bank(1), func=AF.Sqrt),
                     reads=[bk(1)], writes=["nrm"])
                P.op("dve", lambda e: e.tensor_scalar(out=nrm[:], in0=nrm[:], scalar1=1e-12, scalar2=None, op0=ALU.max),
                     reads=["nrm"], writes=["nrm"])
                P.op("dve", lambda e: e.reciprocal(out=nrm[:], in_=nrm[:]), reads=["nrm"], writes=["nrm"])
                P.op("dve", lambda e: e.tensor_tensor(out=kk[:], in0=kk[:], in1=nrm[:], op=ALU.mult), reads=["kk", "nrm"], writes=["kk"])
                P.op("dve", lambda e: e.tensor_tensor(out=kp[:], in0=aa[:], in1=bc(cols[:, C_KA:C_KA + 4], 128), op=ALU.mult),
                     reads=["aa", "cols"], writes=["kp"])
                P.op("dve", lambda e: e.tensor_tensor(out=kp[:], in0=kp[:], in1=bc(cols[:, C_OMKA:C_OMKA + 4], 128), op=ALU.add),
                     reads=["kp", "cols"], writes=["kp"])
                P.op("dve", lambda e: e.tensor_tensor(out=kp[:], in0=kp[:], in1=fm[:, 4:8, :], op=ALU.mult), reads=["kp", "fm"], writes=["kp"])
                P.op("dve", lambda e: e.scalar_tensor_tensor(out=nkka[:].rearrange("p g t -> p (g t)"), in0=kk[:].rearrange("p g t -> p (g t)"),
                                                             scalar=-1.0, in1=aa[:].rearrange("p g t -> p (g t)"), op0=ALU.mult, op1=ALU.mult),
                     reads=["kk", "aa"], writes=["nkka"])
                P.op("dve", lambda e: e.tensor_tensor(out=prk[:], in0=fm[:, 0:4, :], in1=kp[:], op=ALU.mult), reads=["fm", "kp"], writes=["prk"])
                P.op("dve", lambda e: e.tensor_tensor(out=prk[:], in0=prk[:], in1=bc(cols[:, C_RK:C_RK + 4], 128), op=ALU.mult),
                     reads=["prk", "cols"], writes=["prk"])
                for g in range(4):
                    P.op("pe", lambda e, g=g: e.matmul(bank(7, 2 * g, 2 * g + 2), lhsT=prk[:, g, :], rhs=ct("halfmask2"),
                                                       start=True, stop=True),
                         reads=["prk", "ctab"], writes=[bk(7)], inc=(g == 3))
                P.op("act", lambda e: e.copy(out=bonus[:], in_=bank(7, 0, 8)), reads=[bk(7)], writes=["bonus"])
                for g in range(4):
                    P.op("pe", lambda e, g=g: e.transpose(out=bank(7, g * 128, g * 128 + 128), in_=fm[:, 8 + g, :], identity=ident),
                         reads=["fm", "ctab"], writes=[bk(7)], inc=(g == 3))
                P.op("act", lambda e: e.copy(out=vtok[:], in_=bank(7)), reads=[bk(7)], writes=["vtok"])
                P.op("act", lambda e: e.copy(out=vhi[:], in_=fm[:, 8:12, :]), reads=["fm"], writes=["vhi"])
                P.op("dve", lambda e: e.tensor_tensor(out=vlo[:], in0=fm[:, 8:12, :], in1=vhi[:], op=ALU.subtract),
                     reads=["fm", "vhi"], writes=["vlo"])

                ckpt("a7")
                for b in range(NB):
                    if smp:
                        wkv_state_in(b)
                    for l in range(L):
                        t = b * L + l
                        scan_step(t, t % NR, (t - 1) if l > 0 else None)
                    scan_y(b * L + L - 1, (b * L + L - 1) % NR)
                    if smp:
                        wkv_state_out(wkvs_o[b])
                if n == 15:
                    wkv_state_out(wkvp_o)

                ckpt("a8")
                P.op("act", lambda e: e.copy(out=xc2[:].rearrange("p (g hh) i -> p g hh i", hh=2),
                                             in_=bank(6).rearrange("p (hh g i) -> p g hh i", hh=2, g=4)),
                     reads=[bk(6)], writes=["xc"])
                head_norm(xc2[:], "xc", xc2[:], "xc", eps_t[:, 2:3], "w")
                xc2f = xc2[:].rearrange("p h d -> p (h d)")
                P.op("dve", lambda e: e.tensor_tensor(out=xc2f, in0=xc2f, in1=rowsA[:, 0:512], op=ALU.mult), reads=["xc", "rowsA"], writes=["xc"])
                P.op("dve", lambda e: e.tensor_tensor(out=xc2f, in0=xc2f, in1=rowsA[:, 512:1024], op=ALU.add), reads=["xc", "rowsA"], writes=["xc"])
                P.op("dve", lambda e: e.tensor_tensor(out=hn_sq[:].rearrange("p (h d) -> p h d", h=8),
                                                      in0=vtok[:].rearrange("p (h d) -> p h d", h=8), in1=bc(bonus[:], 64), op=ALU.mult),
                     reads=["vtok", "bonus"], writes=["hn_sq"])
                P.op("dve", lambda e: e.tensor_tensor(out=xc2f, in0=xc2f, in1=hn_sq[:], op=ALU.add), reads=["xc", "hn_sq"], writes=["xc"])
                P.op("dve", lambda e: e.tensor_tensor(out=oall[:, 512:1024], in0=xc2f, in1=gate[:], op=ALU.mult),
                     reads=["xc", "gate"], writes=["oall_w"])
                if n == 1:
                    dbg("oall", oall[:], [128, 1024], ["oall_r", "oall_w"])
                for hb, bnk in ((0, 2), (1, 3)):
                    for c4 in range(4):
                        c = hb * 4 + c4
                        P.op("pe", lambda e, c=c, c4=c4, bnk=bnk: e.transpose(out=bank(bnk, c4 * 128, c4 * 128 + 128),
                                                                             in_=oall[:, c * 128:(c + 1) * 128], identity=ident),
                             reads=["oall_r", "oall_w", "ctab"], writes=[bk(bnk)], inc=(c4 == 3))
                    P.op("act", lambda e, hb=hb, bnk=bnk: e.copy(out=oT[:, hb * 4:hb * 4 + 4, :].rearrange("p c t -> p (c t)"), in_=bank(bnk)),
                         reads=[bk(bnk)], writes=["oT"])
                for nb_ in range(2):
                    for c in range(8):
                        P.op("pe", lambda e, nb_=nb_, c=c: e.matmul(bank(nb_), lhsT=oT[:, c, :], rhs=wout_bf[:, c, nb_ * 512:(nb_ + 1) * 512],
                                                                    start=(c == 0), stop=(c == 7)),
                             reads=["oT", "wout_bf"], writes=[bk(nb_)], inc=(c == 7))
                    P.op("dve", lambda e, nb_=nb_: e.tensor_tensor(out=h1t[:, nb_ * 512:(nb_ + 1) * 512], in0=bank(nb_),
                                                                   in1=xt[:, nb_ * 512:(nb_ + 1) * 512], op=ALU.add),
                         reads=[bk(nb_), "xt"], writes=["xs"])
                P.dma("sp", lambda e, r0=r0, r1=r1: e.dma_start(out=h1_d[r0:r1, :], in_=h1t[:]), "h1o",
                      reads=["xs"], writes=["h1d%d" % n])
                if "h1" in debug:
                    if n == 0:
                        dbg_outs["h1"] = dout("dbg_h1", [TOK, D])
                    o_ = dbg_outs["h1"]
                    P.dma("sp", lambda e, r0=r0, r1=r1, o_=o_: e.dma_start(out=o_[r0:r1, :], in_=h1t[:]), "dbgh1",
                          reads=["xs"], is_output=True)
            P.barrier()
            print("arena A watermark", aoff[0], "of", ARENA_F32)
            aoff[0] = a_mark

        if "B" in phases:
            wq_bf = sb("wq_bf", [128, 8, 2048], BF16)
            pg_bf = sb("pg_bf", [128, 8, 1024], BF16)
            pp_bf = sb("pp_bf", [128, 2, 1024], BF16)
            keysT = sb("keysT", [128, 16, 128], BF16)
            keys_st = sb("keys_st", [128, 16, 128])
            B32 = sb("B32", [128, 32, 128], BF16)
            rowsB = sb("rowsB", [128, 2048])
            wq_v = peer_wq.rearrange("(c p) n -> p c n", p=128)
            for c in range(8):
                P.dma("pool", lambda e, c=c: e.dma_start(out=wq_bf[:, c, :], in_=wq_v[:, c, :]), "wload%d" % (c % 4), writes=["wq_bf"])
            pg_v = ple_gate.rearrange("(c p) n -> p c n", p=128)
            for c in range(8):
                P.dma("pool", lambda e, c=c: e.dma_start(out=pg_bf[:, c, :], in_=pg_v[:, c, :]), "wload%d" % (c % 4), writes=["pg_bf"])
            pp_v = ple_proj.rearrange("(c p) n -> p c n", p=128)
            for c in range(2):
                P.dma("pool", lambda e, c=c: e.dma_start(out=pp_bf[:, c, :], in_=pp_v[:, c, :]), "wload", writes=["pp_bf"])
            P.dma("act", lambda e: e.dma_start(out=keys_st[:], in_=peer_keys.rearrange("c n d -> n c d")), "keys", writes=["keys_st"])
            P.dma("act", lambda e: e.dma_start(out=rowsB[:], in_=rows_d[:, 1024:3072]), "rowsB", writes=["rowsB"])
            for c in range(16):
                bnk = 6 + (c // 4) % 2
                P.op("pe", lambda e, c=c, bnk=bnk: e.transpose(out=bank(bnk, (c % 4) * 128, (c % 4) * 128 + 128), in_=keys_st[:, c, :],
                                                               identity=ident),
                     reads=["keys_st", "ctab"], writes=[bk(bnk)], inc=(c % 4 == 3))
                if c % 4 == 3:
                    P.op("act", lambda e, c=c, bnk=bnk: e.copy(out=keysT[:, c - 3:c + 1, :].rearrange("p c n -> p (c n)"), in_=bank(bnk)),
                         reads=[bk(bnk)], writes=["keysT"])
            P.op("pool", lambda e: e.memset(B32[:], 1.0), writes=["B32"])
            for q in range(3):
                P.op("pool", lambda e, q=q: e.affine_select(out=B32[q * 32:(q + 1) * 32], in_=B32[q * 32:(q + 1) * 32],
                                                            pattern=[[1, 32], [0, 128]], compare_op=ALU.is_equal, fill=0.0,
                                                            base=0, channel_multiplier=-1),
                     reads=["B32"], writes=["B32"])

            h1bs = [sb("h1tB%d" % i, [128, 1024]) for i in range(2)]
            pts = [sb("pt%d" % i, [128, 256]) for i in range(2)]
            xs2 = sb("xs2", [128, 1024])
            xn2T = sb("xn2T", [128, 8, 128], BF16)
            xn2b = sb("xn2b", [128, 1024], BF16)
            xhi = sb("xhi", [32, 1024], BF16)
            qTb = sb("qTb", [128, 16, 128], BF16)
            Ssb = sb("Ssb", [128, 16, 128])
            S2 = sb("S2", [128, 256])
            sv = sb("sv", [128, 16, 16])
            siu = sb("siu", [128, 16, 16], U32)
            sif = sb("sif", [128, 16, 16])
            cand = sb("cand", [128, 8, 256])
            cv = sb("cv", [128, 8, 16])
            ciu = sb("ciu", [128, 8, 16], U32)
            abu = sb("abu", [128, 2, 128], U32)
            abf = sb("abf", [128, 2, 128])
            oh = sb("oh", [128, 8, 256])
            e01 = sb("e01", [128, 2, 128])
            ef = sb("ef", [128, 128])
            ex = sb("ex", [128, 8, 16])
            zs = sb("zs", [128, 8])
            gsm = sb("gsm", [128, 128])
            eTus = [sb("eTu%d" % i, [128, 128], U32) for i in range(2)]
            gsmT = sb("gsmT", [128, 128])
            hvT = sb("hvT", [128, 128])
            gl = sb("gl", [128, 128])
            actb = sb("actb", [128, 128], BF16)
            NU, NV, NA = 4, 6, 6
            Ug = [sb("Ug%d" % i, [128, 1024]) for i in range(NU)]
            Vg = [sb("Vg%d" % i, [128, 1024], BF16) for i in range(NV)]
            Awr = [sb("Awr%d" % i, [128, 255], BF16) for i in range(NA)]
            h2t = sb("h2t", [128, 1024])
            xn3T = sb("xn3T", [128, 8, 128], BF16)
            gs = sb("gs", [128, 1024])
            pT = sb("pT", [128, 2, 128], BF16)
            yt = sb("yt", [128, 1024])
            for i in range(NA):
                P.op("pool", lambda e, i=i: e.memset(Awr[i][:], 0.0), writes=["Awr%d" % i])

            def top16(src, src_key, dst_v, dst_i, width):
                P.op("dve", lambda e: e.max(out=dst_v[:, 0:8], in_=src), reads=[src_key], writes=["tk_v"])
                P.op("dve", lambda e: e.max_index(out=dst_i[:, 0:8], in_max=dst_v[:, 0:8], in_values=src), reads=[src_key, "tk_v"], writes=["tk_i"])
                P.op("dve", lambda e: e.match_replace(out=S2[:, 0:width], in_to_replace=dst_v[:, 0:8], in_values=src, imm_value=-1e30),
                     reads=[src_key, "tk_v"], writes=["S2"])
                P.op("dve", lambda e: e.max(out=dst_v[:, 8:16], in_=S2[:, 0:width]), reads=["S2"], writes=["tk_v"])
                P.op("dve", lambda e: e.max_index(out=dst_i[:, 8:16], in_max=dst_v[:, 8:16], in_values=S2[:, 0:width]),
                     reads=["S2", "tk_v"], writes=["tk_i"])

            def stage_F(n):
                r0, r1 = n * 128, (n + 1) * 128
                h1b = h1bs[n % 2]; eTu = eTus[n % 2]; pt = pts[n % 2]
                h1k = "h1t%d" % (n % 2); eTk = "eTu%d" % (n % 2); ptk = "pt%d" % (n % 2)
                P.dma("sp", lambda e, r0=r0, r1=r1: e.dma_start(out=h1b[:], in_=h1_d[r0:r1, :]), "h1i", reads=["h1d%d" % n], writes=[h1k])
                P.dma("sp", lambda e, r0=r0, r1=r1: e.dma_start(out=pt[:], in_=p_all[r0:r1, :]), ptk, writes=[ptk])
                rms_rstd(h1b[:], h1k, junk[:], "junk", ssq[:], rstd[:], "b")
                P.op("dve", lambda e: e.tensor_scalar(out=xs2[:], in0=h1b[:], scalar1=rstd[:, 0:1], scalar2=None, op0=ALU.mult),
                     reads=[h1k, "brstd"], writes=["xs2"])
                transposes8(xs2, "xs2", xn2T, "xn2T", C_GFFN, 6, 7)
                yield
                P.op("dve", lambda e: e.tensor_tensor(out=xn2b[:], in0=xs2[:], in1=rowsB[:, 1024:2048], op=ALU.mult),
                     reads=["xs2", "rowsB"], writes=["xn2b"])
                P.dma("act", lambda e: e.dma_start(out=xhi[:], in_=xn2b[96:128, :]), "xhi", reads=["xn2b"], writes=["xhi"])
                for cg in range(4):
                    bnk = 6 + cg % 2
                    for c4 in range(4):
                        ch = cg * 4 + c4
                        for dc in range(8):
                            P.op("pe", lambda e, ch=ch, c4=c4, dc=dc, bnk=bnk: e.matmul(
                                bank(bnk, c4 * 128, c4 * 128 + 128), lhsT=wq_bf[:, dc, ch * 128:(ch + 1) * 128], rhs=xn2T[:, dc, :],
                                start=(dc == 0), stop=(dc == 7)),
                                reads=["wq_bf", "xn2T"], writes=[bk(bnk)], inc=(dc == 7 and c4 == 3))
                    P.op("act", lambda e, cg=cg, bnk=bnk: e.copy(out=qTb[:, cg * 4:cg * 4 + 4, :].rearrange("p c t -> p (c t)"), in_=bank(bnk)),
                         reads=[bk(bnk)], writes=["qTb"])
                    yield
                for cg in range(4):
                    bnk = 6 + cg % 2
                    for c4 in range(4):
                        ch = cg * 4 + c4
                        P.op("pe", lambda e, ch=ch, c4=c4, bnk=bnk: e.matmul(bank(bnk, c4 * 128, c4 * 128 + 128), lhsT=qTb[:, ch, :],
                                                                             rhs=keysT[:, ch, :], start=True, stop=True),
                             reads=["qTb", "keysT"], writes=[bk(bnk)], inc=(c4 == 3))
                    P.op("act", lambda e, cg=cg, bnk=bnk: e.copy(out=Ssb[:, cg * 4:cg * 4 + 4, :].rearrange("p c t -> p (c t)"), in_=bank(bnk)),
                         reads=[bk(bnk)], writes=["Ssb"])
                    yield
                for ch in range(16):
                    top16(Ssb[:, ch, :], "Ssb", sv[:, ch, :], siu[:, ch, :], 128)
                    if ch % 4 == 3:
                        yield
                P.op("dve", lambda e: e.tensor_copy(out=sif[:], in_=siu[:]), reads=["tk_i"], writes=["sif"])
                sv4 = sv[:].rearrange("p (h c) k -> p h c k", c=2)
                P.op("dve", lambda e: e.tensor_tensor(out=cand[:].rearrange("p h (a b) -> p h a b", a=16),
                                                      in0=sv4[:, :, 0, :].unsqueeze(3).to_broadcast([128, 8, 16, 16]),
                                                      in1=sv4[:, :, 1, :].unsqueeze(2).to_broadcast([128, 8, 16, 16]), op=ALU.add),
                     reads=["tk_v"], writes=["cand"])
                yield
                for h in range(8):
                    top16(cand[:, h, :], "cand", cv[:, h, :], ciu[:, h, :], 256)
                    if h % 2 == 1:
                        yield
                ciu2 = ciu[:].rearrange("p h k -> p (h k)")
                P.op("dve", lambda e: e.tensor_single_scalar(out=abu[:, 0, :], in_=ciu2, scalar=4, op=ALU.logical_shift_right),
                     reads=["tk_i"], writes=["abu"])
                P.op("dve", lambda e: e.tensor_single_scalar(out=abu[:, 1, :], in_=ciu2, scalar=15, op=ALU.bitwise_and),
                     reads=["tk_i"], writes=["abu"])
                P.op("dve", lambda e: e.tensor_copy(out=abf[:], in_=abu[:]), reads=["abu"], writes=["abf"])
                sif4 = sif[:].rearrange("p (h c) k -> p h c k", c=2)
                iob = ct("iota16").unsqueeze(1).unsqueeze(2).to_broadcast([128, 8, 16, 16])
                oh4 = oh[:].rearrange("p h (k a) -> p h k a", k=16)
                for c in range(2):
                    P.op("dve", lambda e, c=c: e.tensor_tensor(
                        out=oh4, in0=iob, in1=abf[:, c, :].rearrange("p (h k) -> p h k", h=8).unsqueeze(3).to_broadcast([128, 8, 16, 16]),
                        op=ALU.is_equal), reads=["abf", "ctab"], writes=["oh"])
                    P.op("dve", lambda e, c=c: e.tensor_tensor(out=oh4, in0=oh4, in1=sif4[:, :, c, :].unsqueeze(2).to_broadcast([128, 8, 16, 16]),
                                                               op=ALU.mult), reads=["oh", "sif"], writes=["oh"])
                    P.op("dve", lambda e, c=c: e.tensor_reduce(out=e01[:, c, :].rearrange("p (h k) -> p h k", h=8), in_=oh4, axis=AX.X, op=ALU.add),
                         reads=["oh"], writes=["e01"])
                    yield
                P.op("dve", lambda e: e.scalar_tensor_tensor(out=ef[:], in0=e01[:, 0, :], scalar=128.0, in1=e01[:, 1, :], op0=ALU.mult, op1=ALU.add),
                     reads=["e01"], writes=["ef"])
                P.op("dve", lambda e: e.tensor_tensor(out=ex[:], in0=cv[:], in1=cv[:, :, 0:1].to_broadcast([128, 8, 16]), op=ALU.subtract),
                     reads=["tk_v"], writes=["ex"])
                P.op("act", lambda e: e.activation(out=ex[:], in_=ex[:], func=AF.Exp), reads=["ex"], writes=["ex"])
                P.op("dve", lambda e: e.tensor_reduce(out=zs[:], in_=ex[:], axis=AX.X, op=ALU.add), reads=["ex"], writes=["zs"])
                P.op("dve", lambda e: e.reciprocal(out=zs[:], in_=zs[:]), reads=["zs"], writes=["zs"])
                P.op("dve", lambda e: e.tensor_tensor(out=gsm[:].rearrange("p (h k) -> p h k", h=8), in0=ex[:], in1=bc(zs[:], 16), op=ALU.mult),
                     reads=["ex", "zs"], writes=["gsm"])
                P.op("pe", lambda e: e.transpose(out=bank(6, 0, 128), in_=ef[:], identity=ident), reads=["ef", "ctab"], writes=[bk(6)])
                P.op("dve", lambda e: e.tensor_copy(out=eTu[:], in_=bank(6, 0, 128)), reads=[bk(6)], writes=[eTk])
                P.op("pe", lambda e: e.transpose(out=bank(7, 0, 128), in_=gsm[:], identity=ident), reads=["gsm", "ctab"], writes=[bk(7)])
                P.op("act", lambda e: e.copy(out=gsmT[:], in_=bank(7, 0, 128)), reads=[bk(7)], writes=["gsmT"])
                if "eT" in debug and n == 0:
                    dbg("eT", ef[:], [128, 128], ["ef"])
                    dbg("gsm", gsm[:], [128, 128], ["gsm"])
                yield

            def stage_U(n):
                r0, r1 = n * 128, (n + 1) * 128
                h1b = h1bs[n % 2]; eTu = eTus[n % 2]; pt = pts[n % 2]
                h1k = "h1t%d" % (n % 2); eTk = "eTu%d" % (n % 2); ptk = "pt%d" % (n % 2)
                for t in range(128):
                    su = t % NU
                    P.dma("pool", lambda e, t=t, su=su: e.indirect_dma_start(
                        out=Ug[su][:], out_offset=None, in_=peer_u, in_offset=bass.IndirectOffsetOnAxis(ap=eTu[:, t:t + 1], axis=0)),
                        "Ug%d" % su, reads=[eTk], writes=["Ug%d" % su])
                    q, rr = t // 32, t % 32
                    xb0 = (t % 2) * 2
                    for hf in range(2):
                        if q < 3:
                            P.op("pe", lambda e, q=q, rr=rr, hf=hf, xb0=xb0: e.matmul(
                                bank(xb0 + hf), lhsT=B32[q * 32:(q + 1) * 32, rr, :], rhs=xn2b[q * 32:(q + 1) * 32, hf * 512:(hf + 1) * 512],
                                start=True, stop=True),
                                reads=["B32", "xn2b"], writes=[bk(xb0 + hf)], inc=(hf == 1))
                        else:
                            P.op("pe", lambda e, rr=rr, hf=hf, xb0=xb0: e.matmul(
                                bank(xb0 + hf), lhsT=B32[0:32, rr, :], rhs=xhi[0:32, hf * 512:(hf + 1) * 512],
                                start=True, stop=True),
                                reads=["B32", "xhi"], writes=[bk(xb0 + hf)], inc=(hf == 1))
                    dummies(DUM_B, 6, wq_bf[:, 0, :], "wq_bf")
                    P.op("dve", lambda e, t=t, su=su, xb0=xb0: e.scalar_tensor_tensor(
                        out=oh[:].rearrange("p h x -> p (h x)")[:, 0:1024], in0=Ug[su][:], scalar=1.0, in1=ps[:, xb0 * 512:xb0 * 512 + 1024],
                        op0=ALU.mult, op1=ALU.mult, accum_out=hvT[:, t:t + 1]),
                        reads=["Ug%d" % su, bk(xb0), bk(xb0 + 1)], writes=["oh", "hvT"])

            def stage_G(n):
                r0, r1 = n * 128, (n + 1) * 128
                h1b = h1bs[n % 2]; eTu = eTus[n % 2]; pt = pts[n % 2]
                h1k = "h1t%d" % (n % 2); eTk = "eTu%d" % (n % 2); ptk = "pt%d" % (n % 2)
                P.op("act", lambda e: e.activation(out=gl[:], in_=hvT[:], func=AF.Gelu), reads=["hvT"], writes=["gl"])
                P.op("dve", lambda e: e.tensor_tensor(out=actb[:], in0=gl[:], in1=gsmT[:], op=ALU.mult), reads=["gl", "gsmT"], writes=["actb"])

            def stage_V(n, gen):
                r0, r1 = n * 128, (n + 1) * 128
                h1b = h1bs[n % 2]; eTu = eTus[n % 2]; pt = pts[n % 2]
                h1k = "h1t%d" % (n % 2); eTk = "eTu%d" % (n % 2); ptk = "pt%d" % (n % 2)
                for t in range(128):
                    sv_ = t % NV
                    sa = t % NA
                    P.dma("pool", lambda e, t=t, sv_=sv_: e.indirect_dma_start(
                        out=Vg[sv_][:], out_offset=None, in_=peer_v, in_offset=bass.IndirectOffsetOnAxis(ap=eTu[:, t:t + 1], axis=0)),
                        "Vg%d" % sv_, reads=[eTk], writes=["Vg%d" % sv_])
                    P.op("act", lambda e, t=t, sa=sa: e.copy(out=Awr[sa][:, 127:128], in_=actb[:, t:t + 1]),
                         reads=["actb"], writes=["Awr%d" % sa])
                    for hf in range(2):
                        P.op("pe", lambda e, t=t, sv_=sv_, sa=sa, hf=hf: e.matmul(
                            bank(4 + hf), lhsT=Awr[sa][:, 127 - t:255 - t], rhs=Vg[sv_][:, hf * 512:(hf + 1) * 512],
                            start=(t == 0), stop=(t == 127), skip_group_check=True),
                            reads=["Awr%d" % sa, "Vg%d" % sv_], writes=[bk(4 + hf)], inc=(hf == 1))
                    dummies(DUM_B, 6, wq_bf[:, 0, :], "wq_bf")
                    if gen is not None and t % 6 == 5:
                        next(gen, None)
                if gen is not None:
                    for _ in gen:
                        pass

            def stage_T(n):
                r0, r1 = n * 128, (n + 1) * 128
                h1b = h1bs[n % 2]; eTu = eTus[n % 2]; pt = pts[n % 2]
                h1k = "h1t%d" % (n % 2); eTk = "eTu%d" % (n % 2); ptk = "pt%d" % (n % 2)
                for hf in range(2):
                    P.op("dve", lambda e, hf=hf: e.tensor_tensor(out=h2t[:, hf * 512:(hf + 1) * 512], in0=bank(4 + hf),
                                                                 in1=h1b[:, hf * 512:(hf + 1) * 512], op=ALU.add),
                         reads=[bk(4 + hf), h1k], writes=["h2t"])
                if "h2" in debug:
                    if n == 0:
                        dbg_outs["h2"] = dout("dbg_h2", [TOK, D])
                    o_ = dbg_outs["h2"]
                    P.dma("sp", lambda e, r0=r0, r1=r1, o_=o_: e.dma_start(out=o_[r0:r1, :], in_=h2t[:]), "dbgh2",
                          reads=["h2t"], is_output=True)
                rms_rstd(h2t[:], "h2t", junk[:], "junk", ssq[:], rstd[:], "c")
                P.op("dve", lambda e: e.tensor_scalar(out=xs2[:], in0=h2t[:], scalar1=rstd[:, 0:1], scalar2=None, op0=ALU.mult),
                     reads=["h2t", "crstd"], writes=["xs2"])
                transposes8(xs2, "xs2", xn3T, "xn3T", C_GPLE, 6, 7)
                for c in range(2):
                    P.op("pe", lambda e, c=c: e.transpose(out=bank(6, c * 128, c * 128 + 128), in_=pt[:, c * 128:(c + 1) * 128], identity=ident),
                         reads=[ptk, "ctab"], writes=[bk(6)], inc=(c == 1))
                P.op("act", lambda e: e.copy(out=pT[:].rearrange("p c t -> p (c t)"), in_=bank(6, 0, 256)), reads=[bk(6)], writes=["pT"])
                for hf in range(2):
                    for dc in range(8):
                        P.op("pe", lambda e, hf=hf, dc=dc: e.matmul(bank(hf), lhsT=xn3T[:, dc, :], rhs=pg_bf[:, dc, hf * 512:(hf + 1) * 512],
                                                                    start=(dc == 0), stop=(dc == 7)),
                             reads=["xn3T", "pg_bf"], writes=[bk(hf)], inc=(dc == 7))
                    P.op("act", lambda e, hf=hf: e.activation(out=gs[:, hf * 512:(hf + 1) * 512], in_=bank(hf), func=AF.Sigmoid),
                         reads=[bk(hf)], writes=["gs"])
                for hf in range(2):
                    for c in range(2):
                        P.op("pe", lambda e, hf=hf, c=c: e.matmul(bank(2 + hf), lhsT=pT[:, c, :], rhs=pp_bf[:, c, hf * 512:(hf + 1) * 512],
                                                                  start=(c == 0), stop=(c == 1)),
                             reads=["pT", "pp_bf"], writes=[bk(2 + hf)], inc=(c == 1))
                    P.op("dve", lambda e, hf=hf: e.tensor_tensor(out=gs[:, hf * 512:(hf + 1) * 512], in0=bank(2 + hf),
                                                                 in1=gs[:, hf * 512:(hf + 1) * 512], op=ALU.mult),
                         reads=[bk(2 + hf), "gs"], writes=["gs"])
                P.op("dve", lambda e: e.tensor_tensor(out=h2t[:], in0=h2t[:], in1=gs[:], op=ALU.add), reads=["h2t", "gs"], writes=["h2t"])
                rms_rstd(h2t[:], "h2t", junk[:], "junk", ssq[:], rstd[:], "d")
                P.op("dve", lambda e: e.tensor_scalar(out=yt[:], in0=h2t[:], scalar1=rstd[:, 0:1], scalar2=None, op0=ALU.mult),
                     reads=["h2t", "drstd"], writes=["yt"])
                P.op("dve", lambda e: e.tensor_tensor(out=yt[:], in0=yt[:], in1=rowsB[:, 0:1024], op=ALU.mult), reads=["yt", "rowsB"], writes=["yt"])
                P.dma("sp", lambda e, r0=r0, r1=r1: e.dma_start(out=y_o[r0:r1, :], in_=yt[:]), "yo", reads=["yt"], is_output=True)

            ntb = min(NT, nt_limit)
            for _ in stage_F(0):
                pass
            for n in range(ntb):
                stage_U(n)
                stage_G(n)
                stage_V(n, stage_F(n + 1) if n + 1 < ntb else None)
                stage_T(n)

        print("arena end watermark", aoff[0], "of", ARENA_F32)
        P.finish()
        P.run_block(block)
    return nc, tabs, dbg_outs


_CACHE = {}


def make_in_maps(inp, tabs):
    f = np.float32
    g = lambda k: np.asarray(inp[k], dtype=f)

    def colsof(v, nch):
        return np.ascontiguousarray(v.reshape(nch, 128).T)

    cols = np.zeros((128, 64), f)
    cols[:, 0:8] = colsof(g("norm_mix")[0], 8)
    cols[:, 8:16] = colsof(g("norm_ffn")[0], 8)
    cols[:, 16:24] = colsof(g("ple_norm")[0], 8)
    cols[:, 24:38] = colsof(g("rw_mu")[0], 14)
    cols[:, 38:42] = colsof(g("rw_w0")[0], 4)
    cols[:, 42:46] = colsof(g("rw_a0")[0], 4)
    cols[:, 46:50] = colsof(g("rw_kk")[0], 4)
    cols[:, 50:54] = colsof(g("rw_ka")[0], 4)
    cols[:, 54:58] = colsof(g("rw_rk")[0].reshape(512), 4)
    rows = np.zeros((128, 3072), f)
    rows[:, 0:512] = g("rw_ln_g")[0][None, :]
    rows[:, 512:1024] = g("rw_ln_b")[0][None, :]
    rows[:, 1024:2048] = g("norm_final")[None, :]
    rows[:, 2048:3072] = g("norm_ffn")[0][None, :]
    ctab = np.concatenate([tabs[k] for k in CT_ORDER], axis=1).astype(f)
    w2a2 = np.concatenate([g("rw_w2")[0], g("rw_a2")[0]], axis=0)
    shared = dict(
        w_in=g("w_in")[0], w_out=g("w_out")[0], peer_wq=g("peer_wq")[0],
        peer_keys=g("peer_keys")[0].reshape(16, 128, 128), peer_u=g("peer_u")[0], peer_v=g("peer_v")[0],
        ple_gate=g("ple_gate")[0], ple_proj=g("ple_proj")[0], rw_w2a2=np.ascontiguousarray(w2a2), rw_g2=g("rw_g2")[0],
        cols=cols, rows=rows, ctab=ctab, cs=tabs["cs"],
    )
    xp, xs_ = g("x_prompt"), g("x_sample")
    pp, ps_ = g("p_prompt")[0], g("p_sample")[0]
    sr, sw, ss = g("state_ret")[0], g("state_wkv")[0], g("state_shift")[0]
    maps = []
    for c in range(NCORES):
        m = dict(shared)
        m["x_all"] = np.ascontiguousarray(np.concatenate([xp[c], xs_[16 * c:16 * c + 16].reshape(128, D)], axis=0))
        m["p_all"] = np.ascontiguousarray(np.concatenate([pp[c], ps_[16 * c:16 * c + 16].reshape(128, 256)], axis=0))
        m["s_ret"] = np.ascontiguousarray(sr[16 * c:16 * c + 16])
        m["s_wkv"] = np.ascontiguousarray(sw[16 * c:16 * c + 16])
        m["s_shift"] = np.ascontiguousarray(ss[16 * c:16 * c + 16])
        maps.append(m)
    return maps


def kernel(**inputs):
    if "prog" not in _CACHE:
        _CACHE["prog"] = build_program()
    nc, tabs, _ = _CACHE["prog"]
    maps = make_in_maps(inputs, tabs)
    res = run_bass_kernel_spmd(nc, maps, core_ids=list(range(NCORES)))
    R = res.results
    f = np.float32
    y_prompt = np.stack([R[c]["y"][0:2048] for c in range(NCORES)]).astype(f)
    y_sample = np.concatenate([R[c]["y"][2048:].reshape(16, 8, D) for c in range(NCORES)], axis=0).astype(f)
    ret_prompt = np.stack([R[c]["ret_p"] for c in range(NCORES)])[None].astype(f)
    wkv_prompt = np.stack([R[c]["wkv_p"] for c in range(NCORES)])[None].astype(f)
    shift_prompt = np.stack([R[c]["shift_p"] for c in range(NCORES)])[None].astype(f)
    ret_sample = np.concatenate([R[c]["ret_s"] for c in range(NCORES)], axis=0)[None].astype(f)
    wkv_sample = np.concatenate([R[c]["wkv_s"] for c in range(NCORES)], axis=0)[None].astype(f)
    shift_sample = np.concatenate([R[c]["shift_s"] for c in range(NCORES)], axis=0)[None].astype(f)
    return (y_prompt, y_sample, ret_prompt, wkv_prompt, shift_prompt, ret_sample, wkv_sample, shift_sample)
```

```python
import numpy as np
from contextlib import ExitStack
import concourse.bass as bass
import concourse.mybir as mybir
from concourse.bass_utils import run_bass_kernel_spmd

F32 = mybir.dt.float32
BF16 = mybir.dt.bfloat16
U32 = mybir.dt.uint32
AF = mybir.ActivationFunctionType
ALU = mybir.AluOpType
AX = mybir.AxisListType

NCORES = 8
NT = 17
TOK = NT * 128
D = 1024
RMS_EPS = 1e-6
GN_EPS = 1e-5
RWKV_LN_EPS = 64e-5
SAME_ENGINE_SYNC = True
DUM_A = 0
DUM_B = 0


class Prog:
    ENG = ("pe", "dve", "act", "pool", "sp")

    def __init__(self, nc, n_dma_sems=64):
        self.nc = nc
        self.lists = {k: [] for k in self.ENG}
        self.cnt = {k: 0 for k in self.ENG}
        self.waited = {k: {} for k in self.ENG}
        self.bufs = {}
        self.pending = {k: ([], []) for k in self.ENG}
        self.sems = {}
        self.n_dma_sems = n_dma_sems
        self.chan_sem = {}
        self.chan_cnt = {}
        self.chan_last = {}
        self.free_dma = list(range(n_dma_sems))
        self.out_events = []
        self.dead = False

    def alloc_sems(self, stack):
        for k in self.ENG:
            self.sems["prog_" + k] = stack.enter_context(self.nc.semaphore("prog_" + k))
        for i in range(self.n_dma_sems):
            self.sems["dma%d" % i] = stack.enter_context(self.nc.semaphore("dma%d" % i))

    def _chan(self, chan):
        if chan not in self.chan_sem:
            if not self.free_dma:
                raise RuntimeError("out of DMA semaphores")
            self.chan_sem[chan] = "dma%d" % self.free_dma.pop(0)
            self.chan_cnt[chan] = 0
        return self.chan_sem[chan]

    def _b(self, key):
        if key not in self.bufs:
            self.bufs[key] = {"w": None, "r": []}
        return self.bufs[key]

    def _deps(self, reads, writes):
        deps = []
        for k in reads:
            b = self._b(k)
            if b["w"] is not None:
                deps.append(b["w"])
        for k in writes:
            b = self._b(k)
            if b["w"] is not None:
                deps.append(b["w"])
            deps.extend(b["r"])
        return deps

    def _emit_waits(self, e, deps, ss=True):
        own = "prog_" + e
        for (s, v) in deps:
            if s == own and (e == "pe" or not SAME_ENGINE_SYNC):
                continue
            if self.waited[e].get(s, 0) < v:
                self.waited[e][s] = v
                self.lists[e].append(("wait", s, v))

    def _commit(self, ev, reads, writes):
        for k in reads:
            self._b(k)["r"].append(ev)
        for k in writes:
            self.bufs[k] = {"w": ev, "r": []}

    def op(self, e, fn, reads=(), writes=(), inc=True, ss=True):
        if self.dead:
            return None
        reads = list(reads); writes = list(writes)
        self._emit_waits(e, self._deps(reads, writes), ss)
        if not inc:
            self.pending[e][0].extend(reads)
            self.pending[e][1].extend(writes)
            self.lists[e].append(("op", fn, None))
            return None
        self.cnt[e] += 1
        ev = ("prog_" + e, self.cnt[e])
        self.lists[e].append(("op", fn, "prog_" + e))
        pr, pw = self.pending[e]
        self._commit(ev, reads + pr, writes + pw)
        self.pending[e] = ([], [])
        return ev

    def dma(self, q, fn, chan, reads=(), writes=(), is_output=False):
        if self.dead:
            return None
        reads = list(reads); writes = list(writes)
        sem = self._chan(chan)
        deps = self._deps(reads, writes)
        if chan in self.chan_last:
            deps.append(self.chan_last[chan])
        self._emit_waits(q, deps)
        self.chan_cnt[chan] += 16
        ev = (sem, self.chan_cnt[chan])
        self.chan_last[chan] = ev
        self.lists[q].append(("dma", fn, sem))
        self._commit(ev, reads, writes)
        if is_output:
            self.out_events.append(ev)
        return ev

    def barrier(self):
        deps = [ev for ev in self.chan_last.values()]
        for e in self.ENG:
            if self.cnt[e] > 0:
                deps.append(("prog_" + e, self.cnt[e]))
        for e in self.ENG:
            self._emit_waits(e, deps)

    def finish(self):
        deps = list(self.out_events) + [ev for ev in self.chan_last.values()]
        for e in self.ENG:
            if e != "sp" and self.cnt[e] > 0:
                deps.append(("prog_" + e, self.cnt[e]))
        self._emit_waits("sp", deps)

    def replay(self, e, engine):
        for item in self.lists[e]:
            if item[0] == "wait":
                engine.wait_ge(self.sems[item[1]], item[2])
            elif item[0] == "op":
                ins = item[1](engine)
                if item[2] is not None:
                    ins.then_inc(self.sems[item[2]], 1)
            else:
                ins = item[1](engine)
                ins.then_inc(self.sems[item[2]], 16)

    def run_block(self, block):
        p = self

        @block.sync
        def _(eng):
            p.replay("sp", eng)

        @block.tensor
        def _(eng):
            p.replay("pe", eng)

        @block.vector
        def _(eng):
            p.replay("dve", eng)

        @block.scalar
        def _(eng):
            p.replay("act", eng)

        @block.gpsimd
        def _(eng):
            p.replay("pool", eng)


def _const_tables():
    f = np.float32
    H = 8
    lg = np.log(f(1.0) - f(2.0) ** (-5.0 - np.arange(H, dtype=f))).astype(f)
    t = {}

    def decay(e):
        return np.exp(e[None].astype(f) * lg.reshape((H,) + (1,) * e.ndim)).astype(f)

    idx = np.arange(128, dtype=f)
    diff = idx[None, :] - idx[:, None]
    dp = np.where(diff >= 0, decay(np.maximum(diff, 0)), 0).astype(f)
    t["DT_p"] = np.ascontiguousarray(dp.transpose(1, 0, 2)).reshape(128, H * 128)
    b = np.arange(128) // 8
    l = (np.arange(128) % 8).astype(f)
    same = (b[:, None] == b[None, :])
    dl = l[None, :] - l[:, None]
    ds = np.where(same[None] & (dl[None] >= 0), decay(np.maximum(dl, 0)), 0).astype(f)
    t["DT_s"] = np.ascontiguousarray(ds.transpose(1, 0, 2)).reshape(128, H * 128)
    def xi_tab(e):
        x = decay(e)
        o = np.zeros((128, 4, 128), f)
        for g in range(4):
            for hh in range(2):
                o[hh * 64:(hh + 1) * 64, g, :] = x[2 * g + hh][None, :]
        return o.reshape(128, 512)
    t["XI_p"] = xi_tab(idx + 1.0)
    t["XI_s"] = xi_tab(l + 1.0)
    t["Z_p"] = (decay(127.0 - idx).T * f(0.125)).astype(f)
    t["Z_s"] = (decay(7.0 - l).T * f(0.125)).astype(f)
    def gc_tab(C):
        g_ = np.exp(f(C) * lg).astype(f)
        o = np.zeros((128, 4), f)
        for g in range(4):
            for hh in range(2):
                o[hh * 64:(hh + 1) * 64, g] = g_[2 * g + hh]
        return o
    t["GC_p"] = gc_tab(128.0)
    t["GC_s"] = gc_tab(8.0)
    t["Mk2"] = (b[:, None] == np.arange(16)[None, :]).astype(f)
    bm = np.zeros((128, 128), f)
    bm[:64, :64] = 1; bm[64:, 64:] = 1
    t["blockmask"] = bm
    hm = np.zeros((128, 2), f)
    hm[:64, 0] = 1; hm[64:, 1] = 1
    t["halfmask2"] = hm
    aw = np.zeros((128, 2, 255), f)
    aw[:64, 0, 127] = 1; aw[64:, 1, 127] = 1
    t["Awin"] = aw.reshape(128, 510)
    i2 = np.zeros((128, 64), f)
    i2[np.arange(128), np.arange(128) % 64] = 1
    t["I2"] = i2
    t["ident"] = np.eye(128, dtype=f)
    t["iota16"] = np.tile(np.arange(16, dtype=f)[None, :], (128, 1))
    half = 32
    inv = (f(10000.0) ** (-np.arange(half, dtype=f) / f(half))).astype(f)
    cs = np.zeros((NT, 128, 64), f)
    for n in range(NT):
        if n < 16:
            pos = (n * 128 + np.arange(128)).astype(f)
        else:
            pos = (16384 + (np.arange(128) % 8)).astype(f)
        ang = (pos[:, None] * inv[None, :]).astype(f)
        cs[n, :, :32] = np.cos(ang.astype(np.float64)).astype(f)
        cs[n, :, 32:] = np.sin(ang.astype(np.float64)).astype(f)
    t["cs"] = cs
    return t


CT_ORDER = ["DT_p", "DT_s", "XI_p", "XI_s", "Z_p", "Z_s", "GC_p", "GC_s", "Mk2", "blockmask",
            "halfmask2", "Awin", "I2", "ident", "iota16"]


def build_program(phases=("A", "B"), debug=(), nt_limit=NT):
    nc = bass.Bass("TRN2", target_bir_lowering=False)
    P = Prog(nc)
    tabs = _const_tables()
    ct_off = {}
    off = 0
    for k in CT_ORDER:
        ct_off[k] = (off, tabs[k].shape[1])
        off += tabs[k].shape[1]
    NCT = off

    def din(name, shape, dt=F32):
        return nc.dram_tensor(name, list(shape), dt, kind="ExternalInput").ap()

    def dout(name, shape, dt=F32):
        return nc.dram_tensor(name, list(shape), dt, kind="ExternalOutput").ap()

    x_all = din("x_all", [TOK, D])
    p_all = din("p_all", [TOK, 256])
    s_ret = din("s_ret", [16, 8, 64, 64])
    s_wkv = din("s_wkv", [16, 8, 64, 64])
    s_shift = din("s_shift", [16, 1792])
    w_in = din("w_in", [D, 3840])
    w_out = din("w_out", [D, D])
    peer_wq = din("peer_wq", [D, 2048])
    peer_keys = din("peer_keys", [16, 128, 128])
    peer_u = din("peer_u", [16384, D])
    peer_v = din("peer_v", [16384, D])
    ple_gate = din("ple_gate", [D, D])
    ple_proj = din("ple_proj", [256, D])
    rw_w2a2 = din("rw_w2a2", [128, 512])
    rw_g2 = din("rw_g2", [128, 512])
    cols_d = din("cols", [128, 64])
    rows_d = din("rows", [128, 3072])
    ctab_d = din("ctab", [128, NCT])
    cs_d = din("cs", [NT, 128, 64])

    y_o = dout("y", [TOK, D])
    retp_o = dout("ret_p", [8, 64, 64])
    wkvp_o = dout("wkv_p", [8, 64, 64])
    shiftp_o = dout("shift_p", [1792])
    rets_o = dout("ret_s", [16, 8, 64, 64])
    wkvs_o = dout("wkv_s", [16, 8, 64, 64])
    shifts_o = dout("shift_s", [16, 1792])
    h1_d = dout("h1_scr", [TOK, D])
    dbg_outs = {}

    with ExitStack() as st:
        ARENA_F32 = 53200
        arena = st.enter_context(nc.sbuf_tensor("arena", [128, ARENA_F32], F32))
        aoff = [0]

        def sb(name, shape, dt=F32):
            nel = 1
            for d_ in shape[1:]:
                nel *= d_
            esz = 4 if dt in (F32, U32) else 2
            n4 = ((nel * esz + 31) // 32) * 8
            o = aoff[0]
            aoff[0] += n4
            assert aoff[0] <= ARENA_F32, ("arena overflow", name, aoff[0])
            v = arena[0:shape[0], o:o + n4]
            if dt != F32:
                v = v.bitcast(dt)
            v = v[:, 0:nel]
            if len(shape) == 3:
                v = v.rearrange("p (a b) -> p a b", a=shape[1])
            elif len(shape) == 4:
                v = v.rearrange("p (a b c) -> p a b c", a=shape[1], b=shape[2])
            elif len(shape) == 5:
                v = v.rearrange("p (a b c d) -> p a b c d", a=shape[1], b=shape[2], c=shape[3])
            return v

        ps = st.enter_context(nc.psum_tensor("ps", [128, 4096], F32))

        def bank(b, lo=0, hi=512):
            return ps[:, b * 512 + lo:b * 512 + hi]

        def bk(b):
            return "ps%d" % b

        NDUM = [0]

        def dummies(k, bnk, src, src_key):
            for _ in range(k):
                P.op("pe", lambda e: e.matmul(bank(bnk), lhsT=src[:, 0:128], rhs=src[:, 0:512], start=True, stop=True),
                     reads=[src_key], writes=[bk(bnk)], inc=False)

        cols = sb("cols", [128, 64])
        ctab = sb("ctab", [128, NCT])
        C_GMIX, C_GFFN, C_GPLE, C_MU, C_W0, C_A0, C_KK, C_KA, C_RK, C_OMKA = 0, 8, 16, 24, 38, 42, 46, 50, 54, 58

        def ct(name):
            o, w = ct_off[name]
            return ctab[:, o:o + w]

        ident = ct("ident")
        P.alloc_sems(st)
        block = st.enter_context(nc.Block())

        def ckpt(name):
            if ("cut_" + name) in debug:
                P.dead = True

        def dbg(name, ap, shape, reads):
            if name not in debug:
                return
            o = dout("dbg_" + name, shape)
            dbg_outs[name] = o
            P.dma("sp", lambda e: e.dma_start(out=o, in_=ap), "dbg_" + name, reads=reads, is_output=True)

        P.dma("sp", lambda e: e.dma_start(out=cols[:], in_=cols_d), "cols", writes=["cols"])
        P.dma("sp", lambda e: e.dma_start(out=ctab[:], in_=ctab_d), "ctab", writes=["ctab"])
        P.op("dve", lambda e: e.tensor_scalar(out=cols[:, C_OMKA:C_OMKA + 4], in0=cols[:, C_KA:C_KA + 4],
                                              scalar1=-1.0, scalar2=1.0, op0=ALU.mult, op1=ALU.add),
             reads=["cols"], writes=["cols"])

        def bc(ap2, n, axis_last=True):
            k = ap2.shape[1]
            return ap2.unsqueeze(2).to_broadcast([ap2.shape[0], k, n])

        def rms_rstd(src, src_key, junk, junk_key, ssq, rstd, tag):
            P.op("act", lambda e: e.activation(out=junk, in_=src, func=AF.Square, accum_out=ssq),
                 reads=[src_key], writes=[junk_key, tag + "ssq"])
            P.op("act", lambda e: e.activation(out=rstd, in_=ssq, func=AF.Sqrt, scale=1.0 / D, bias=eps_t[:, 0:1]),
                 reads=[tag + "ssq", "eps"], writes=[tag + "rstd"])
            P.op("dve", lambda e: e.reciprocal(out=rstd, in_=rstd), reads=[tag + "rstd"], writes=[tag + "rstd"])

        def transposes8(src, src_key, dstT, dst_key, gcol_off, b0, b1):
            for hb, bnk in ((0, b0), (1, b1)):
                for c4 in range(4):
                    c = hb * 4 + c4
                    P.op("pe", lambda e, c=c, c4=c4, bnk=bnk: e.transpose(out=bank(bnk, c4 * 128, c4 * 128 + 128),
                                                                         in_=src[:, c * 128:(c + 1) * 128], identity=ident),
                         reads=[src_key, "ctab"], writes=[bk(bnk)], inc=(c4 == 3))
                if gcol_off is None:
                    P.op("act", lambda e, hb=hb, bnk=bnk: e.copy(out=dstT[:, hb * 4:hb * 4 + 4, :],
                                                                 in_=bank(bnk).rearrange("p (c t) -> p c t", c=4)),
                         reads=[bk(bnk)], writes=[dst_key])
                else:
                    P.op("dve", lambda e, hb=hb, bnk=bnk: e.tensor_tensor(
                        out=dstT[:, hb * 4:hb * 4 + 4, :], in0=bank(bnk).rearrange("p (c t) -> p c t", c=4),
                        in1=bc(cols[:, gcol_off + hb * 4:gcol_off + hb * 4 + 4], 128), op=ALU.mult),
                        reads=[bk(bnk), "cols"], writes=[dst_key])

        def head_norm(src3, src_key, xc, xc_key, eps_ap, tag):
            P.op("dve", lambda e: e.tensor_reduce(out=hn_m[:], in_=src3, axis=AX.X, op=ALU.add),
                 reads=[src_key], writes=["hn_m"])
            P.op("dve", lambda e: e.tensor_scalar(out=hn_m[:], in0=hn_m[:], scalar1=-1.0 / 64, scalar2=None, op0=ALU.mult),
                 reads=["hn_m"], writes=["hn_m"])
            P.op("dve", lambda e: e.tensor_tensor(out=xc, in0=src3, in1=bc(hn_m[:], 64), op=ALU.add),
                 reads=[src_key, "hn_m"], writes=[xc_key])
            P.op("act", lambda e: e.activation(out=hn_sq[:], in_=xc.rearrange("p h d -> p (h d)"), func=AF.Square),
                 reads=[xc_key], writes=["hn_sq"])
            P.op("dve", lambda e: e.tensor_reduce(out=hn_v[:], in_=hn_sq[:].rearrange("p (h d) -> p h d", h=8), axis=AX.X, op=ALU.add),
                 reads=["hn_sq"], writes=["hn_v"])
            P.op("act", lambda e: e.activation(out=hn_v[:], in_=hn_v[:], func=AF.Sqrt, scale=1.0 / 64, bias=eps_ap),
                 reads=["hn_v", "eps"], writes=["hn_v"])
            P.op("dve", lambda e: e.reciprocal(out=hn_v[:], in_=hn_v[:]), reads=["hn_v"], writes=["hn_v"])
            P.op("dve", lambda e: e.tensor_tensor(out=xc, in0=xc, in1=bc(hn_v[:], 64), op=ALU.mult),
                 reads=[xc_key, "hn_v"], writes=[xc_key])

        eps_t = sb("eps_t", [128, 4])
        P.op("pool", lambda e: e.memset(eps_t[:, 0:1], RMS_EPS), writes=["eps"], inc=False)
        P.op("pool", lambda e: e.memset(eps_t[:, 1:2], GN_EPS), writes=["eps"], inc=False)
        P.op("pool", lambda e: e.memset(eps_t[:, 2:3], RWKV_LN_EPS), writes=["eps"], inc=False)
        P.op("pool", lambda e: e.memset(eps_t[:, 3:4], 0.0), writes=["eps"])
        hn_m = sb("hn_m", [128, 8])
        hn_v = sb("hn_v", [128, 8])
        hn_sq = sb("hn_sq", [128, 512])
        ssq = sb("ssq", [128, 1])
        rstd = sb("rstd", [128, 1])
        junk = sb("junk", [128, 1024])

        if "A" in phases:
            a_mark = aoff[0]
            sa_ = sb

            win_bf = sa_("win_bf", [128, 8, 3840], BF16)
            wout_bf = sa_("wout_bf", [128, 8, 1024], BF16)
            w2a2 = sa_("w2a2", [128, 512])
            g2 = sa_("g2", [128, 512])
            rowsA = sa_("rowsA", [128, 1024])
            w_in_v = w_in.rearrange("(c p) n -> p c n", p=128)
            for c in range(8):
                for hf in range(2):
                    P.dma("pool", lambda e, c=c, hf=hf: e.dma_start(out=win_bf[:, c, hf * 1920:(hf + 1) * 1920],
                                                                    in_=w_in_v[:, c, hf * 1920:(hf + 1) * 1920]),
                          "wload%d" % (c % 4), writes=["win_bf"])
            w_out_v = w_out.rearrange("(c p) n -> p c n", p=128)
            for c in range(8):
                P.dma("pool", lambda e, c=c: e.dma_start(out=wout_bf[:, c, :], in_=w_out_v[:, c, :]),
                      "wload%d" % (c % 4), writes=["wout_bf"])
            P.dma("act", lambda e: e.dma_start(out=w2a2[:], in_=rw_w2a2), "w2a2", writes=["w2a2"])
            P.dma("act", lambda e: e.dma_start(out=g2[:], in_=rw_g2), "g2", writes=["g2"])
            P.dma("act", lambda e: e.dma_start(out=rowsA[:], in_=rows_d[:, 0:1024]), "rowsA", writes=["rowsA"])

            xt = sa_("xt", [128, 1024])
            xs = sa_("xs", [128, 1024])
            xnT = sa_("xnT", [128, 8, 128], BF16)
            cst = sa_("cst", [128, 64])
            qkr = sa_("qkr", [128, 16, 64])
            rt1 = sa_("rt1", [128, 16, 32])
            rt2 = sa_("rt2", [128, 16, 32])
            vtb = sa_("vtb", [128, 512], BF16)
            sgl = sa_("sgl", [128, 512])
            kz = sa_("kz", [128, 8, 64], BF16)
            qbd = sa_("qbd", [128, 4, 2, 128], BF16)
            kT = sa_("kT", [128, 4, 128], BF16)
            qxT = sa_("qxT", [128, 4, 128], BF16)
            PT = sa_("PT", [128, 8, 128], BF16)
            Sst = sa_("Sst", [128, 4, 64])
            Sbd = sa_("Sbd", [128, 4, 2, 64], BF16)
            S0b = sa_("S0b", [128, 2, 4, 64])
            S0bd = sa_("S0bd", [128, 2, 4, 2, 64], BF16)
            kzm = sa_("kzm", [128, 2, 512], BF16)
            xc = sa_("xc", [128, 8, 64])
            oall = sa_("oall", [128, 1024])
            rw = sa_("rw", [128, 14, 144])
            fm = sa_("fm", [128, 14, 128])
            th = sa_("th", [128, 128])
            sgm = sa_("sgm", [128, 4, 128])
            ew = sa_("ew", [128, 4, 128])
            aa = sa_("aa", [128, 4, 128])
            sigfg = sa_("sigfg", [128, 128])
            gate = sa_("gate", [128, 512])
            kk = sa_("kk", [128, 4, 128])
            nrm = sa_("nrm", [128, 4, 128])
            kp = sa_("kp", [128, 4, 128])
            nkka = sa_("nkka", [128, 4, 128])
            prk = sa_("prk", [128, 4, 128])
            sqk = prk
            bonus = sa_("bonus", [128, 8])
            vtok = sa_("vtok", [128, 512])
            ST = sa_("ST", [128, 256])
            NR = 2
            RH = [sa_("RH%d" % i, [128, 256], BF16) for i in range(NR)]
            T2 = [sa_("T2_%d" % i, [128, 256], BF16) for i in range(NR)]
            DV = [sa_("DV%d" % i, [128, 2, 4, 256], BF16) for i in range(2)]
            Qr = [sa_("Qr%d" % i, [128, 256]) for i in range(NR)]
            Pr = [sa_("Pr%d" % i, [128, 256]) for i in range(NR)]
            Mm = sa_("Mm", [128, 256])
            vhi = sa_("vhi", [128, 4, 128], BF16)
            vlo = sa_("vlo", [128, 4, 128], BF16)
            I2h = sa_("I2h", [128, 64], BF16)
            bmh = sa_("bmh", [128, 128], BF16)
            Awh = sa_("Awh", [128, 2, 255], BF16)
            Xw = sa_("Xw", [64, 8, 64])
            shst = Xw[0:16].rearrange("p h j -> p (h j)")
            xc2 = xc
            oT = sa_("oT", [128, 8, 128], BF16)
            h1t = xs

            P.op("pool", lambda e: e.memset(Sst[:], 0.0), writes=["Sst"])
            P.op("pool", lambda e: e.memset(Sbd[:], 0.0), writes=["Sbd"])
            P.op("pool", lambda e: e.memset(qbd[:], 0.0), writes=["qbd"])
            P.op("pool", lambda e: e.memset(S0bd[:], 0.0), writes=["S0bd0", "S0bd1"])
            P.op("pool", lambda e: e.memset(ST[:], 0.0), writes=["ST0", "ST1", "ST2", "ST3"])
            P.op("pool", lambda e: e.memset(rw[:], 0.0), writes=["rw"])

            ST3 = ST[:].rearrange("p (g i) -> p g i", g=4)
            P.op("act", lambda e: e.copy(out=I2h[:], in_=ct("I2")), reads=["ctab"], writes=["I2h"])
            P.op("act", lambda e: e.copy(out=bmh[:], in_=ct("blockmask")), reads=["ctab"], writes=["bmh"])
            P.op("act", lambda e: e.copy(out=Awh[:].rearrange("p h w -> p (h w)"), in_=ct("Awin")), reads=["ctab"], writes=["Awh"])
            I2hb = I2h[:].unsqueeze(1).to_broadcast([128, 4, 64])
            blockmask = ct("blockmask")
            Aw = ct("Awin").rearrange("p (h w) -> p h w", h=2)

            def colb(t_, tl):
                return t_[:, :, tl:tl + 1].to_broadcast([128, 4, 64])

            def wkv_state_in(b):
                P.dma("sp", lambda e: e.dma_start(out=Xw[:], in_=s_wkv[b].rearrange("h i j -> i h j")), "Xw",
                      writes=["Xw"])
                for g in range(4):
                    P.op("pe", lambda e, g=g: e.transpose(out=bank(7, g * 64, g * 64 + 64),
                                                          in_=Xw[:, 2 * g:2 * g + 2, :].rearrange("i h j -> i (h j)"),
                                                          identity=ident[0:64, 0:64]),
                         reads=["Xw", "ctab"], writes=[bk(7)], inc=(g == 3))
                P.op("act", lambda e: e.copy(out=ST[:], in_=bank(7, 0, 256)), reads=[bk(7)], writes=["ST0", "ST1", "ST2", "ST3"])

            def wkv_state_out(dst):
                for g in range(4):
                    P.op("pe", lambda e, g=g: e.transpose(out=ps[0:64, 7 * 512 + g * 128:7 * 512 + g * 128 + 128],
                                                          in_=ST[:, g * 64:(g + 1) * 64], identity=ident),
                         reads=["ST0", "ST1", "ST2", "ST3", "ctab"], writes=[bk(7)], inc=(g == 3))
                P.op("act", lambda e: e.copy(out=Xw[:].rearrange("i h j -> i (h j)"), in_=ps[0:64, 7 * 512:7 * 512 + 512]),
                     reads=[bk(7)], writes=["Xw"])
                P.dma("sp", lambda e: e.dma_start(out=dst.rearrange("h i j -> i h j"), in_=Xw[:]), "Xwo",
                      reads=["Xw"], is_output=True)

            def scan_pre(t, r):
                vb = 2 + (t % 2)
                sl = t % 2
                for g in range(4):
                    P.op("act", lambda e, g=g: e.activation(out=DV[sl][:, 0, 0, g * 64:(g + 1) * 64], in_=I2h[:],
                                                            func=AF.Copy, scale=fm[:, 8 + g, t:t + 1]),
                         reads=["fm", "I2h"], writes=["DV%d" % sl], inc=(g == 3))
                P.op("pe", lambda e: e.matmul(bank(vb, 0, 256), lhsT=bmh[:], rhs=DV[sl][:, 0, 0, :], start=True, stop=True),
                     reads=["DV%d" % sl, "bmh"], writes=[bk(vb)])

            def scan_y(t, r):
                t2 = T2[r]
                P.op("dve", lambda e: e.tensor_tensor(out=t2[:].rearrange("p (g i) -> p g i", g=4), in0=ST3,
                                                      in1=colb(fm[:, 0:4, :], t), op=ALU.mult),
                     reads=["ST0", "ST1", "ST2", "ST3", "fm"], writes=["T2_%d" % r])
                for hh in range(2):
                    P.op("pe", lambda e, hh=hh: e.matmul(
                        bank(6, hh * 256, hh * 256 + 256),
                        lhsT=Awh[:, hh, 127 - t:255 - t], rhs=t2[:], start=(t == 0 and hh == 0), stop=(t == 127 and hh == 1),
                        skip_group_check=True),
                        reads=["T2_%d" % r, "Awh"], writes=[bk(6)], inc=(hh == 1))

            def scan_step(t, r, prev_t):
                rh = RH[r]
                pb = 4 + (t % 2)
                scan_pre(t, r)
                P.op("dve", lambda e: e.tensor_tensor(out=rh[:].rearrange("p (g i) -> p g i", g=4), in0=ST3,
                                                      in1=colb(kk, t), op=ALU.mult),
                     reads=["ST0", "ST1", "ST2", "ST3", "kk"], writes=["RH%d" % r])
                if prev_t is not None:
                    scan_y(prev_t, prev_t % NR)
                P.op("pe", lambda e: e.matmul(bank(pb, 0, 256), lhsT=bmh[:], rhs=rh[:], start=True, stop=True),
                     reads=["RH%d" % r, "bmh"], writes=[bk(pb)])
                dummies(DUM_A, 7, wout_bf[:, 0, :], "wout_bf")
                vb = 2 + (t % 2)
                P.op("pool", lambda e: e.tensor_tensor(out=Pr[r][:].rearrange("p (g i) -> p g i", g=4), in0=ST3,
                                                       in1=colb(ew, t), op=ALU.mult),
                     reads=["ST0", "ST1", "ST2", "ST3", "ew"], writes=["P%d" % r])
                for g in range(4):
                    gs_ = slice(g * 64, (g + 1) * 64)
                    P.op("dve", lambda e, g=g, gs_=gs_: e.scalar_tensor_tensor(
                        out=Qr[r][:, gs_], in0=bank(vb, g * 64, g * 64 + 64), scalar=kp[:, g, t:t + 1], in1=Pr[r][:, gs_],
                        op0=ALU.mult, op1=ALU.add),
                        reads=[bk(vb), "kp", "P%d" % r], writes=["Q%d_%d" % (r, g)])
                for g in range(4):
                    gs_ = slice(g * 64, (g + 1) * 64)
                    P.op("dve", lambda e, g=g, gs_=gs_: e.scalar_tensor_tensor(
                        out=ST[:, gs_], in0=bank(pb, g * 64, g * 64 + 64), scalar=nkka[:, g, t:t + 1], in1=Qr[r][:, gs_],
                        op0=ALU.mult, op1=ALU.add),
                        reads=[bk(pb), "nkka", "Q%d_%d" % (r, g)], writes=["ST%d" % g])

            for n in range(min(NT, nt_limit)):
                smp = (n == 16)
                sfx = "_s" if smp else "_p"
                NB, L = (16, 8) if smp else (1, 128)
                r0, r1 = n * 128, (n + 1) * 128
                P.dma("sp", lambda e, r0=r0, r1=r1: e.dma_start(out=xt[:], in_=x_all[r0:r1, :]), "xt", writes=["xt"])
                P.dma("sp", lambda e, n=n: e.dma_start(out=cst[:], in_=cs_d[n]), "cst", writes=["cst"])
                rms_rstd(xt[:], "xt", junk[:], "junk", ssq[:], rstd[:], "a")
                P.op("dve", lambda e: e.tensor_scalar(out=xs[:], in0=xt[:], scalar1=rstd[:, 0:1], scalar2=None, op0=ALU.mult),
                     reads=["xt", "arstd"], writes=["xs"])
                transposes8(xs, "xs", xnT, "xnT", C_GMIX, 6, 7)
                ckpt("a1")
                for blk in range(4):
                    for c in range(8):
                        P.op("pe", lambda e, blk=blk, c=c: e.matmul(bank(blk), lhsT=xnT[:, c, :],
                                                                    rhs=win_bf[:, c, blk * 512:(blk + 1) * 512],
                                                                    start=(c == 0), stop=(c == 7)),
                             reads=["xnT", "win_bf"], writes=[bk(blk)], inc=(c == 7))
                if n > 0 and not smp:
                    P.op("pool", lambda e: e.tensor_copy(out=rw[:, :, 0:1], in_=rw[:, :, 128:129]), reads=["rw"], writes=["rw"])
                if smp:
                    ssh = fm[0:16, :, :].rearrange("p c t -> p (c t)")
                    P.dma("sp", lambda e: e.dma_start(out=ssh, in_=s_shift), "ssh", writes=["fm"])
                    for c in range(14):
                        P.op("pe", lambda e, c=c: e.transpose(out=bank(6, c * 16, c * 16 + 16), in_=ssh[:, c * 128:(c + 1) * 128],
                                                              identity=ident[0:16, 0:16]),
                             reads=["fm", "ctab"], writes=[bk(6)], inc=(c == 13))
                    P.op("act", lambda e: e.copy(out=rw[:].rearrange("p c (b l) -> p c b l", l=9)[:, :, :, 0:1],
                                                 in_=bank(6, 0, 224).rearrange("p (c b o) -> p c b o", c=14, o=1)),
                         reads=[bk(6)], writes=["rw"])
                for cg in range(4):
                    bnk = 4 + cg % 2
                    ncs = 4 if cg < 3 else 2
                    for c4 in range(ncs):
                        c = cg * 4 + c4
                        for dc in range(8):
                            P.op("pe", lambda e, c=c, c4=c4, dc=dc, bnk=bnk: e.matmul(
                                bank(bnk, c4 * 128, c4 * 128 + 128), lhsT=win_bf[:, dc, 2048 + c * 128:2048 + (c + 1) * 128],
                                rhs=xnT[:, dc, :], start=(dc == 0), stop=(dc == 7)),
                                reads=["xnT", "win_bf"], writes=[bk(bnk)], inc=(dc == 7 and c4 == ncs - 1))
                    if smp:
                        P.op("act", lambda e, cg=cg, ncs=ncs, bnk=bnk: e.copy(
                            out=rw[:, cg * 4:cg * 4 + ncs, :].rearrange("p c (b l) -> p c b l", l=9)[:, :, :, 1:9],
                            in_=bank(bnk, 0, ncs * 128).rearrange("p (c b l) -> p c b l", c=ncs, l=8)),
                            reads=[bk(bnk)], writes=["rw"])
                    else:
                        P.op("act", lambda e, cg=cg, ncs=ncs, bnk=bnk: e.copy(
                            out=rw[:, cg * 4:cg * 4 + ncs, 1:129],
                            in_=bank(bnk, 0, ncs * 128).rearrange("p (c t) -> p c t", c=ncs)),
                            reads=[bk(bnk)], writes=["rw"])
                ckpt("a2")
                qk3 = ps[:, 0:1024].rearrange("p (h d) -> p h d", d=64)
                cosb = cst[:, 0:32].unsqueeze(1).to_broadcast([128, 16, 32])
                sinb = cst[:, 32:64].unsqueeze(1).to_broadcast([128, 16, 32])
                P.op("dve", lambda e: e.tensor_tensor(out=rt1[:], in0=qk3[:, :, 0:32], in1=cosb, op=ALU.mult),
                     reads=[bk(0), bk(1), "cst"], writes=["rt1"])
                P.op("dve", lambda e: e.tensor_tensor(out=rt2[:], in0=qk3[:, :, 32:64], in1=sinb, op=ALU.mult),
                     reads=[bk(0), bk(1), "cst"], writes=["rt2"])
                P.op("pool", lambda e: e.tensor_tensor(out=qkr[:, :, 0:32], in0=rt1[:], in1=rt2[:], op=ALU.subtract),
                     reads=["rt1", "rt2"], writes=["qkr"])
                P.op("dve", lambda e: e.tensor_tensor(out=rt1[:], in0=qk3[:, :, 32:64], in1=cosb, op=ALU.mult),
                     reads=[bk(0), bk(1), "cst"], writes=["rt1"])
                P.op("dve", lambda e: e.tensor_tensor(out=rt2[:], in0=qk3[:, :, 0:32], in1=sinb, op=ALU.mult),
                     reads=[bk(0), bk(1), "cst"], writes=["rt2"])
                P.op("pool", lambda e: e.tensor_tensor(out=qkr[:, :, 32:64], in0=rt1[:], in1=rt2[:], op=ALU.add),
                     reads=["rt1", "rt2"], writes=["qkr"])
                P.op("act", lambda e: e.copy(out=vtb[:], in_=bank(2)), reads=[bk(2)], writes=["vtb"])
                P.op("act", lambda e: e.activation(out=sgl[:], in_=bank(3), func=AF.Silu), reads=[bk(3)], writes=["sgl"])
                Zt = ct("Z" + sfx)
                P.op("dve", lambda e, Zt=Zt: e.tensor_tensor(out=kz[:], in0=qkr[:, 8:16, :], in1=bc(Zt, 64), op=ALU.mult),
                     reads=["qkr", "ctab"], writes=["kz"])
                ckpt("a3")
                qkr2 = qkr[:].rearrange("p (a b) d -> p a (b d)", b=2)
                for pr in range(8):
                    bnk = 2 + pr // 4
                    P.op("pe", lambda e, pr=pr, bnk=bnk: e.transpose(out=bank(bnk, (pr % 4) * 128, (pr % 4) * 128 + 128),
                                                                    in_=qkr2[:, pr, :], identity=ident),
                         reads=["qkr", "ctab"], writes=[bk(bnk)], inc=(pr % 4 == 3))
                for hh in range(2):
                    P.op("act", lambda e, hh=hh: e.copy(out=qbd[hh * 64:(hh + 1) * 64, :, hh, :],
                                                        in_=ps[hh * 64:(hh + 1) * 64, 1024:1536].rearrange("p (g t) -> p g t", g=4)),
                         reads=[bk(2)], writes=["qbd"])
                P.op("act", lambda e: e.activation(out=kT[:].rearrange("p g t -> p (g t)"), in_=bank(3), func=AF.Copy, scale=0.125),
                     reads=[bk(3)], writes=["kT"])
                XIt = ct("XI" + sfx)
                P.op("dve", lambda e, XIt=XIt: e.tensor_tensor(out=qxT[:].rearrange("p g t -> p (g t)"), in0=bank(2), in1=XIt, op=ALU.mult),
                     reads=[bk(2), "ctab"], writes=["qxT"])
                ckpt("a4")
                for g in range(4):
                    bnk = g // 2
                    P.op("pe", lambda e, g=g, bnk=bnk: e.matmul(
                        bank(bnk, (g % 2) * 256, (g % 2) * 256 + 256), lhsT=kT[:, g, :],
                        rhs=qbd[:, g, :, :].rearrange("p a t -> p (a t)"), start=True, stop=True),
                        reads=["kT", "qbd"], writes=[bk(bnk)], inc=(g % 2 == 1))
                DTt = ct("DT" + sfx)
                for hb in range(2):
                    P.op("dve", lambda e, hb=hb, DTt=DTt: e.tensor_tensor(
                        out=PT[:, hb * 4:hb * 4 + 4, :].rearrange("p h t -> p (h t)"), in0=bank(hb),
                        in1=DTt[:, hb * 512:(hb + 1) * 512], op=ALU.mult),
                        reads=[bk(hb), "ctab"], writes=["PT"])
                GCt = ct("GC" + sfx)
                ckpt("a5")

                def state_upd(dst, dst_key, lhs_of_g, lhs_key, ub, GCt=GCt):
                    for g in range(4):
                        P.op("pe", lambda e, g=g: e.matmul(bank(ub, g * 128, g * 128 + 128), lhsT=lhs_of_g(g),
                                                           rhs=vtb[:, g * 128:(g + 1) * 128], start=True, stop=True),
                             reads=[lhs_key, "vtb"], writes=[bk(ub)], inc=(g == 3))
                    P.op("dve", lambda e: e.tensor_tensor(out=dst, in0=dst, in1=bc(GCt, 64), op=ALU.mult),
                         reads=[dst_key, "ctab"], writes=[dst_key])
                    for hh in range(2):
                        P.op("dve", lambda e, hh=hh: e.tensor_tensor(
                            out=dst[hh * 64:(hh + 1) * 64], in0=dst[hh * 64:(hh + 1) * 64],
                            in1=ps[hh * 64:(hh + 1) * 64, ub * 512:(ub + 1) * 512].rearrange("p (g x) -> p g x", g=4)[:, :, hh * 64:hh * 64 + 64],
                            op=ALU.add),
                            reads=[dst_key, bk(ub)], writes=[dst_key])

                if not smp:
                    for g in range(4):
                        for hh in range(2):
                            h = 2 * g + hh
                            P.op("pe", lambda e, h=h, hh=hh: e.matmul(bank(2, h * 64, h * 64 + 64), lhsT=PT[:, h, :],
                                                                      rhs=vtb[:, h * 64:(h + 1) * 64], start=(hh == 0), stop=False,
                                                                      skip_group_check=True),
                                 reads=["PT", "vtb"], writes=[bk(2)], inc=False)
                        P.op("pe", lambda e, g=g: e.matmul(bank(2, g * 128, g * 128 + 128), lhsT=qxT[:, g, :],
                                                           rhs=Sbd[:, g, :, :].rearrange("p a v -> p (a v)"), start=False, stop=True,
                                                           skip_group_check=True),
                             reads=["qxT", "Sbd"], writes=[bk(2)], inc=(g == 3))
                    state_upd(Sst[:], "Sst", lambda g: kz[:, 2 * g:2 * g + 2, :].rearrange("p a d -> p (a d)"), "kz", 3)
                    for hh in range(2):
                        P.op("act", lambda e, hh=hh: e.copy(out=Sbd[hh * 64:(hh + 1) * 64, :, hh, :], in_=Sst[hh * 64:(hh + 1) * 64, :, :]),
                             reads=["Sst"], writes=["Sbd"])
                    if n == 15:
                        for hh in range(2):
                            P.dma("sp", lambda e, hh=hh: e.dma_start(
                                out=retp_o.rearrange("(g hh) d v -> hh d g v", hh=2)[hh], in_=Sst[hh * 64:(hh + 1) * 64, :, :]),
                                "retp", reads=["Sst"], is_output=True)
                    head_norm(bank(2).rearrange("p (h d) -> p h d", h=8), bk(2), xc[:], "xc", eps_t[:, 1:2], "r")
                else:
                    for h in range(8):
                        P.op("pe", lambda e, h=h: e.matmul(bank(2, h * 64, h * 64 + 64), lhsT=PT[:, h, :],
                                                           rhs=vtb[:, h * 64:(h + 1) * 64], start=True, stop=True),
                             reads=["PT", "vtb"], writes=[bk(2)], inc=(h == 7))
                    kzf = kz[:].rearrange("p h d -> p (h d)")
                    for b in range(16):
                        sl = b % 2
                        for hh in range(2):
                            P.dma("sp", lambda e, b=b, hh=hh, sl=sl: e.dma_start(
                                out=S0b[hh * 64:(hh + 1) * 64, sl, :, :],
                                in_=s_ret[b].rearrange("(g hh) d v -> hh d g v", hh=2)[hh]),
                                "S0b%d" % sl, writes=["S0b%d" % sl])
                        for hh in range(2):
                            P.op("act", lambda e, sl=sl, hh=hh: e.copy(out=S0bd[hh * 64:(hh + 1) * 64, sl, :, hh, :],
                                                                       in_=S0b[hh * 64:(hh + 1) * 64, sl, :, :]),
                                 reads=["S0b%d" % sl], writes=["S0bd%d" % sl])
                        for g in range(4):
                            P.op("pe", lambda e, g=g, b=b, sl=sl: e.matmul(
                                bank(3, g * 128 + b * 8, g * 128 + b * 8 + 8), lhsT=S0bd[:, sl, g, :, :].rearrange("p a v -> p (a v)"),
                                rhs=qxT[:, g, b * 8:b * 8 + 8], start=True, stop=True),
                                reads=["S0bd%d" % sl, "qxT"], writes=[bk(3)], inc=(g == 3))
                        P.op("dve", lambda e, b=b, sl=sl: e.tensor_scalar(out=kzm[:, sl, :], in0=kzf, scalar1=ct("Mk2")[:, b:b + 1],
                                                                          scalar2=None, op0=ALU.mult),
                             reads=["kz", "ctab"], writes=["kzm%d" % sl])
                        state_upd(S0b[:, sl], "S0b%d" % sl, lambda g, sl=sl: kzm[:, sl, g * 128:(g + 1) * 128], "kzm%d" % sl, sl)
                        for hh in range(2):
                            P.dma("sp", lambda e, b=b, hh=hh, sl=sl: e.dma_start(
                                out=rets_o[b].rearrange("(g hh) d v -> hh d g v", hh=2)[hh],
                                in_=S0b[hh * 64:(hh + 1) * 64, sl, :, :]),
                                "S0o%d" % sl, reads=["S0b%d" % sl], is_output=True)
                    P.op("act", lambda e: e.copy(out=junk[:, 0:512], in_=bank(3)), reads=[bk(3)], writes=["junk"])
                    for g in range(4):
                        P.op("pe", lambda e, g=g: e.transpose(out=bank(3, g * 128, g * 128 + 128), in_=junk[:, g * 128:(g + 1) * 128],
                                                              identity=ident),
                             reads=["junk", "ctab"], writes=[bk(3)], inc=(g == 3))
                    P.op("act", lambda e: e.copy(out=junk[:, 512:1024], in_=bank(3)), reads=[bk(3)], writes=["junk"])
                    P.op("dve", lambda e: e.tensor_tensor(out=xc[:].rearrange("p h d -> p (h d)"), in0=bank(2), in1=junk[:, 512:1024], op=ALU.add),
                         reads=[bk(2), "junk"], writes=["xc"])
                    head_norm(xc[:], "xc", xc[:], "xc", eps_t[:, 1:2], "r")
                P.op("dve", lambda e: e.tensor_tensor(out=oall[:, 0:512], in0=xc[:].rearrange("p h d -> p (h d)"), in1=sgl[:], op=ALU.mult),
                     reads=["xc", "sgl"], writes=["oall_r"])

                ckpt("a6")
                if smp:
                    rwv = rw[:].rearrange("p c (b l) -> p c b l", l=9)
                    prev, cur = rwv[:, :, :, 0:8], rwv[:, :, :, 1:9]
                    fmv = fm[:].rearrange("p c (b l) -> p c b l", l=8)
                    mub = cols[:, C_MU:C_MU + 14].unsqueeze(2).unsqueeze(3).to_broadcast([128, 14, 16, 8])
                else:
                    prev, cur = rw[:, :, 0:128], rw[:, :, 1:129]
                    fmv = fm[:]
                    mub = bc(cols[:, C_MU:C_MU + 14], 128)
                P.op("dve", lambda e, prev=prev, cur=cur, fmv=fmv: e.tensor_tensor(out=fmv, in0=prev, in1=cur, op=ALU.subtract),
                     reads=["rw"], writes=["fm"])
                P.op("dve", lambda e, fmv=fmv, mub=mub: e.tensor_tensor(out=fmv, in0=fmv, in1=mub, op=ALU.mult),
                     reads=["fm", "cols"], writes=["fm"])
                P.op("dve", lambda e, fmv=fmv, cur=cur: e.tensor_tensor(out=fmv, in0=fmv, in1=cur, op=ALU.add),
                     reads=["fm", "rw"], writes=["fm"])
                if n == 15:
                    P.op("pe", lambda e: e.transpose(out=ps[0:14, 7 * 512:7 * 512 + 128], in_=rw[:, :, 128], identity=ident),
                         reads=["rw", "ctab"], writes=[bk(7)])
                    P.op("act", lambda e: e.copy(out=shst[0:14, 0:128], in_=ps[0:14, 7 * 512:7 * 512 + 128]), reads=[bk(7)], writes=["Xw"])
                    P.dma("sp", lambda e: e.dma_start(out=shiftp_o.rearrange("(c p) -> c p", p=128), in_=shst[0:14, 0:128]),
                          "shp", reads=["Xw"], is_output=True)
                if smp:
                    rwl = rw[:].rearrange("p c (b l) -> p c b l", l=9)
                    for cg in range(4):
                        ncs = 4 if cg < 3 else 2
                        for c4 in range(ncs):
                            c = cg * 4 + c4
                            P.op("pe", lambda e, c=c, c4=c4: e.transpose(out=ps[0:16, 7 * 512 + c4 * 128:7 * 512 + c4 * 128 + 128],
                                                                        in_=rwl[:, c, :, 8], identity=ident),
                                 reads=["rw", "ctab"], writes=[bk(7)], inc=(c4 == ncs - 1))
                        P.op("act", lambda e, ncs=ncs: e.copy(out=shst[0:16, 0:ncs * 128], in_=ps[0:16, 7 * 512:7 * 512 + ncs * 128]),
                             reads=[bk(7)], writes=["Xw"])
                        P.dma("sp", lambda e, cg=cg, ncs=ncs: e.dma_start(out=shifts_o[:, cg * 512:cg * 512 + ncs * 128], in_=shst[0:16, 0:ncs * 128]),
                              "shs", reads=["Xw"], is_output=True)
                P.op("act", lambda e: e.activation(out=th[0:64, :], in_=fm[0:64, 12, :], func=AF.Tanh), reads=["fm"], writes=["th"])
                for g in range(4):
                    P.op("pe", lambda e, g=g: e.matmul(bank(0, g * 128, g * 128 + 128), lhsT=w2a2[0:64, g * 128:(g + 1) * 128],
                                                       rhs=th[0:64, :], start=True, stop=True),
                         reads=["w2a2", "th"], writes=[bk(0)], inc=(g == 3))
                for g in range(4):
                    P.op("pe", lambda e, g=g: e.matmul(bank(1, g * 128, g * 128 + 128), lhsT=w2a2[64:128, g * 128:(g + 1) * 128],
                                                       rhs=fm[64:128, 12, :], start=True, stop=True),
                         reads=["w2a2", "fm"], writes=[bk(1)], inc=(g == 3))
                for g in range(4):
                    P.op("act", lambda e, g=g: e.activation(out=sgm[:, g, :], in_=bank(0, g * 128, g * 128 + 128), func=AF.Sigmoid,
                                                            bias=cols[:, C_W0 + g:C_W0 + g + 1]),
                         reads=[bk(0), "cols"], writes=["sgm"])
                P.op("act", lambda e: e.activation(out=ew[:], in_=sgm[:], func=AF.Exp, scale=-0.6065306597126334),
                     reads=["sgm"], writes=["ew"])
                for g in range(4):
                    P.op("act", lambda e, g=g: e.activation(out=aa[:, g, :], in_=bank(1, g * 128, g * 128 + 128), func=AF.Sigmoid,
                                                            bias=cols[:, C_A0 + g:C_A0 + g + 1]),
                         reads=[bk(1), "cols"], writes=["aa"])
                P.op("act", lambda e: e.activation(out=sigfg[:], in_=fm[:, 13, :], func=AF.Sigmoid), reads=["fm"], writes=["sigfg"])
                P.op("pe", lambda e: e.matmul(bank(0), lhsT=sigfg[:], rhs=g2[:], start=True, stop=True),
                     reads=["sigfg", "g2"], writes=[bk(0)])
                P.op("act", lambda e: e.copy(out=gate[:], in_=bank(0)), reads=[bk(0)], writes=["gate"])
                P.op("dve", lambda e: e.tensor_tensor(out=kk[:], in0=fm[:, 4:8, :], in1=bc(cols[:, C_KK:C_KK + 4], 128), op=ALU.mult),
                     reads=["fm", "cols"], writes=["kk"])
                P.op("act", lambda e: e.activation(out=sqk[:], in_=kk[:], func=AF.Square), reads=["kk"], writes=["prk"])
                P.op("pe", lambda e: e.matmul(bank(1), lhsT=blockmask, rhs=sqk[:].rearrange("p g t -> p (g t)"), start=True, stop=True),
                     reads=["prk", "ctab"], writes=[bk(1)])
                P.op("act", lambda e: e.activation(out=nrm[:].rearrange("p g t -> p (g t)"), in_=bank(1), func=AF.Sqrt),
                     reads=[bk(1)], writes=["nrm"])
                P.op("dve", lambda e: e.tensor_scalar(out=nrm[:], in0=nrm[:], scalar1=1e-12, scalar2=None, op0=ALU.max),
                     reads=["nrm"], writes=["nrm"])
                P.op("dve", lambda e: e.reciprocal(out=nrm[:], in_=nrm[:]), reads=["nrm"], writes=["nrm"])
                P.op("dve", lambda e: e.tensor_tensor(out=kk[:], in0=kk[:], in1=nrm[:], op=ALU.mult), reads=["kk", "nrm"], writes=["kk"])
                P.op("dve", lambda e: e.tensor_tensor(out=kp[:], in0=aa[:], in1=bc(cols[:, C_KA:C_KA + 4], 128), op=ALU.mult),
                     reads=["aa", "cols"], writes=["kp"])
                P.op("dve", lambda e: e.tensor_tensor(out=kp[:], in0=kp[:], in1=bc(cols[:, C_OMKA:C_OMKA + 4], 128), op=ALU.add),
                     reads=["kp", "cols"], writes=["kp"])
                P.op("dve", lambda e: e.tensor_tensor(out=kp[:], in0=kp[:], in1=fm[:, 4:8, :], op=ALU.mult), reads=["kp", "fm"], writes=["kp"])
                P.op("dve", lambda e: e.scalar_tensor_tensor(out=nkka[:].rearrange("p g t -> p (g t)"), in0=kk[:].rearrange("p g t -> p (g t)"),
                                                             scalar=-1.0, in1=aa[:].rearrange("p g t -> p (g t)"), op0=ALU.mult, op1=ALU.mult),
                     reads=["kk", "aa"], writes=["nkka"])
                P.op("dve", lambda e: e.tensor_tensor(out=prk[:], in0=fm[:, 0:4, :], in1=kp[:], op=ALU.mult), reads=["fm", "kp"], writes=["prk"])
                P.op("dve", lambda e: e.tensor_tensor(out=prk[:], in0=prk[:], in1=bc(cols[:, C_RK:C_RK + 4], 128), op=ALU.mult),
                     reads=["prk", "cols"], writes=["prk"])
                for g in range(4):
                    P.op("pe", lambda e, g=g: e.matmul(bank(7, 2 * g, 2 * g + 2), lhsT=prk[:, g, :], rhs=ct("halfmask2"),
                                                       start=True, stop=True),
                         reads=["prk", "ctab"], writes=[bk(7)], inc=(g == 3))
                P.op("act", lambda e: e.copy(out=bonus[:], in_=bank(7, 0, 8)), reads=[bk(7)], writes=["bonus"])
                for g in range(4):
                    P.op("pe", lambda e, g=g: e.transpose(out=bank(7, g * 128, g * 128 + 128), in_=fm[:, 8 + g, :], identity=ident),
                         reads=["fm", "ctab"], writes=[bk(7)], inc=(g == 3))
                P.op("act", lambda e: e.copy(out=vtok[:], in_=bank(7)), reads=[bk(7)], writes=["vtok"])
                P.op("act", lambda e: e.copy(out=vhi[:], in_=fm[:, 8:12, :]), reads=["fm"], writes=["vhi"])
                P.op("dve", lambda e: e.tensor_tensor(out=vlo[:], in0=fm[:, 8:12, :], in1=vhi[:], op=ALU.subtract),
                     reads=["fm", "vhi"], writes=["vlo"])

                ckpt("a7")
                for b in range(NB):
                    if smp:
                        wkv_state_in(b)
                    for l in range(L):
                        t = b * L + l
                        scan_step(t, t % NR, (t - 1) if l > 0 else None)
                    scan_y(b * L + L - 1, (b * L + L - 1) % NR)
                    if smp:
                        wkv_state_out(wkvs_o[b])
                if n == 15:
                    wkv_state_out(wkvp_o)

                ckpt("a8")
                P.op("act", lambda e: e.copy(out=xc2[:].rearrange("p (g hh) i -> p g hh i", hh=2),
                                             in_=bank(6).rearrange("p (hh g i) -> p g hh i", hh=2, g=4)),
                     reads=[bk(6)], writes=["xc"])
                head_norm(xc2[:], "xc", xc2[:], "xc", eps_t[:, 2:3], "w")
                xc2f = xc2[:].rearrange("p h d -> p (h d)")
                P.op("dve", lambda e: e.tensor_tensor(out=xc2f, in0=xc2f, in1=rowsA[:, 0:512], op=ALU.mult), reads=["xc", "rowsA"], writes=["xc"])
                P.op("dve", lambda e: e.tensor_tensor(out=xc2f, in0=xc2f, in1=rowsA[:, 512:1024], op=ALU.add), reads=["xc", "rowsA"], writes=["xc"])
                P.op("dve", lambda e: e.tensor_tensor(out=hn_sq[:].rearrange("p (h d) -> p h d", h=8),
                                                      in0=vtok[:].rearrange("p (h d) -> p h d", h=8), in1=bc(bonus[:], 64), op=ALU.mult),
                     reads=["vtok", "bonus"], writes=["hn_sq"])
                P.op("dve", lambda e: e.tensor_tensor(out=xc2f, in0=xc2f, in1=hn_sq[:], op=ALU.add), reads=["xc", "hn_sq"], writes=["xc"])
                P.op("dve", lambda e: e.tensor_tensor(out=oall[:, 512:1024], in0=xc2f, in1=gate[:], op=ALU.mult),
                     reads=["xc", "gate"], writes=["oall_w"])
                if n == 1:
                    dbg("oall", oall[:], [128, 1024], ["oall_r", "oall_w"])
                for hb, bnk in ((0, 2), (1, 3)):
                    for c4 in range(4):
                        c = hb * 4 + c4
                        P.op("pe", lambda e, c=c, c4=c4, bnk=bnk: e.transpose(out=bank(bnk, c4 * 128, c4 * 128 + 128),
                                                                             in_=oall[:, c * 128:(c + 1) * 128], identity=ident),
                             reads=["oall_r", "oall_w", "ctab"], writes=[bk(bnk)], inc=(c4 == 3))
                    P.op("act", lambda e, hb=hb, bnk=bnk: e.copy(out=oT[:, hb * 4:hb * 4 + 4, :].rearrange("p c t -> p (c t)"), in_=bank(bnk)),
                         reads=[bk(bnk)], writes=["oT"])
                for nb_ in range(2):
                    for c in range(8):
                        P.op("pe", lambda e, nb_=nb_, c=c: e.matmul(bank(nb_), lhsT=oT[:, c, :], rhs=wout_bf[:, c, nb_ * 512:(nb_ + 1) * 512],
                                                                    start=(c == 0), stop=(c == 7)),
                             reads=["oT", "wout_bf"], writes=[bk(nb_)], inc=(c == 7))
                    P.op("dve", lambda e, nb_=nb_: e.tensor_tensor(out=h1t[:, nb_ * 512:(nb_ + 1) * 512], in0=bank(nb_),
                                                                   in1=xt[:, nb_ * 512:(nb_ + 1) * 512], op=ALU.add),
                         reads=[bk(nb_), "xt"], writes=["xs"])
                P.dma("sp", lambda e, r0=r0, r1=r1: e.dma_start(out=h1_d[r0:r1, :], in_=h1t[:]), "h1o",
                      reads=["xs"], writes=["h1d%d" % n])
                if "h1" in debug:
                    if n == 0:
                        dbg_outs["h1"] = dout("dbg_h1", [TOK, D])
                    o_ = dbg_outs["h1"]
                    P.dma("sp", lambda e, r0=r0, r1=r1, o_=o_: e.dma_start(out=o_[r0:r1, :], in_=h1t[:]), "dbgh1",
                          reads=["xs"], is_output=True)
            P.barrier()
            print("arena A watermark", aoff[0], "of", ARENA_F32)
            aoff[0] = a_mark

        if "B" in phases:
            wq_bf = sb("wq_bf", [128, 8, 2048], BF16)
            pg_bf = sb("pg_bf", [128, 8, 1024], BF16)
            pp_bf = sb("pp_bf", [128, 2, 1024], BF16)
            keysT = sb("keysT", [128, 16, 128], BF16)
            keys_st = sb("keys_st", [128, 16, 128])
            B32 = sb("B32", [128, 32, 128], BF16)
            rowsB = sb("rowsB", [128, 2048])
            wq_v = peer_wq.rearrange("(c p) n -> p c n", p=128)
            for c in range(8):
                P.dma("pool", lambda e, c=c: e.dma_start(out=wq_bf[:, c, :], in_=wq_v[:, c, :]), "wload%d" % (c % 4), writes=["wq_bf"])
            pg_v = ple_gate.rearrange("(c p) n -> p c n", p=128)
            for c in range(8):
                P.dma("pool", lambda e, c=c: e.dma_start(out=pg_bf[:, c, :], in_=pg_v[:, c, :]), "wload%d" % (c % 4), writes=["pg_bf"])
            pp_v = ple_proj.rearrange("(c p) n -> p c n", p=128)
            for c in range(2):
                P.dma("pool", lambda e, c=c: e.dma_start(out=pp_bf[:, c, :], in_=pp_v[:, c, :]), "wload", writes=["pp_bf"])
            P.dma("act", lambda e: e.dma_start(out=keys_st[:], in_=peer_keys.rearrange("c n d -> n c d")), "keys", writes=["keys_st"])
            P.dma("act", lambda e: e.dma_start(out=rowsB[:], in_=rows_d[:, 1024:3072]), "rowsB", writes=["rowsB"])
            for c in range(16):
                bnk = 6 + (c // 4) % 2
                P.op("pe", lambda e, c=c, bnk=bnk: e.transpose(out=bank(bnk, (c % 4) * 128, (c % 4) * 128 + 128), in_=keys_st[:, c, :],
                                                               identity=ident),
                     reads=["keys_st", "ctab"], writes=[bk(bnk)], inc=(c % 4 == 3))
                if c % 4 == 3:
                    P.op("act", lambda e, c=c, bnk=bnk: e.copy(out=keysT[:, c - 3:c + 1, :].rearrange("p c n -> p (c n)"), in_=bank(bnk)),
                         reads=[bk(bnk)], writes=["keysT"])
            P.op("pool", lambda e: e.memset(B32[:], 1.0), writes=["B32"])
            for q in range(3):
                P.op("pool", lambda e, q=q: e.affine_select(out=B32[q * 32:(q + 1) * 32], in_=B32[q * 32:(q + 1) * 32],
                                                            pattern=[[1, 32], [0, 128]], compare_op=ALU.is_equal, fill=0.0,
                                                            base=0, channel_multiplier=-1),
                     reads=["B32"], writes=["B32"])

            h1bs = [sb("h1tB%d" % i, [128, 1024]) for i in range(2)]
            pts = [sb("pt%d" % i, [128, 256]) for i in range(2)]
            xs2 = sb("xs2", [128, 1024])
            xn2T = sb("xn2T", [128, 8, 128], BF16)
            xn2b = sb("xn2b", [128, 1024], BF16)
            xhi = sb("xhi", [32, 1024], BF16)
            qTb = sb("qTb", [128, 16, 128], BF16)
            Ssb = sb("Ssb", [128, 16, 128])
            S2 = sb("S2", [128, 256])
            sv = sb("sv", [128, 16, 16])
            siu = sb("siu", [128, 16, 16], U32)
            sif = sb("sif", [128, 16, 16])
            cand = sb("cand", [128, 8, 256])
            cv = sb("cv", [128, 8, 16])
            ciu = sb("ciu", [128, 8, 16], U32)
            abu = sb("abu", [128, 2, 128], U32)
            abf = sb("abf", [128, 2, 128])
            oh = cand
            e01 = sb("e01", [128, 2, 128])
            ef = sb("ef", [128, 128])
            ex = sb("ex", [128, 8, 16])
            zs = sb("zs", [128, 8])
            gsm = sb("gsm", [128, 128])
            eTus = [sb("eTu%d" % i, [128, 128], U32) for i in range(2)]
            gsmT = sb("gsmT", [128, 128])
            hvT = sb("hvT", [128, 128])
            gl = sb("gl", [128, 128])
            actb = sb("actb", [128, 128], BF16)
            NU, NV, NA = 6, 6, 6
            Ug = [sb("Ug%d" % i, [128, 1024]) for i in range(NU)]
            Vg = [sb("Vg%d" % i, [128, 1024], BF16) for i in range(NV)]
            Awr = [sb("Awr%d" % i, [128, 255], BF16) for i in range(NA)]
            h2t = sb("h2t", [128, 1024])
            xn3T = sb("xn3T", [128, 8, 128], BF16)
            gs = sb("gs", [128, 1024])
            pT = sb("pT", [128, 2, 128], BF16)
            yt = sb("yt", [128, 1024])
            for i in range(NA):
                P.op("pool", lambda e, i=i: e.memset(Awr[i][:], 0.0), writes=["Awr%d" % i])

            def top16(src, src_key, dst_v, dst_i, width):
                P.op("dve", lambda e: e.max(out=dst_v[:, 0:8], in_=src), reads=[src_key], writes=["tk_v"])
                P.op("dve", lambda e: e.max_index(out=dst_i[:, 0:8], in_max=dst_v[:, 0:8], in_values=src), reads=[src_key, "tk_v"], writes=["tk_i"])
                P.op("dve", lambda e: e.match_replace(out=S2[:, 0:width], in_to_replace=dst_v[:, 0:8], in_values=src, imm_value=-1e30),
                     reads=[src_key, "tk_v"], writes=["S2"])
                P.op("dve", lambda e: e.max(out=dst_v[:, 8:16], in_=S2[:, 0:width]), reads=["S2"], writes=["tk_v"])
                P.op("dve", lambda e: e.max_index(out=dst_i[:, 8:16], in_max=dst_v[:, 8:16], in_values=S2[:, 0:width]),
                     reads=["S2", "tk_v"], writes=["tk_i"])

            def stage_F(n):
                r0, r1 = n * 128, (n + 1) * 128
                h1b = h1bs[n % 2]; eTu = eTus[n % 2]; pt = pts[n % 2]
                h1k = "h1t%d" % (n % 2); eTk = "eTu%d" % (n % 2); ptk = "pt%d" % (n % 2)
                P.dma("sp", lambda e, r0=r0, r1=r1: e.dma_start(out=h1b[:], in_=h1_d[r0:r1, :]), "h1i", reads=["h1d%d" % n], writes=[h1k])
                P.dma("sp", lambda e, r0=r0, r1=r1: e.dma_start(out=pt[:], in_=p_all[r0:r1, :]), ptk, writes=[ptk])
                rms_rstd(h1b[:], h1k, junk[:], "junk", ssq[:], rstd[:], "b")
                P.op("dve", lambda e: e.tensor_scalar(out=xs2[:], in0=h1b[:], scalar1=rstd[:, 0:1], scalar2=None, op0=ALU.mult),
                     reads=[h1k, "brstd"], writes=["xs2"])
                transposes8(xs2, "xs2", xn2T, "xn2T", C_GFFN, 6, 7)
                yield
                P.op("dve", lambda e: e.tensor_tensor(out=xn2b[:], in0=xs2[:], in1=rowsB[:, 1024:2048], op=ALU.mult),
                     reads=["xs2", "rowsB"], writes=["xn2b"])
                P.dma("act", lambda e: e.dma_start(out=xhi[:], in_=xn2b[96:128, :]), "xhi", reads=["xn2b"], writes=["xhi"])
                for cg in range(4):
                    bnk = 6 + cg % 2
                    for c4 in range(4):
                        ch = cg * 4 + c4
                        for dc in range(8):
                            P.op("pe", lambda e, ch=ch, c4=c4, dc=dc, bnk=bnk: e.matmul(
                                bank(bnk, c4 * 128, c4 * 128 + 128), lhsT=wq_bf[:, dc, ch * 128:(ch + 1) * 128], rhs=xn2T[:, dc, :],
                                start=(dc == 0), stop=(dc == 7)),
                                reads=["wq_bf", "xn2T"], writes=[bk(bnk)], inc=(dc == 7 and c4 == 3))
                    P.op("act", lambda e, cg=cg, bnk=bnk: e.copy(out=qTb[:, cg * 4:cg * 4 + 4, :].rearrange("p c t -> p (c t)"), in_=bank(bnk)),
                         reads=[bk(bnk)], writes=["qTb"])
                    yield
                for cg in range(4):
                    bnk = 6 + cg % 2
                    for c4 in range(4):
                        ch = cg * 4 + c4
                        P.op("pe", lambda e, ch=ch, c4=c4, bnk=bnk: e.matmul(bank(bnk, c4 * 128, c4 * 128 + 128), lhsT=qTb[:, ch, :],
                                                                             rhs=keysT[:, ch, :], start=True, stop=True),
                             reads=["qTb", "keysT"], writes=[bk(bnk)], inc=(c4 == 3))
                    P.op("act", lambda e, cg=cg, bnk=bnk: e.copy(out=Ssb[:, cg * 4:cg * 4 + 4, :].rearrange("p c t -> p (c t)"), in_=bank(bnk)),
                         reads=[bk(bnk)], writes=["Ssb"])
                    yield
                for ch in range(16):
                    top16(Ssb[:, ch, :], "Ssb", sv[:, ch, :], siu[:, ch, :], 128)
                    if ch % 4 == 3:
                        yield
                P.op("dve", lambda e: e.tensor_copy(out=sif[:], in_=siu[:]), reads=["tk_i"], writes=["sif"])
                sv4 = sv[:].rearrange("p (h c) k -> p h c k", c=2)
                P.op("dve", lambda e: e.tensor_tensor(out=cand[:].rearrange("p h (a b) -> p h a b", a=16),
                                                      in0=sv4[:, :, 0, :].unsqueeze(3).to_broadcast([128, 8, 16, 16]),
                                                      in1=sv4[:, :, 1, :].unsqueeze(2).to_broadcast([128, 8, 16, 16]), op=ALU.add),
                     reads=["tk_v"], writes=["cand"])
                yield
                for h in range(8):
                    top16(cand[:, h, :], "cand", cv[:, h, :], ciu[:, h, :], 256)
                    if h % 2 == 1:
                        yield
                ciu2 = ciu[:].rearrange("p h k -> p (h k)")
                P.op("dve", lambda e: e.tensor_single_scalar(out=abu[:, 0, :], in_=ciu2, scalar=4, op=ALU.logical_shift_right),
                     reads=["tk_i"], writes=["abu"])
                P.op("dve", lambda e: e.tensor_single_scalar(out=abu[:, 1, :], in_=ciu2, scalar=15, op=ALU.bitwise_and),
                     reads=["tk_i"], writes=["abu"])
                P.op("dve", lambda e: e.tensor_copy(out=abf[:], in_=abu[:]), reads=["abu"], writes=["abf"])
                sif4 = sif[:].rearrange("p (h c) k -> p h c k", c=2)
                iob = ct("iota16").unsqueeze(1).unsqueeze(2).to_broadcast([128, 8, 16, 16])
                oh4 = oh[:].rearrange("p h (k a) -> p h k a", k=16)
                for c in range(2):
                    P.op("dve", lambda e, c=c: e.tensor_tensor(
                        out=oh4, in0=iob, in1=abf[:, c, :].rearrange("p (h k) -> p h k", h=8).unsqueeze(3).to_broadcast([128, 8, 16, 16]),
                        op=ALU.is_equal), reads=["abf", "ctab"], writes=["cand"])
                    P.op("dve", lambda e, c=c: e.tensor_tensor(out=oh4, in0=oh4, in1=sif4[:, :, c, :].unsqueeze(2).to_broadcast([128, 8, 16, 16]),
                                                               op=ALU.mult), reads=["cand", "sif"], writes=["cand"])
                    P.op("dve", lambda e, c=c: e.tensor_reduce(out=e01[:, c, :].rearrange("p (h k) -> p h k", h=8), in_=oh4, axis=AX.X, op=ALU.add),
                         reads=["cand"], writes=["e01"])
                    yield
                P.op("dve", lambda e: e.scalar_tensor_tensor(out=ef[:], in0=e01[:, 0, :], scalar=128.0, in1=e01[:, 1, :], op0=ALU.mult, op1=ALU.add),
                     reads=["e01"], writes=["ef"])
                P.op("dve", lambda e: e.tensor_tensor(out=ex[:], in0=cv[:], in1=cv[:, :, 0:1].to_broadcast([128, 8, 16]), op=ALU.subtract),
                     reads=["tk_v"], writes=["ex"])
                P.op("act", lambda e: e.activation(out=ex[:], in_=ex[:], func=AF.Exp), reads=["ex"], writes=["ex"])
                P.op("dve", lambda e: e.tensor_reduce(out=zs[:], in_=ex[:], axis=AX.X, op=ALU.add), reads=["ex"], writes=["zs"])
                P.op("dve", lambda e: e.reciprocal(out=zs[:], in_=zs[:]), reads=["zs"], writes=["zs"])
                P.op("dve", lambda e: e.tensor_tensor(out=gsm[:].rearrange("p (h k) -> p h k", h=8), in0=ex[:], in1=bc(zs[:], 16), op=ALU.mult),
                     reads=["ex", "zs"], writes=["gsm"])
                P.op("pe", lambda e: e.transpose(out=bank(6, 0, 128), in_=ef[:], identity=ident), reads=["ef", "ctab"], writes=[bk(6)])
                P.op("dve", lambda e: e.tensor_copy(out=eTu[:], in_=bank(6, 0, 128)), reads=[bk(6)], writes=[eTk])
                P.op("pe", lambda e: e.transpose(out=bank(7, 0, 128), in_=gsm[:], identity=ident), reads=["gsm", "ctab"], writes=[bk(7)])
                P.op("act", lambda e: e.copy(out=gsmT[:], in_=bank(7, 0, 128)), reads=[bk(7)], writes=["gsmT"])
                if "eT" in debug and n == 0:
                    dbg("eT", ef[:], [128, 128], ["ef"])
                    dbg("gsm", gsm[:], [128, 128], ["gsm"])
                yield

            def stage_U(n):
                r0, r1 = n * 128, (n + 1) * 128
                h1b = h1bs[n % 2]; eTu = eTus[n % 2]; pt = pts[n % 2]
                h1k = "h1t%d" % (n % 2); eTk = "eTu%d" % (n % 2); ptk = "pt%d" % (n % 2)
                for t in range(128):
                    su = t % NU
                    P.dma("pool", lambda e, t=t, su=su: e.indirect_dma_start(
                        out=Ug[su][:], out_offset=None, in_=peer_u, in_offset=bass.IndirectOffsetOnAxis(ap=eTu[:, t:t + 1], axis=0)),
                        "Ug%d" % su, reads=[eTk], writes=["Ug%d" % su])
                    q, rr = t // 32, t % 32
                    xb0 = (t % 2) * 2
                    for hf in range(2):
                        if q < 3:
                            P.op("pe", lambda e, q=q, rr=rr, hf=hf, xb0=xb0: e.matmul(
                                bank(xb0 + hf), lhsT=B32[q * 32:(q + 1) * 32, rr, :], rhs=xn2b[q * 32:(q + 1) * 32, hf * 512:(hf + 1) * 512],
                                start=True, stop=True),
                                reads=["B32", "xn2b"], writes=[bk(xb0 + hf)], inc=(hf == 1))
                        else:
                            P.op("pe", lambda e, rr=rr, hf=hf, xb0=xb0: e.matmul(
                                bank(xb0 + hf), lhsT=B32[0:32, rr, :], rhs=xhi[0:32, hf * 512:(hf + 1) * 512],
                                start=True, stop=True),
                                reads=["B32", "xhi"], writes=[bk(xb0 + hf)], inc=(hf == 1))
                    dummies(DUM_B, 6, wq_bf[:, 0, :], "wq_bf")
                    P.op("dve", lambda e, t=t, su=su, xb0=xb0: e.scalar_tensor_tensor(
                        out=oh[:].rearrange("p h x -> p (h x)")[:, 0:1024], in0=Ug[su][:], scalar=1.0, in1=ps[:, xb0 * 512:xb0 * 512 + 1024],
                        op0=ALU.mult, op1=ALU.mult, accum_out=hvT[:, t:t + 1]),
                        reads=["Ug%d" % su, bk(xb0), bk(xb0 + 1)], writes=["cand", "hvT"])

            def stage_G(n):
                r0, r1 = n * 128, (n + 1) * 128
                h1b = h1bs[n % 2]; eTu = eTus[n % 2]; pt = pts[n % 2]
                h1k = "h1t%d" % (n % 2); eTk = "eTu%d" % (n % 2); ptk = "pt%d" % (n % 2)
                P.op("act", lambda e: e.activation(out=gl[:], in_=hvT[:], func=AF.Gelu), reads=["hvT"], writes=["gl"])
                P.op("dve", lambda e: e.tensor_tensor(out=actb[:], in0=gl[:], in1=gsmT[:], op=ALU.mult), reads=["gl", "gsmT"], writes=["actb"])

            def stage_V(n, gen):
                r0, r1 = n * 128, (n + 1) * 128
                h1b = h1bs[n % 2]; eTu = eTus[n % 2]; pt = pts[n % 2]
                h1k = "h1t%d" % (n % 2); eTk = "eTu%d" % (n % 2); ptk = "pt%d" % (n % 2)
                for t in range(128):
                    sv_ = t % NV
                    sa = t % NA
                    P.dma("pool", lambda e, t=t, sv_=sv_: e.indirect_dma_start(
                        out=Vg[sv_][:], out_offset=None, in_=peer_v, in_offset=bass.IndirectOffsetOnAxis(ap=eTu[:, t:t + 1], axis=0)),
                        "Vg%d" % sv_, reads=[eTk], writes=["Vg%d" % sv_])
                    P.op("act", lambda e, t=t, sa=sa: e.copy(out=Awr[sa][:, 127:128], in_=actb[:, t:t + 1]),
                         reads=["actb"], writes=["Awr%d" % sa])
                    for hf in range(2):
                        P.op("pe", lambda e, t=t, sv_=sv_, sa=sa, hf=hf: e.matmul(
                            bank(4 + hf), lhsT=Awr[sa][:, 127 - t:255 - t], rhs=Vg[sv_][:, hf * 512:(hf + 1) * 512],
                            start=(t == 0), stop=(t == 127), skip_group_check=True),
                            reads=["Awr%d" % sa, "Vg%d" % sv_], writes=[bk(4 + hf)], inc=(hf == 1))
                    dummies(DUM_B, 6, wq_bf[:, 0, :], "wq_bf")
                    if gen is not None and t % 6 == 5:
                        next(gen, None)
                if gen is not None:
                    for _ in gen:
                        pass

            def stage_T(n):
                r0, r1 = n * 128, (n + 1) * 128
                h1b = h1bs[n % 2]; eTu = eTus[n % 2]; pt = pts[n % 2]
                h1k = "h1t%d" % (n % 2); eTk = "eTu%d" % (n % 2); ptk = "pt%d" % (n % 2)
                for hf in range(2):
                    P.op("dve", lambda e, hf=hf: e.tensor_tensor(out=h2t[:, hf * 512:(hf + 1) * 512], in0=bank(4 + hf),
                                                                 in1=h1b[:, hf * 512:(hf + 1) * 512], op=ALU.add),
                         reads=[bk(4 + hf), h1k], writes=["h2t"])
                if "h2" in debug:
                    if n == 0:
                        dbg_outs["h2"] = dout("dbg_h2", [TOK, D])
                    o_ = dbg_outs["h2"]
                    P.dma("sp", lambda e, r0=r0, r1=r1, o_=o_: e.dma_start(out=o_[r0:r1, :], in_=h2t[:]), "dbgh2",
                          reads=["h2t"], is_output=True)
                rms_rstd(h2t[:], "h2t", junk[:], "junk", ssq[:], rstd[:], "c")
                P.op("dve", lambda e: e.tensor_scalar(out=xs2[:], in0=h2t[:], scalar1=rstd[:, 0:1], scalar2=None, op0=ALU.mult),
                     reads=["h2t", "crstd"], writes=["xs2"])
                transposes8(xs2, "xs2", xn3T, "xn3T", C_GPLE, 6, 7)
                for c in range(2):
                    P.op("pe", lambda e, c=c: e.transpose(out=bank(6, c * 128, c * 128 + 128), in_=pt[:, c * 128:(c + 1) * 128], identity=ident),
                         reads=[ptk, "ctab"], writes=[bk(6)], inc=(c == 1))
                P.op("act", lambda e: e.copy(out=pT[:].rearrange("p c t -> p (c t)"), in_=bank(6, 0, 256)), reads=[bk(6)], writes=["pT"])
                for hf in range(2):
                    for dc in range(8):
                        P.op("pe", lambda e, hf=hf, dc=dc: e.matmul(bank(hf), lhsT=xn3T[:, dc, :], rhs=pg_bf[:, dc, hf * 512:(hf + 1) * 512],
                                                                    start=(dc == 0), stop=(dc == 7)),
                             reads=["xn3T", "pg_bf"], writes=[bk(hf)], inc=(dc == 7))
                    P.op("act", lambda e, hf=hf: e.activation(out=gs[:, hf * 512:(hf + 1) * 512], in_=bank(hf), func=AF.Sigmoid),
                         reads=[bk(hf)], writes=["gs"])
                for hf in range(2):
                    for c in range(2):
                        P.op("pe", lambda e, hf=hf, c=c: e.matmul(bank(2 + hf), lhsT=pT[:, c, :], rhs=pp_bf[:, c, hf * 512:(hf + 1) * 512],
                                                                  start=(c == 0), stop=(c == 1)),
                             reads=["pT", "pp_bf"], writes=[bk(2 + hf)], inc=(c == 1))
                    P.op("dve", lambda e, hf=hf: e.tensor_tensor(out=gs[:, hf * 512:(hf + 1) * 512], in0=bank(2 + hf),
                                                                 in1=gs[:, hf * 512:(hf + 1) * 512], op=ALU.mult),
                         reads=[bk(2 + hf), "gs"], writes=["gs"])
                P.op("dve", lambda e: e.tensor_tensor(out=h2t[:], in0=h2t[:], in1=gs[:], op=ALU.add), reads=["h2t", "gs"], writes=["h2t"])
                rms_rstd(h2t[:], "h2t", junk[:], "junk", ssq[:], rstd[:], "d")
                P.op("dve", lambda e: e.tensor_scalar(out=yt[:], in0=h2t[:], scalar1=rstd[:, 0:1], scalar2=None, op0=ALU.mult),
                     reads=["h2t", "drstd"], writes=["yt"])
                P.op("dve", lambda e: e.tensor_tensor(out=yt[:], in0=yt[:], in1=rowsB[:, 0:1024], op=ALU.mult), reads=["yt", "rowsB"], writes=["yt"])
                P.dma("sp", lambda e, r0=r0, r1=r1: e.dma_start(out=y_o[r0:r1, :], in_=yt[:]), "yo", reads=["yt"], is_output=True)

            ntb = min(NT, nt_limit)
            for _ in stage_F(0):
                pass
            for n in range(ntb):
                stage_U(n)
                stage_G(n)
                stage_V(n, stage_F(n + 1) if n + 1 < ntb else None)
                stage_T(n)

        print("arena end watermark", aoff[0], "of", ARENA_F32)
        P.finish()
        P.run_block(block)
    return nc, tabs, dbg_outs


_CACHE = {}


def make_in_maps(inp, tabs):
    f = np.float32
    g = lambda k: np.asarray(inp[k], dtype=f)

    def colsof(v, nch):
        return np.ascontiguousarray(v.reshape(nch, 128).T)

    cols = np.zeros((128, 64), f)
    cols[:, 0:8] = colsof(g("norm_mix")[0], 8)
    cols[:, 8:16] = colsof(g("norm_ffn")[0], 8)
    cols[:, 16:24] = colsof(g("ple_norm")[0], 8)
    cols[:, 24:38] = colsof(g("rw_mu")[0], 14)
    cols[:, 38:42] = colsof(g("rw_w0")[0], 4)
    cols[:, 42:46] = colsof(g("rw_a0")[0], 4)
    cols[:, 46:50] = colsof(g("rw_kk")[0], 4)
    cols[:, 50:54] = colsof(g("rw_ka")[0], 4)
    cols[:, 54:58] = colsof(g("rw_rk")[0].reshape(512), 4)
    rows = np.zeros((128, 3072), f)
    rows[:, 0:512] = g("rw_ln_g")[0][None, :]
    rows[:, 512:1024] = g("rw_ln_b")[0][None, :]
    rows[:, 1024:2048] = g("norm_final")[None, :]
    rows[:, 2048:3072] = g("norm_ffn")[0][None, :]
    ctab = np.concatenate([tabs[k] for k in CT_ORDER], axis=1).astype(f)
    w2a2 = np.concatenate([g("rw_w2")[0], g("rw_a2")[0]], axis=0)
    shared = dict(
        w_in=g("w_in")[0], w_out=g("w_out")[0], peer_wq=g("peer_wq")[0],
        peer_keys=g("peer_keys")[0].reshape(16, 128, 128), peer_u=g("peer_u")[0], peer_v=g("peer_v")[0],
        ple_gate=g("ple_gate")[0], ple_proj=g("ple_proj")[0], rw_w2a2=np.ascontiguousarray(w2a2), rw_g2=g("rw_g2")[0],
        cols=cols, rows=rows, ctab=ctab, cs=tabs["cs"],
    )
    xp, xs_ = g("x_prompt"), g("x_sample")
    pp, ps_ = g("p_prompt")[0], g("p_sample")[0]
    sr, sw, ss = g("state_ret")[0], g("state_wkv")[0], g("state_shift")[0]
    maps = []
    for c in range(NCORES):
        m = dict(shared)
        m["x_all"] = np.ascontiguousarray(np.concatenate([xp[c], xs_[16 * c:16 * c + 16].reshape(128, D)], axis=0))
        m["p_all"] = np.ascontiguousarray(np.concatenate([pp[c], ps_[16 * c:16 * c + 16].reshape(128, 256)], axis=0))
        m["s_ret"] = np.ascontiguousarray(sr[16 * c:16 * c + 16])
        m["s_wkv"] = np.ascontiguousarray(sw[16 * c:16 * c + 16])
        m["s_shift"] = np.ascontiguousarray(ss[16 * c:16 * c + 16])
        maps.append(m)
    return maps


def kernel(**inputs):
    if "prog" not in _CACHE:
        _CACHE["prog"] = build_program()
    nc, tabs, _ = _CACHE["prog"]
    maps = make_in_maps(inputs, tabs)
    res = run_bass_kernel_spmd(nc, maps, core_ids=list(range(NCORES)))
    R = res.results
    f = np.float32
    y_prompt = np.stack([R[c]["y"][0:2048] for c in range(NCORES)]).astype(f)
    y_sample = np.concatenate([R[c]["y"][2048:].reshape(16, 8, D) for c in range(NCORES)], axis=0).astype(f)
    ret_prompt = np.stack([R[c]["ret_p"] for c in range(NCORES)])[None].astype(f)
    wkv_prompt = np.stack([R[c]["wkv_p"] for c in range(NCORES)])[None].astype(f)
    shift_prompt = np.stack([R[c]["shift_p"] for c in range(NCORES)])[None].astype(f)
    ret_sample = np.concatenate([R[c]["ret_s"] for c in range(NCORES)], axis=0)[None].astype(f)
    wkv_sample = np.concatenate([R[c]["wkv_s"] for c in range(NCORES)], axis=0)[None].astype(f)
    shift_sample = np.concatenate([R[c]["shift_s"] for c in range(NCORES)], axis=0)[None].astype(f)
    return (y_prompt, y_sample, ret_prompt, wkv_prompt, shift_prompt, ret_sample, wkv_sample, shift_sample)
```
